# Optimizing a Trainium2 kernel written in Bass

```python
import math
import jax
import jax.numpy as jnp
from jax import lax
import numpy as np

D_MODEL = 1024
BATCH = 32
SEQ = 2048
DEPTH = 1
DEC_BATCH = 8
DEC_SEQ = 16
PAST_LEN = 4096

CHUNK = 64
Q_BLOCK = 128
DIFF_HEADS = 8
DIFF_HEAD_DIM = 64
DIFF_V_DIM = 2 * DIFF_HEAD_DIM
MLA_HEADS = 16
MLA_NOPE_DIM = 64
MLA_ROPE_DIM = 32
MLA_V_DIM = 64
MLA_Q_LORA = 256
MLA_KV_LORA = 256
D_FF = 4 * D_MODEL
NUM_BUCKETS = 32
MAX_DISTANCE = 128
ROPE_THETA = 10000.0
EPS = 1e-6
NEG_INF = -1e30

DIFF_QK_W = DIFF_HEADS * 2 * DIFF_HEAD_DIM
DIFF_V_W = DIFF_HEADS * DIFF_V_DIM
IN_COLS = 2 * DIFF_QK_W + DIFF_V_W + MLA_Q_LORA + MLA_KV_LORA + MLA_ROPE_DIM + 2 * D_MODEL

kernel_name = 'hybrid_diffattn_mla_stream_step'


def _in_splits():
    sizes = (DIFF_QK_W, DIFF_QK_W, DIFF_V_W, MLA_Q_LORA, MLA_KV_LORA, MLA_ROPE_DIM, D_MODEL, D_MODEL)
    out, acc = [], 0
    for s in sizes[:-1]:
        acc += s
        out.append(acc)
    return tuple(out)


def _rmsnorm(x, g):
    xf = x.astype(jnp.float32)
    y = xf * lax.rsqrt(jnp.mean(xf * xf, axis=-1, keepdims=True) + EPS) * g.astype(jnp.float32)
    return y.astype(x.dtype)


def _rope(x, pos):
    r = x.shape[-1]
    half = r // 2
    inv = jnp.power(ROPE_THETA, -jnp.arange(half, dtype=jnp.float32) * 2.0 / r)
    ang = pos.astype(jnp.float32)[:, None] * inv[None, :]
    ang = ang.reshape((ang.shape[0],) + (1,) * (x.ndim - 3) + (half,))
    cos, sin = jnp.cos(ang), jnp.sin(ang)
    xf = x.astype(jnp.float32)
    x1, x2 = xf[..., :half], xf[..., half:]
    return jnp.concatenate([x1 * cos - x2 * sin, x2 * cos + x1 * sin], axis=-1).astype(x.dtype)


def _t5_bias(q_pos, k_pos, table):
    rel = k_pos[None, :] - q_pos[:, None]
    half = NUM_BUCKETS // 2
    max_exact = half // 2
    n = jnp.abs(rel)
    nf = jnp.maximum(n, max_exact).astype(jnp.float32)
    large = max_exact + (jnp.log(nf / max_exact) / math.log(MAX_DISTANCE / max_exact)
                         * (half - max_exact)).astype(jnp.int32)
    large = jnp.minimum(large, half - 1)
    bucket = jnp.where(rel > 0, half, 0) + jnp.where(n < max_exact, n, large)
    return jnp.transpose(table[bucket].astype(jnp.float32), (2, 0, 1))


def _chunk_mask(q_pos, k_pos):
    return (k_pos // CHUNK)[None, :] <= (q_pos // CHUNK)[:, None]


def _sweep_queries(fn, qs, q_pos):
    t = q_pos.shape[0]
    if t <= Q_BLOCK:
        return fn(*qs, q_pos)
    nb = t // Q_BLOCK

    def to_blocks(a):
        return jnp.moveaxis(a.reshape((a.shape[0], nb, Q_BLOCK) + a.shape[2:]), 1, 0)

    out = lax.map(lambda args: fn(*args[0], args[1]),
                  (tuple(to_blocks(a) for a in qs), q_pos.reshape(nb, Q_BLOCK)))
    out = jnp.moveaxis(out, 0, 1)
    return out.reshape((out.shape[0], t) + out.shape[3:])


def _diff_attention(q, k_all, v_all, q_pos, k_pos, rel_bias, lam, subln, lambda_init):
    b, kl = k_all.shape[0], k_all.shape[1]
    k = k_all.reshape(b, kl, DIFF_HEADS, 2, DIFF_HEAD_DIM)
    scale = DIFF_HEAD_DIM ** -0.5

    def block(qb, qp):
        s = jnp.einsum('bqhnd,bkhnd->bnhqk', qb, k).astype(jnp.float32) * scale
        s = s + _t5_bias(qp, k_pos, rel_bias)[None, None]
        s = jnp.where(_chunk_mask(qp, k_pos)[None, None, None], s, NEG_INF)
        p = jax.nn.softmax(s, axis=-1)
        a = (p[:, 0] - lam * p[:, 1]).astype(v_all.dtype)
        return jnp.einsum('bhqk,bkhe->bqhe', a, v_all)

    o = _sweep_queries(block, (q,), q_pos)
    return _rmsnorm(o, subln) * (1.0 - lambda_init)


def _mla_attention(q_lat, q_rope, ckv_all, kr_all, q_pos, k_pos):
    scale = (MLA_NOPE_DIM + MLA_ROPE_DIM) ** -0.5

    def block(ql, qr, qp):
        s = (jnp.einsum('bqhc,bkc->bhqk', ql, ckv_all)
             + jnp.einsum('bqhr,bkr->bhqk', qr, kr_all)).astype(jnp.float32) * scale
        s = jnp.where(_chunk_mask(qp, k_pos)[None, None], s, NEG_INF)
        p = jax.nn.softmax(s, axis=-1).astype(ckv_all.dtype)
        return jnp.einsum('bhqk,bkc->bqhc', p, ckv_all)

    return _sweep_queries(block, (q_lat, q_rope), q_pos)


def _layer(x, q_pos, k_pos, past, layer_idx, rel_bias, norm_mix, w_in, lam_q1, lam_k1, lam_q2, lam_k2,
           diff_subln, mla_q_norm, mla_w_uq, mla_kv_norm, mla_w_uk, mla_w_uv, w_o_diff, w_o_mla, w_out,
           norm_mlp, w_up, w_down):
    b, t, _ = x.shape
    h = _rmsnorm(x, norm_mix)
    z = jnp.einsum('btd,dc->btc', h, w_in)
    q_d, k_d, v_d, c_q, c_kv, k_r, g_d, g_m = jnp.split(z, _in_splits(), axis=-1)

    k_new = k_d.reshape(b, t, DIFF_HEADS, 2 * DIFF_HEAD_DIM)
    v_new = v_d.reshape(b, t, DIFF_HEADS, DIFF_V_DIM)
    ckv_new = _rmsnorm(c_kv, mla_kv_norm)
    kr_new = _rope(k_r, q_pos)
    if past is None:
        k_all, v_all, ckv_all, kr_all = k_new, v_new, ckv_new, kr_new
    else:
        k_all = jnp.concatenate([past[0], k_new], axis=1)
        v_all = jnp.concatenate([past[1], v_new], axis=1)
        ckv_all = jnp.concatenate([past[2], ckv_new], axis=1)
        kr_all = jnp.concatenate([past[3], kr_new], axis=1)

    lambda_init = 0.8 - 0.6 * math.exp(-0.3 * layer_idx)
    lam = (jnp.exp(jnp.sum(lam_q1.astype(jnp.float32) * lam_k1.astype(jnp.float32)))
           - jnp.exp(jnp.sum(lam_q2.astype(jnp.float32) * lam_k2.astype(jnp.float32))) + lambda_init)
    q = q_d.reshape(b, t, DIFF_HEADS, 2, DIFF_HEAD_DIM)
    diff_out = _diff_attention(q, k_all, v_all, q_pos, k_pos, rel_bias, lam, diff_subln, lambda_init)

    cq = _rmsnorm(c_q, mla_q_norm)
    qm = jnp.einsum('btc,chd->bthd', cq, mla_w_uq)
    q_nope, q_rope = qm[..., :MLA_NOPE_DIM], _rope(qm[..., MLA_NOPE_DIM:], q_pos)
    q_lat = jnp.einsum('bthd,chd->bthc', q_nope, mla_w_uk)
    o_lat = _mla_attention(q_lat, q_rope, ckv_all, kr_all, q_pos, k_pos)
    mla_out = jnp.einsum('bthc,chd->bthd', o_lat, mla_w_uv)

    o_d = jnp.einsum('bthe,hed->btd', diff_out, w_o_diff)
    o_m = jnp.einsum('bthe,hed->btd', mla_out, w_o_mla)
    merged = jax.nn.sigmoid(g_d) * o_d + jax.nn.sigmoid(g_m) * o_m
    x = x + jnp.einsum('btd,de->bte', merged, w_out)

    u = jax.nn.relu(jnp.einsum('btd,df->btf', _rmsnorm(x, norm_mlp), w_up))
    x = x + jnp.einsum('btf,fd->btd', u * u, w_down)
    return x, (k_new, v_new, ckv_new, kr_new)


def setup_inputs(seed: int = 0) -> dict:
    key = jax.random.key(seed)
    ks = jax.random.split(key, 32)

    def nrm(k, shape, scale):
        return jax.random.normal(k, shape, dtype=jnp.float32) * scale

    def gain(k, shape):
        return 1.0 + nrm(k, shape, 0.05)

    L = DEPTH
    return {
        'x_prompt': nrm(ks[0], (BATCH, SEQ, D_MODEL), 1.0),
        'x_sample': nrm(ks[1], (DEC_BATCH, DEC_SEQ, D_MODEL), 1.0),
        'cache_diff_k': nrm(ks[2], (L, DEC_BATCH, PAST_LEN, DIFF_HEADS, 2 * DIFF_HEAD_DIM), 1.0),
        'cache_diff_v': nrm(ks[3], (L, DEC_BATCH, PAST_LEN, DIFF_HEADS, DIFF_V_DIM), 1.0),
        'cache_mla_ckv': nrm(ks[4], (L, DEC_BATCH, PAST_LEN, MLA_KV_LORA), 1.0),
        'cache_mla_krope': nrm(ks[5], (L, DEC_BATCH, PAST_LEN, MLA_ROPE_DIM), 1.0),
        'rel_bias': nrm(ks[6], (NUM_BUCKETS, DIFF_HEADS), 0.5),
        'norm_mix': gain(ks[7], (L, D_MODEL)),
        'w_in': nrm(ks[8], (L, D_MODEL, IN_COLS), D_MODEL ** -0.5),
        'lam_q1': nrm(ks[9], (L, DIFF_HEAD_DIM), 0.1),
        'lam_k1': nrm(ks[10], (L, DIFF_HEAD_DIM), 0.1),
        'lam_q2': nrm(ks[11], (L, DIFF_HEAD_DIM), 0.1),
        'lam_k2': nrm(ks[12], (L, DIFF_HEAD_DIM), 0.1),
        'diff_subln': gain(ks[13], (L, DIFF_V_DIM)),
        'mla_q_norm': gain(ks[14], (L, MLA_Q_LORA)),
        'mla_w_uq': nrm(ks[15], (L, MLA_Q_LORA, MLA_HEADS, MLA_NOPE_DIM + MLA_ROPE_DIM), MLA_Q_LORA ** -0.5),
        'mla_kv_norm': gain(ks[16], (L, MLA_KV_LORA)),
        'mla_w_uk': nrm(ks[17], (L, MLA_KV_LORA, MLA_HEADS, MLA_NOPE_DIM), MLA_KV_LORA ** -0.5),
        'mla_w_uv': nrm(ks[18], (L, MLA_KV_LORA, MLA_HEADS, MLA_V_DIM), MLA_KV_LORA ** -0.5),
        'w_o_diff': nrm(ks[19], (L, DIFF_HEADS, DIFF_V_DIM, D_MODEL), DIFF_V_W ** -0.5),
        'w_o_mla': nrm(ks[20], (L, MLA_HEADS, MLA_V_DIM, D_MODEL), (MLA_HEADS * MLA_V_DIM) ** -0.5),
        'w_out': nrm(ks[21], (L, D_MODEL, D_MODEL), D_MODEL ** -0.5),
        'norm_mlp': gain(ks[22], (L, D_MODEL)),
        'w_up': nrm(ks[23], (L, D_MODEL, D_FF), D_MODEL ** -0.5),
        'w_down': nrm(ks[24], (L, D_FF, D_MODEL), D_FF ** -0.5),
        'norm_final': gain(ks[25], (D_MODEL,)),
    }


def reference(x_prompt, x_sample, cache_diff_k, cache_diff_v, cache_mla_ckv, cache_mla_krope, rel_bias,
              norm_mix, w_in, lam_q1, lam_k1, lam_q2, lam_k2, diff_subln, mla_q_norm, mla_w_uq, mla_kv_norm,
              mla_w_uk, mla_w_uv, w_o_diff, w_o_mla, w_out, norm_mlp, w_up, w_down, norm_final):
    past_len = cache_diff_k.shape[2]
    pos_p = jnp.arange(x_prompt.shape[1], dtype=jnp.int32)
    pos_s = past_len + jnp.arange(x_sample.shape[1], dtype=jnp.int32)
    kpos_s = jnp.arange(past_len + x_sample.shape[1], dtype=jnp.int32)

    xp, xs = x_prompt, x_sample
    rows_p, rows_s = [], []
    for l in range(DEPTH):
        lw = (norm_mix[l], w_in[l], lam_q1[l], lam_k1[l], lam_q2[l], lam_k2[l], diff_subln[l], mla_q_norm[l],
              mla_w_uq[l], mla_kv_norm[l], mla_w_uk[l], mla_w_uv[l], w_o_diff[l], w_o_mla[l], w_out[l],
              norm_mlp[l], w_up[l], w_down[l])
        xp, rp = _layer(xp, pos_p, pos_p, None, l, rel_bias, *lw)
        past = (cache_diff_k[l], cache_diff_v[l], cache_mla_ckv[l], cache_mla_krope[l])
        xs, rs = _layer(xs, pos_s, kpos_s, past, l, rel_bias, *lw)
        rows_p.append(rp)
        rows_s.append(rs)

    y_prompt = _rmsnorm(xp, norm_final)
    y_sample = _rmsnorm(xs, norm_final)
    new_diff_k_prompt = jnp.stack([r[0] for r in rows_p])
    new_diff_v_prompt = jnp.stack([r[1] for r in rows_p])
    new_mla_ckv_prompt = jnp.stack([r[2] for r in rows_p])
    new_mla_krope_prompt = jnp.stack([r[3] for r in rows_p])
    new_diff_k_sample = jnp.stack([r[0] for r in rows_s])
    new_diff_v_sample = jnp.stack([r[1] for r in rows_s])
    new_mla_ckv_sample = jnp.stack([r[2] for r in rows_s])
    new_mla_krope_sample = jnp.stack([r[3] for r in rows_s])
    return (y_prompt, y_sample, new_diff_k_prompt, new_diff_v_prompt, new_mla_ckv_prompt, new_mla_krope_prompt,
            new_diff_k_sample, new_diff_v_sample, new_mla_ckv_sample, new_mla_krope_sample)
```

```python
import math
import os
import numpy as np
import concourse.bass as bass
import concourse.mybir as mybir
from concourse.bass_utils import run_bass_kernel_spmd

F32 = mybir.dt.float32
BF16 = mybir.dt.bfloat16
ALU = mybir.AluOpType
AF = mybir.ActivationFunctionType

D = 1024
NH = 8
MH = 16
DFF = 4096
COL_Q, COL_K, COL_V, COL_CQ, COL_CKV, COL_KR, COL_GD, COL_GM, COL_KRS, NCOL = (
    0, 1024, 2048, 3072, 3328, 3584, 3616, 4640, 5664, 5696)
EPS = 1e-6
LAMBDA_INIT = 0.8 - 0.6 * math.exp(-0.3 * 0)
NEG = -30000.0
DEC = 16


class Res:
    __slots__ = ("name", "lw", "rd")

    def __init__(self, name=""):
        self.name = name
        self.lw = []
        self.rd = {}


class Eng:
    def __init__(self, S, name, eng, is_pe=False):
        self.name = name
        self.eng = eng
        self.sem = S.nc.alloc_semaphore("cs_" + name)
        self.count = 0
        self.semval = 0
        self.rank = {}
        self.waited = {}
        self.is_pe = is_pe


class Sched:
    def __init__(self, nc, needed=None, n_dma_sems=(28, 28)):
        self.nc = nc
        self.needed = needed
        self.used = {}
        self.pe = Eng(self, "pe", nc.tensor, True)
        self.act = Eng(self, "act", nc.scalar)
        self.dve = Eng(self, "dve", nc.vector)
        self.pool = Eng(self, "pool", nc.gpsimd)
        self.sp = Eng(self, "sp", nc.sync)
        self.dq = {}
        for q, n in zip((self.sp, self.pool), n_dma_sems):
            self.dq[q.name] = dict(sems=[nc.alloc_semaphore(f"dq_{q.name}_{i}") for i in range(n)],
                                   cnt=[0] * n, i=0)
        self.n_wait = 0
        self.n_inst = 0

    def _wait(self, E, dep):
        if dep[0] == 'eng':
            _, P, c = dep
            if P is E and E.is_pe:
                return
            key = P.name
            if E.waited.get(key, 0) >= c:
                return
            self.used.setdefault(P.name, set()).add(c)
            if self.needed is None:
                E.eng.wait_ge(P.sem, c)
            else:
                E.eng.wait_ge(P.sem, P.rank[c])
            E.waited[key] = c
        else:
            _, sem, v, key = dep
            if E.waited.get(key, 0) >= v:
                return
            E.eng.wait_ge(sem, v)
            E.waited[key] = v
        self.n_wait += 1

    def _deps(self, E, reads, writes, same_war=False):
        for r in reads:
            for d in r.lw:
                self._wait(E, d)
        for w in writes:
            for d in w.lw:
                self._wait(E, d)
            for d in list(w.rd.values()):
                if d[0] == 'eng' and d[1] is E and E.is_pe:
                    continue
                self._wait(E, d)

    def _record(self, me, reads, writes, add):
        key = me[1].name if me[0] == 'eng' else me[3]
        for r in reads:
            r.rd[key] = me
        for w in writes:
            if add:
                w.lw.append(me)
            else:
                w.lw = [me]
                w.rd = {}

    def op(self, E, fn, reads=(), writes=(), add=False):
        self._deps(E, reads, writes)
        inst = fn()
        E.count += 1
        if self.needed is None:
            inst.then_inc(E.sem, 1)
        elif E.count in self.needed.get(E.name, ()):
            inst.then_inc(E.sem, 1)
            E.semval += 1
            E.rank[E.count] = E.semval
        self._record(('eng', E, E.count), reads, writes, add)
        self.n_inst += 1
        return inst

    def dma(self, Q, out, in_, reads=(), writes=(), add=False):
        q = self.dq[Q.name]
        i = q['i']
        q['i'] = (i + 1) % len(q['sems'])
        sem = q['sems'][i]
        key = f"dq_{Q.name}_{i}"
        if q['cnt'][i] > 0:
            self._wait(Q, ('dma', sem, q['cnt'][i], key))
        self._deps(Q, reads, writes, same_war=True)
        inst = Q.eng.dma_start(out=out, in_=in_)
        q['cnt'][i] += 16
        inst.then_inc(sem, 16)
        self._record(('dma', sem, q['cnt'][i], key), reads, writes, add)
        self.n_inst += 1
        return inst

    def finish(self, E):
        for qn, q in self.dq.items():
            for i, sem in enumerate(q['sems']):
                if q['cnt'][i] > 0:
                    self._wait(E, ('dma', sem, q['cnt'][i], f"dq_{qn}_{i}"))


def fence(old, new):
    pend = {}

    def put(d):
        key = d[1].name if d[0] == 'eng' else d[3]
        cur = pend.get(key)
        val = d[2]
        if cur is None or cur[2] < val:
            pend[key] = d
    for r in old:
        for d in r.lw:
            put(d)
        for d in r.rd.values():
            put(d)
    for r in new:
        for key, d in pend.items():
            cur = r.rd.get(key)
            if cur is None or cur[2] < d[2]:
                r.rd[key] = d


def _bucket(rel):
    half, max_exact = 16, 8
    n = np.abs(rel)
    nf = np.maximum(n, max_exact).astype(np.float32)
    large = max_exact + (np.log(nf / np.float32(max_exact)) / np.float32(math.log(128 / max_exact))
                         * np.float32(half - max_exact)).astype(np.int32)
    large = np.minimum(large, half - 1)
    return np.where(rel > 0, half, 0) + np.where(n < max_exact, n, large)


def _static_tables(T, PAST):
    k = np.arange(128)[:, None]
    q = np.arange(128)[None, :]
    b0 = _bucket(k - q)
    b1 = _bucket(k - q - 128)
    E = np.zeros((47, 128, 128), np.float32)
    for b in range(32):
        E[b] = (b0 == b)
    for b in range(1, 16):
        E[31 + b] = (b1 == b)
    emask = np.ascontiguousarray(E.transpose(1, 0, 2))
    inv = (10000.0 ** (-np.arange(16, dtype=np.float32) * 2.0 / 32)).astype(np.float32)

    def tabs(pos):
        ang = pos.astype(np.float32)[:, None] * inv[None, :]
        c, s = np.cos(ang).astype(np.float32), np.sin(ang).astype(np.float32)
        return np.concatenate([c, c], 1), np.concatenate([-s, s], 1)
    cp, sp = tabs(np.arange(T))
    cs, ss = tabs(PAST + np.arange(DEC))
    sc = np.float32(96 ** -0.5)
    return dict(
        emask=emask,
        ident=np.eye(128, dtype=np.float32),
        cosk_p=np.ascontiguousarray(cp.reshape(T // 128, 128, 32).transpose(1, 0, 2)),
        sink_p=np.ascontiguousarray(sp.reshape(T // 128, 128, 32).transpose(1, 0, 2)),
        cosk_s=cs, sink_s=ss,
        cosq_p=np.ascontiguousarray((cp * sc).T), sinq_p=np.ascontiguousarray((sp * sc).T),
        cosq_s=np.ascontiguousarray((cs * sc).T), sinq_s=np.ascontiguousarray((ss * sc).T),
    )


def build(NSEQ, T, PAST, needed=None):
    assert T % 512 == 0 and PAST % 128 == 0
    NB = T // 128
    NG = T // 512
    nc = bass.Bass("TRN2", target_bir_lowering=False)
    S = Sched(nc, needed)
    PE, ACT, DVE, POOL, SP = S.pe, S.act, S.dve, S.pool, S.sp

    def din(name, shape):
        return nc.dram_tensor(name, list(shape), F32, kind="ExternalInput").ap()

    def dout(name, shape):
        return nc.dram_tensor(name, list(shape), F32, kind="ExternalOutput").ap()

    xp = din("xp", [NSEQ * T, D])
    xs = din("xs", [DEC, D])
    ck = din("ck", [PAST, 1024]); cv = din("cv", [PAST, 1024])
    cckv = din("cckv", [PAST, 256]); ckr = din("ckr", [PAST, 32])
    wsrc = dict(
        w_in=din("w_in", [D, NCOL]), w_uqA=din("w_uqA", [256, MH * 96]), w_uqB=din("w_uqB", [256, MH * 96]),
        w_uk=din("w_uk", [256, 1024]), w_uv=din("w_uv", [256, 1024]), w_od=din("w_od", [1024, D]),
        w_om=din("w_om", [1024, D]), w_out=din("w_out", [D, D]), w_up=din("w_up", [D, DFF]),
        w_dn=din("w_dn", [DFF, D]))
    g_mix = din("g_mix", [1, D]); g_mlp = din("g_mlp", [1, D]); g_fin = din("g_fin", [1, D])
    g_sub = din("g_sub", [1, 128]); g_q = din("g_q", [1, 256]); g_kv = din("g_kv", [1, 256])
    lamv = din("lamv", [1, 256]); relb = din("relb", [1, 256])
    emask_d = din("emask", [128, 47, 128]); ident_d = din("ident", [128, 128])
    cosk_p_d = din("cosk_p", [128, NB, 32]); sink_p_d = din("sink_p", [128, NB, 32])
    cosk_s_d = din("cosk_s", [DEC, 32]); sink_s_d = din("sink_s", [DEC, 32])
    cosq_p_d = din("cosq_p", [32, T]); sinq_p_d = din("sinq_p", [32, T])
    cosq_s_d = din("cosq_s", [32, DEC]); sinq_s_d = din("sinq_s", [32, DEC])

    y_p = dout("y_p", [NSEQ * T, D]); y_s = dout("y_s", [DEC, D])
    kd_p = dout("kd_p", [NSEQ * T, 1024]); vd_p = dout("vd_p", [NSEQ * T, 1024])
    ckv_p = dout("ckv_p", [NSEQ * T, 256]); kr_p = dout("kr_p", [NSEQ * T, 32])
    kd_s = dout("kd_s", [DEC, 1024]); vd_s = dout("vd_s", [DEC, 1024])
    ckv_s = dout("ckv_s", [DEC, 256]); kr_s = dout("kr_s", [DEC, 32])

    wb = {}
    wbR = {}
    for name, src in wsrc.items():
        wb[name] = nc.dram_tensor(name + "_b", list(src.shape), BF16).ap()
        wbR[name] = Res(name + "_b")

    cache_src = dict(ck=ck, cv=cv, cckv=cckv, ckr=ckr)
    cb = {k_: nc.dram_tensor(k_ + "_b", list(v_.shape), BF16).ap() for k_, v_ in cache_src.items()}
    cacheR = {k_: Res(k_ + "_b") for k_ in cache_src}
    NSEGS = max(1, (PAST + 2047) // 2048)
    ckvT_d = [nc.dram_tensor(f"ckvT_d{i}", [128, 2, 2048], BF16).ap() for i in range(NSEGS)]
    krT_d = [nc.dram_tensor(f"krT_d{i}", [32, 2048], BF16).ap() for i in range(NSEGS)]
    latR = [Res(f"lat{i}") for i in range(NSEGS)]

    def sb(name, shape, dt):
        return nc.alloc_sbuf_tensor(name, list(shape), dt)

    TK = 2048 if PAST > 0 else T
    TKm = max(T, min(TK, max(PAST, 128)))
    NBK = TKm // 128
    NEWC = TKm
    hT = sb("hT", [128, 8, T], BF16)
    doT = sb("doT", [128, 8, T], BF16)
    sizes = dict(cqT=2 * T, ckvT=2 * (TKm + 128), krT=TKm + 128, QT0=T, QT1=T, KT0=TKm, KT1=TKm, Vd0=NBK * 130,
                 Vm0=NBK * 66, Vm1=NBK * 66, PT0=512, PT1=512, PT2=512, PT3=512)
    offs = {}
    o_ = 0
    for k_, v_ in sizes.items():
        offs[k_] = o_
        o_ += (v_ + 15) // 16 * 16
    ARENA_N = max(o_, 16384 + 4096 + 2048)
    arena = sb("arena", [128, ARENA_N], BF16)

    def av(k_):
        return arena[:, offs[k_]:offs[k_] + sizes[k_]]
    cqT = av("cqT").rearrange("p (c t) -> p c t", c=2)
    ckvT = av("ckvT").rearrange("p (c t) -> p c t", c=2)
    krT = av("krT")
    QT = [av("QT0"), av("QT1")]
    KT = [av("KT0"), av("KT1")]
    Vd = [arena[:, offs["Vd0"]:offs["Vd0"] + NBK * 130].rearrange("p (b e) -> p b e", e=130)]
    Vm = [arena[:, offs[k_]:offs[k_] + NBK * 66].rearrange("p (b e) -> p b e", e=66) for k_ in ("Vm0", "Vm1")]
    assert offs["Vm1"] == offs["Vm0"] + NBK * 66 + (-(NBK * 66) % 16) and 2 * NBK * 66 >= NBK * 130
    Vd.append(arena[:, offs["Vm0"]:offs["Vm0"] + NBK * 130].rearrange("p (b e) -> p b e", e=130))
    NPT = 4
    PT = [av(f"PT{i}") for i in range(NPT)]
    uT = arena[:, 0:16384].rearrange("p (f t) -> p f t", t=512)
    mT = arena[:, 16384:20480].rearrange("p (c t) -> p c t", t=512)
    sigb = [arena[:, 20480 + i * 1024:20480 + (i + 1) * 1024].bitcast(F32) for i in range(2)]
    EARLY = T >= 1024
    if EARLY:
        emask = doT[:, :, :].rearrange("p h t -> p (h t)")[:, 0:47 * 128].rearrange("p (b q) -> p b q", q=128)
    else:
        emask = arena[:, 0:47 * 128].rearrange("p (b q) -> p b q", q=128)
    NWB = 3
    WBUF = [sb(f"wbuf{i}", [128, 4096], BF16) for i in range(NWB)]
    wtok2 = sb("wtok2", [128, 8, 64], BF16)
    xres = sb("xres", [128, 4, D], F32)
    hb = sb("hb", [128, D], BF16)
    zt = sb("zt", [128, 576], F32)
    ostage = [sb(f"ostage{i}", [128, D], F32) for i in range(2)]
    kvst = [ostage[i][:, :].rearrange("p (b e) -> p b e", e=256) for i in range(2)]
    smallb = sb("smallb", [128, 512], BF16)
    krpad = sb("krpad", [128, 96], BF16)
    krf = [sb(f"krf{i}", [128, 32], F32) for i in range(2)]
    stat = sb("stat", [128, 64], F32)
    cmb = [dict(t1=sb(f"cm_t1_{i}", [128, 128], F32), o=sb(f"cm_o_{i}", [128, 128], F32),
                ob=sb(f"cm_ob_{i}", [128, 128], BF16), rr=sb(f"cm_rr_{i}", [128, 8], F32)) for i in range(4)]
    accs = sb("accs", [128, 3, 390], F32)
    junk = accs[:, :, :].rearrange("p b e -> p (b e)")[:, 0:D]
    ropet = sb("ropet", [128, 5, 256], F32)
    gtmp = sb("gtmp", [128, D], F32)
    gsub_b = sb("gsub_b", [128, 128], F32); gq_b = sb("gq_b", [128, 256], F32); gkv_b = sb("gkv_b", [128, 256], F32)
    lam_b = sb("lam_b", [128, 256], F32); tb = sb("tb", [128, 256], F32)
    cst = sb("cst", [128, 16], F32)
    ident = sb("ident_b", [128, 128], BF16)
    B0 = xres[:, 0, :].rearrange("p (h q) -> p h q", q=128); B1 = xres[:, 1, :].rearrange("p (h q) -> p h q", q=128)
    Bh = [sb(f"Bh{i}", [128, 8, 128], BF16) for i in range(2)]
    Bl = [sb(f"Bl{i}", [128, 8, 128], BF16) for i in range(2)]
    M0 = sb("M0", [128, 128], BF16)
    cosk = sb("cosk", [128, NB, 32], F32); sink = sb("sink", [128, NB, 32], F32)
    cosks = sb("cosks", [128, 32], F32); sinks = sb("sinks", [128, 32], F32)
    kstg2 = [sb("kstg0", [128, 8, 128], BF16), hb[:, :].rearrange("p (b e) -> p b e", e=128)]

    Sb = [nc.alloc_psum_tensor(f"S{i}", [128, 512], F32) for i in range(3)]
    Ab = [nc.alloc_psum_tensor(f"A{i}", [128, 512], F32) for i in range(3)]
    Pb = [nc.alloc_psum_tensor(f"P{i}", [128, 512], F32) for i in range(1)]
    P1t = nc.alloc_psum_tensor("P1t", [128, 512], F32)
    TR = P1t[:, :].bitcast(BF16)
    SbR = [Res(f"S{i}") for i in range(3)]
    PbR = [Res(f"P{i}") for i in range(1)]
    Pb = Pb + Sb
    PbR = PbR + SbR
    TRR = Res("TR")
    _bankR = [Res(f"accbank{i}") for i in range(3)]
    accsR = [Res(f"accs{i}") for i in range(3)]
    JK = accsR
    accR = [_bankR[i // 3] for i in range(9)]

    def acc_ap(a, n, w):
        return Ab[a // 3][0:n, (a % 3) * 130:(a % 3) * 130 + w]

    def acc_sb(a, n, w):
        return accs[0:n, a // 3, (a % 3) * 130:(a % 3) * 130 + w]

    def acc_copy_out(used, n):
        for b in sorted({a // 3 for a in used}):
            cols = [(a % 3) * 130 for a in used if a // 3 == b]
            c0, c1 = min(cols), max(cols) + 130
            if False:
                pass
            else:
                S.op(DVE, lambda b=b, c0=c0, c1=c1: V.tensor_copy(out=accs[0:n, b, c0:c1], in_=Ab[b][0:n, c0:c1]),
                     reads=[_bankR[b]], writes=[accsR[b]])

    R = lambda n: Res(n)
    hTR = [R(f"hT{b}") for b in range(NB)]
    cqTR = [R(f"cqT{b}") for b in range(NB)]
    ckvTR = [R(f"ckvT{b}") for b in range(NBK + 1)]
    krTR = [R(f"krT{b}") for b in range(NBK + 1)]
    QTR = [[R(f"QT{i}_{g}") for g in range(NG)] for i in range(2)]
    KTR = [[R(f"KT{i}_{g}") for g in range(NBK // 4 if NBK >= 4 else 1)] for i in range(2)]
    KTropeR = [R(f"KTrope{i}") for i in range(2)]
    VdR = [[R(f"Vd{i}_{g}") for g in range(max(1, NBK // 4))] for i in range(2)]
    VmR = [[R(f"Vm{i}_{g}") for g in range(max(1, NBK // 4))] for i in range(2)]
    PTR = [R(f"PT{i}") for i in range(NPT)]
    WBR = [R(f"wbuf{i}") for i in range(NWB)]
    xresR = [R(f"xres{i}") for i in range(4)]
    doTR = [[R(f"doT{h}_{g}") for g in range(NG)] for h in range(8)]
    moTR = [[R(f"moT{h}_{g}") for g in range(NG)] for h in range(8)]
    hbR = R("hb"); ztR = R("zt"); smallR = R("smallb"); krpadR = R("krpad")
    ostR = [R("ost0"), R("ost1")]; kvstR = ostR; gtmpR = R("gtmp"); emR = R("emask"); krfR = [R("krf0"), R("krf1")]
    statR = [R(f"stat{i}") for i in range(16)]
    cmbR = [dict(t1=R("t1"), o=R("o"), ob=R("ob"), rr=R("rr")) for i in range(4)]
    ropeR = R("ropet"); ropeTR = R("ropetab"); ropeT2R = [R("ropetab0"), R("ropetab1")]; sigR = [R("sig0"), R("sig1")]
    mTR = R("mT"); uTR = [R(f"uT{f}") for f in range(32)]
    constR = R("const")
    wtok2R = R("wtok2")
    ARES = (cqTR + ckvTR + krTR + [r for l in QTR for r in l] + [r for l in KTR for r in l] + KTropeR
            + [r for l in VdR for r in l] + [r for l in VmR for r in l] + PTR)
    kstgR2 = [R("kstg0"), hbR]
    kstgi = [0]

    def next_kstg():
        i = kstgi[0]
        kstgi[0] = 1 - i
        return kstg2[i], kstgR2[i]

    wbi = [0]

    def wbuf():
        i = wbi[0]
        wbi[0] = (i + 1) % NWB
        return WBUF[i], WBR[i]

    sti = [0]

    def statcol():
        i = sti[0]
        sti[0] = (i + 1) % 16
        return (lambda n, c, i=i: stat[0:n, i * 4 + c:i * 4 + c + 1]), statR[i]

    V = nc.vector; A = nc.scalar; G = nc.gpsimd; TE = nc.tensor

    S.dma(POOL, ident[:], ident_d, writes=[constR], add=True)
    S.dma(POOL, emask, emask_d, writes=[emR])
    bg = []

    def cast_weights(names, defer=False):
        for name in names:
            src = wsrc[name]
            rows = src.shape[0]
            step = 128 if src.shape[1] > 2048 else 256
            for r0 in range(0, rows, step):
                fn = (lambda name=name, src=src, r0=r0, step=step:
                      S.dma(POOL, wb[name][r0:r0 + step, :], src[r0:r0 + step, :], writes=[wbR[name]], add=True))
                if defer:
                    bg.append(fn)
                else:
                    fn()

    def cast_caches(defer=False):
        for k_, src in cache_src.items():
            for r0 in range(0, PAST, 512):
                r1 = min(PAST, r0 + 512)
                fn = (lambda k_=k_, src=src, r0=r0, r1=r1:
                      S.dma(POOL, cb[k_][r0:r1, :], src[r0:r1, :], writes=[cacheR[k_]], add=True))
                if defer:
                    bg.append(fn)
                else:
                    fn()

    def bg_flush():
        while bg:
            bg.pop(0)()
    cast_weights(["w_in"])
    if not EARLY:
        cast_weights(["w_uqA", "w_uqB", "w_uk", "w_uv", "w_od", "w_om", "w_out", "w_up", "w_dn"])
        cast_caches()
    def bload(dst, src_row, n):
        S.dma(SP, dst, src_row.broadcast(0, 128) if hasattr(src_row, "broadcast") else src_row, writes=[constR], add=True)

    def bc(src, n):
        return bass.AP(src.tensor, src.offset, [[0, 128], [1, n]])

    for dst, src, n in ((gsub_b, g_sub, 128),
                        (gq_b, g_q, 256), (gkv_b, g_kv, 256), (lam_b, lamv, 256), (tb, relb, 256)):
        S.dma(SP, dst[:], src.broadcast_to([128, n]), writes=[constR], add=True)
    S.dma(SP, cosk[:], cosk_p_d, writes=[constR], add=True)
    S.dma(SP, sink[:], sink_p_d, writes=[constR], add=True)
    S.dma(SP, cosks[0:DEC, :], cosk_s_d, writes=[constR], add=True)
    S.dma(SP, sinks[0:DEC, :], sink_s_d, writes=[constR], add=True)
    c2R = R("const2")
    S.op(POOL, lambda: G.memset(cst[:, 0:1], EPS), writes=[c2R], add=True)
    S.op(POOL, lambda: G.memset(cst[:, 1:2], 0.0), writes=[c2R], add=True)
    S.op(POOL, lambda: G.memset(krpad[:], 0.0), writes=[krpadR])
    S.op(POOL, lambda: G.memset(M0[:], 0.0), writes=[c2R], add=True)
    S.op(POOL, lambda: G.memset(M0[64:128, 0:64], NEG), reads=[c2R], writes=[c2R], add=True)
    S.op(DVE, lambda: V.tensor_scalar(out=gsub_b[:], in0=gsub_b[:], scalar1=1.0 - LAMBDA_INIT, scalar2=None, op0=ALU.mult),
         reads=[constR], writes=[c2R], add=True)
    S.op(DVE, lambda: V.scalar_tensor_tensor(out=junk[:, 0:64], in0=lam_b[:, 0:64], scalar=1.0, in1=lam_b[:, 64:128],
                                             op0=ALU.mult, op1=ALU.mult, accum_out=cst[:, 3:4]),
         reads=[constR], writes=JK + [c2R], add=True)
    S.op(DVE, lambda: V.scalar_tensor_tensor(out=junk[:, 64:128], in0=lam_b[:, 128:192], scalar=1.0, in1=lam_b[:, 192:256],
                                             op0=ALU.mult, op1=ALU.mult, accum_out=cst[:, 4:5]),
         reads=[constR], writes=JK + [c2R], add=True)
    c3R = R("const3")
    S.op(ACT, lambda: A.activation(out=cst[:, 5:7], in_=cst[:, 3:5], func=AF.Exp), reads=[c2R], writes=[c3R])
    c4R = R("const4")
    S.op(DVE, lambda: V.scalar_tensor_tensor(out=cst[:, 7:8], in0=cst[:, 6:7], scalar=-LAMBDA_INIT, in1=cst[:, 5:6],
                                             op0=ALU.add, op1=ALU.subtract), reads=[c3R], writes=[c4R])
    NLAM = lambda n: cst[0:n, 7:8]
    bR = [R(f"bias{h}") for h in range(16)]
    bhR = [R("bh0"), R("bh1")]; blR = [R("bl0"), R("bl1")]

    def build_bias():
        fence(xresR[0:2], bR)
        for b in range(32):
            for h in range(8):
                sc_ap = tb[:, b * 8 + h:b * 8 + h + 1]
                if b == 0:
                    S.op(DVE, lambda h=h, sc_ap=sc_ap: V.tensor_scalar(out=B0[:, h, :], in0=emask[:, 0, :], scalar1=sc_ap,
                                                                       scalar2=None, op0=ALU.mult), reads=[constR, emR], writes=[bR[h]])
                else:
                    S.op(DVE, lambda h=h, b=b, sc_ap=sc_ap: V.scalar_tensor_tensor(
                        out=B0[:, h, :], in0=emask[:, b, :], scalar=sc_ap, in1=B0[:, h, :], op0=ALU.mult, op1=ALU.add),
                        reads=[constR, emR, bR[h]], writes=[bR[h]])
        for b in range(1, 16):
            for h in range(8):
                sc_ap = tb[:, b * 8 + h:b * 8 + h + 1]
                if b == 1:
                    S.op(DVE, lambda h=h, sc_ap=sc_ap: V.tensor_scalar(out=B1[:, h, :], in0=emask[:, 32, :], scalar1=sc_ap,
                                                                       scalar2=None, op0=ALU.mult), reads=[constR, emR], writes=[bR[8 + h]])
                else:
                    S.op(DVE, lambda h=h, b=b, sc_ap=sc_ap: V.scalar_tensor_tensor(
                        out=B1[:, h, :], in0=emask[:, 31 + b, :], scalar=sc_ap, in1=B1[:, h, :], op0=ALU.mult, op1=ALU.add),
                        reads=[constR, emR, bR[8 + h]], writes=[bR[8 + h]])
        for h in range(8):
            c_ap = tb[:, 15 * 8 + h:15 * 8 + h + 1]
            S.op(DVE, lambda h=h, c_ap=c_ap: V.tensor_scalar(out=B0[:, h, :], in0=B0[:, h, :], scalar1=c_ap, scalar2=None,
                                                             op0=ALU.subtract), reads=[bR[h]], writes=[bR[h]])
            S.op(DVE, lambda h=h, c_ap=c_ap: V.tensor_scalar(out=B1[:, h, :], in0=B1[:, h, :], scalar1=c_ap, scalar2=None,
                                                             op0=ALU.subtract), reads=[bR[8 + h]], writes=[bR[8 + h]])
        for h in range(8):
            S.op(DVE, lambda h=h: V.memset(B0[64:128, h, 0:64], NEG), reads=[bR[h]], writes=[bR[h]])
        for i_, Bf in enumerate((B0, B1)):
            rs = bR[8 * i_:8 * i_ + 8]
            S.op(DVE, lambda i_=i_, Bf=Bf: V.tensor_copy(out=Bh[i_][:, :, :], in_=Bf), reads=rs, writes=[bhR[i_]])
            S.op(DVE, lambda i_=i_, Bf=Bf: V.tensor_tensor(out=Bf, in0=Bf, in1=Bh[i_][:, :, :], op=ALU.subtract),
                 reads=rs + [bhR[i_]], writes=rs)
            S.op(DVE, lambda i_=i_, Bf=Bf: V.tensor_copy(out=Bl[i_][:, :, :], in_=Bf), reads=rs, writes=[blR[i_]])
        fence(bR, xresR[0:2])
        if EARLY:
            fence([emR], [r for l in doTR for r in l])
    if not EARLY:
        build_bias()
    allconst = [constR, c2R, c3R, c4R]
    if not EARLY:
        fence([emR], ARES)
    for t_ in Vd[0:1]:
        S.op(POOL, lambda t_=t_: G.memset(t_, 1.0), writes=VdR[0])
    for i_, t_ in enumerate(Vm):
        S.op(POOL, lambda t_=t_: G.memset(t_, 1.0), writes=VmR[i_])

    def rstd_from_sumsq(n, ss_ap, out_ap, inv_n, sres):
        S.op(ACT, lambda: A.activation(out=out_ap, in_=ss_ap, func=AF.Ln, scale=inv_n, bias=cst[0:n, 0:1]),
             reads=[sres, c2R], writes=[sres])
        S.op(ACT, lambda: A.activation(out=out_ap, in_=out_ap, func=AF.Exp, scale=-0.5), reads=[sres], writes=[sres])

    def wload3(name, c0, ncols, kch, dst_ap, dres, add=False):
        src = wb[name].rearrange("(c p) n -> p c n", p=128)[:, 0:kch, c0:c0 + ncols]
        S.dma(SP, dst_ap, src, reads=[wbR[name]], writes=[dres], add=add)

    def transposes_to(dst_fn, src_tile, n, nchunks, width, src_res, dst_res, evac_eng):
        for c in range(nchunks):
            S.op(PE, lambda c=c: TE.transpose(TR[0:width, c * 128:c * 128 + n], src_tile[0:n, c * width:(c + 1) * width],
                                              ident[0:n, 0:n]), reads=[src_res, constR], writes=[TRR], add=(c > 0))
        src_ap = TR[0:width, 0:nchunks * 128].rearrange("p (c t) -> p c t", t=128)[:, :, 0:n]
        if evac_eng is ACT:
            S.op(ACT, lambda: A.copy(out=dst_fn(), in_=src_ap), reads=[TRR], writes=dst_res)
        else:
            S.op(DVE, lambda: V.tensor_copy(out=dst_fn(), in_=src_ap), reads=[TRR], writes=dst_res)

    pbi = [0]

    def pbank():
        i = pbi[0]
        pbi[0] = (i + 1) % 4
        return Pb[i], PbR[i]

    osti = [0]

    def ost():
        i = osti[0]
        osti[0] = 1 - i
        return ostage[i], ostR[i]

    def phase1(x_src, n, blk, hcol, wtok, wtokR, ckv_dst, kr_dst, cosk_ap, sink_ap, kcol):
        xs_i = blk % 4
        xt = xres[0:n, xs_i, :]
        S.dma(SP, xt, x_src, writes=[xresR[xs_i]])
        sc, sres = statcol()
        S.op(DVE, lambda: V.scalar_tensor_tensor(out=junk[0:n, :], in0=xt, scalar=1.0, in1=xt, op0=ALU.mult, op1=ALU.mult,
                                                 accum_out=sc(n, 0)), reads=[xresR[xs_i]], writes=JK + [sres])
        rstd_from_sumsq(n, sc(n, 0), sc(n, 1), 1.0 / D, sres)
        S.op(DVE, lambda: V.scalar_tensor_tensor(out=hb[0:n, :], in0=xt, scalar=sc(n, 1), in1=gtmp[0:n, :],
                                                 op0=ALU.mult, op1=ALU.mult), reads=[xresR[xs_i], sres, gtmpR], writes=[hbR])
        hres = hTR[hcol // 128]
        transposes_to(lambda: hT[:, :, hcol:hcol + n], hb, n, 8, 128, hbR, [hres], ACT)
        pa, paR = Pb[0], PbR[0]
        pb_, pbR_ = Pb[3], PbR[3]
        for c in range(8):
            S.op(PE, lambda c=c: TE.matmul(pa[0:n, 0:512], hT[:, c, hcol:hcol + n], wtok[:, c, 0:512], start=(c == 0), stop=(c == 7)),
                 reads=[hres, wtokR], writes=[paR])
        for c in range(8):
            S.op(PE, lambda c=c: TE.matmul(pb_[0:n, 0:64], hT[:, c, hcol:hcol + n], wtok2[:, c, 0:64], start=(c == 0), stop=(c == 7)),
                 reads=[hres, wtok2R], writes=[pbR_])
        S.op(ACT, lambda: A.copy(out=zt[0:n, 0:512], in_=pa[0:n, 0:512]), reads=[paR], writes=[ztR])
        S.op(ACT, lambda: A.copy(out=zt[0:n, 512:576], in_=pb_[0:n, 0:64]), reads=[pbR_], writes=[ztR], add=True)
        sc2, sres2 = statcol()
        S.op(DVE, lambda: V.scalar_tensor_tensor(out=junk[0:n, 0:256], in0=zt[0:n, 0:256], scalar=1.0, in1=zt[0:n, 0:256],
                                                 op0=ALU.mult, op1=ALU.mult, accum_out=sc2(n, 0)), reads=[ztR], writes=JK + [sres2])
        S.op(DVE, lambda: V.scalar_tensor_tensor(out=junk[0:n, 256:512], in0=zt[0:n, 256:512], scalar=1.0, in1=zt[0:n, 256:512],
                                                 op0=ALU.mult, op1=ALU.mult, accum_out=sc2(n, 2)), reads=[ztR], writes=JK + [sres2], add=True)
        rstd_from_sumsq(n, sc2(n, 0), sc2(n, 1), 1.0 / 256, sres2)
        rstd_from_sumsq(n, sc2(n, 2), sc2(n, 3), 1.0 / 256, sres2)
        S.op(DVE, lambda: V.scalar_tensor_tensor(out=smallb[0:n, 0:256], in0=zt[0:n, 0:256], scalar=sc2(n, 1), in1=gq_b[0:n, :],
                                                 op0=ALU.mult, op1=ALU.mult), reads=[ztR, sres2, constR], writes=[smallR])
        ot, otR = ost()
        S.op(DVE, lambda: V.scalar_tensor_tensor(out=ot[0:n, 0:256], in0=zt[0:n, 256:512], scalar=sc2(n, 3), in1=gkv_b[0:n, :],
                                                 op0=ALU.mult, op1=ALU.mult), reads=[ztR, sres2, constR], writes=[otR])
        S.dma(POOL, ckv_dst, ot[0:n, 0:256], reads=[otR])
        S.op(ACT, lambda: A.copy(out=smallb[0:n, 256:512], in_=ot[0:n, 0:256]), reads=[otR], writes=[smallR], add=True)
        cres = cqTR[hcol // 128]
        kres = ckvTR[kcol // 128]
        for c in range(4):
            S.op(PE, lambda c=c: TE.transpose(TR[:, c * 128:c * 128 + n], smallb[0:n, c * 128:(c + 1) * 128], ident[0:n, 0:n]),
                 reads=[smallR, constR], writes=[TRR], add=(c > 0))
        trv = TR[:, 0:512].rearrange("p (c t) -> p c t", t=128)
        S.op(DVE, lambda: V.tensor_copy(out=cqT[:, :, hcol:hcol + n], in_=trv[:, 0:2, 0:n]), reads=[TRR], writes=[cres])
        S.op(DVE, lambda: V.tensor_copy(out=ckvT[:, :, kcol:kcol + n], in_=trv[:, 2:4, 0:n]), reads=[TRR], writes=[kres])
        kf, kfR = krf[blk % 2], krfR[blk % 2]
        S.op(DVE, lambda: V.tensor_tensor(out=junk[0:n, 512:544], in0=zt[0:n, 544:576], in1=sink_ap, op=ALU.mult),
             reads=[ztR, constR], writes=JK)
        S.op(DVE, lambda: V.tensor_tensor(out=kf[0:n, :], in0=zt[0:n, 512:544], in1=cosk_ap, op=ALU.mult),
             reads=[ztR, constR], writes=[kfR])
        S.op(DVE, lambda: V.tensor_tensor(out=kf[0:n, :], in0=kf[0:n, :], in1=junk[0:n, 512:544], op=ALU.add),
             reads=[kfR] + JK, writes=[kfR])
        S.dma(POOL, kr_dst, kf[0:n, :], reads=[kfR])
        S.op(ACT, lambda: A.copy(out=krpad[0:n, 64:96], in_=kf[0:n, :]), reads=[kfR], writes=[krpadR])
        S.op(PE, lambda: TE.transpose(TR[0:96, 0:n], krpad[0:n, 0:96], ident[0:n, 0:n]), reads=[krpadR, constR], writes=[TRR])
        S.op(DVE, lambda: V.tensor_copy(out=krT[64:96, kcol:kcol + n], in_=TR[64:96, 0:n]), reads=[TRR], writes=[krTR[kcol // 128]])

    chunk_ctr = [0]

    def bias_mm(dst, nk, nq, kind, rel, head, sbR):
        if kind == 'diff':
            tiles = [(Bh[rel][0:nk, head, 0:nq], bhR[rel]), (Bl[rel][0:nk, head, 0:nq], blR[rel])]
        else:
            tiles = [(M0[0:nk, 0:nq], c2R)]
        for (bt, br) in tiles:
            S.op(PE, lambda bt=bt: TE.matmul(dst, ident[0:nk, 0:nk], bt, start=False, stop=False, skip_group_check=True),
                 reads=[br, constR], writes=[sbR], add=True)

    def attn_group(units, qbs, kbs, is_prompt, kind, head, first_seg=True, last_seg=True, defer=False):
        chunks = []
        for (kbi, kcol, nk, vblk) in kbs:
            vis = [qb for qb in qbs if (not is_prompt) or qb[0] >= kbi]
            if not vis:
                continue
            for u in units:
                chunks.append((u, kbi, kcol, nk, vblk, vis))
        started = set()
        base = chunk_ctr[0]
        chunk_ctr[0] += len(chunks)
        last_kb = kbs[-1][0]
        first_kb = kbs[0][0]

        def emit_qk(ci):
            u, kbi, kcol, nk, vblk, vis = chunks[ci]
            sbk, sbR = Sb[(base + ci) % 3], SbR[(base + ci) % 3]
            q0 = vis[0][1]
            ncols = vis[-1][1] + vis[-1][2] - q0
            lo, hi = u['qrows']
            S.op(PE, lambda: TE.matmul(sbk[0:nk, 0:ncols], u['KT'][lo:hi, kcol:kcol + nk], u['QT'][lo:hi, q0:q0 + ncols],
                                       start=True, stop=False, skip_group_check=True),
                 reads=u['kres'](kbi) + u['qres'](vis), writes=[sbR])
            for (qbi, qcol, nq, slot) in vis:
                rel = qbi - kbi
                o_ = qcol - q0
                if kind == 'diff' and rel in (0, 1):
                    bias_mm(sbk[0:nk, o_:o_ + nq], nk, nq, 'diff', rel, head, sbR)
                elif kind == 'mla' and rel == 0 and is_prompt:
                    bias_mm(sbk[0:nk, o_:o_ + nq], nk, nq, 'mla', 0, 0, sbR)
            pt, ptR = PT[(base + ci) % NPT], PTR[(base + ci) % NPT]
            bias_ap = tb[0:nk, 15 * 8 + head:15 * 8 + head + 1] if kind == 'diff' else cst[0:nk, 1:2]
            S.op(ACT, lambda: A.activation(out=pt[0:nk, 0:ncols], in_=sbk[0:nk, 0:ncols], func=AF.Exp, bias=bias_ap, scale=1.0),
                 reads=[sbR, constR, c2R], writes=[ptR])

        def emit_av(ci):
            u, kbi, kcol, nk, vblk, vis = chunks[ci]
            pt, ptR = PT[(base + ci) % NPT], PTR[(base + ci) % NPT]
            q0 = vis[0][1]
            vc = u['vc']
            for (qbi, qcol, nq, slot) in vis:
                a = u['accbase'] + slot
                o_ = qcol - q0
                st_ = first_seg and (kbi == first_kb) and ((a // 3) not in started)
                if first_seg and (kbi == first_kb):
                    started.add(a // 3)
                sp_ = False
                S.op(PE, lambda a=a, o_=o_, nq=nq, st_=st_, sp_=sp_: TE.matmul(
                    acc_ap(a, nq, vc), pt[0:nk, o_:o_ + nq], u['V'][0:nk, vblk, 0:vc], start=st_, stop=sp_,
                    skip_group_check=True),
                    reads=[ptR] + u['vres'](kbi), writes=[accR[a]])

        n = len(chunks)
        LA = 2
        steps = []
        for ci in range(n + LA):
            def step(ci=ci):
                if ci < n:
                    emit_qk(ci)
                if 0 <= ci - LA < n:
                    emit_av(ci - LA)
            steps.append(step)
        if defer:
            return steps
        for st in steps:
            st()

    def attn_group_sample(units, qb, kbs, kind, head, first_seg, last_seg):
        (qbi, qcol, nq, slot) = qb
        supers = []
        cur = []
        for kb in kbs:
            if cur and (kb[2] != cur[0][2] or len(cur) >= 16):
                supers.append(cur)
                cur = []
            cur.append(kb)
        if cur:
            supers.append(cur)
        chunks = [(u, sup) for sup in supers for u in units]
        base = chunk_ctr[0]
        chunk_ctr[0] += len(chunks)
        started = set()
        first_kb = kbs[0][0]

        def emit_qk(ci):
            u, sup = chunks[ci]
            sbk, sbR = Sb[(base + ci) % 3], SbR[(base + ci) % 3]
            nk = sup[0][2]
            lo, hi = u['qrows']
            for idx, (kbi, kcol, nk_, vblk) in enumerate(sup):
                S.op(PE, lambda idx=idx, kcol=kcol: TE.matmul(sbk[0:nk, idx * nq:(idx + 1) * nq], u['KT'][lo:hi, kcol:kcol + nk],
                                                              u['QT'][lo:hi, qcol:qcol + nq], start=True, stop=False,
                                                              skip_group_check=True),
                     reads=u['kres'](kbi) + u['qres']([qb]), writes=[sbR], add=(idx > 0))
                if kind == 'diff' and (qbi - kbi) in (0, 1):
                    bias_mm(sbk[0:nk, idx * nq:(idx + 1) * nq], nk, nq, 'diff', qbi - kbi, head, sbR)
            pt, ptR = PT[(base + ci) % NPT], PTR[(base + ci) % NPT]
            ncols = len(sup) * nq
            bias_ap = tb[0:nk, 15 * 8 + head:15 * 8 + head + 1] if kind == 'diff' else cst[0:nk, 1:2]
            S.op(ACT, lambda: A.activation(out=pt[0:nk, 0:ncols], in_=sbk[0:nk, 0:ncols], func=AF.Exp, bias=bias_ap, scale=1.0),
                 reads=[sbR, constR, c2R], writes=[ptR])

        def emit_av(ci):
            u, sup = chunks[ci]
            pt, ptR = PT[(base + ci) % NPT], PTR[(base + ci) % NPT]
            nk = sup[0][2]
            vc = u['vc']
            a = u['accbase'] + slot
            for idx, (kbi, kcol, nk_, vblk) in enumerate(sup):
                st_ = first_seg and (kbi == first_kb) and ((a // 3) not in started)
                if first_seg and (kbi == first_kb):
                    started.add(a // 3)
                S.op(PE, lambda idx=idx, vblk=vblk, st_=st_: TE.matmul(
                    acc_ap(a, nq, vc), pt[0:nk, idx * nq:(idx + 1) * nq], u['V'][0:nk, vblk, 0:vc], start=st_, stop=False,
                    skip_group_check=True), reads=[ptR] + u['vres'](kbi), writes=[accR[a]])

        n = len(chunks)
        LA = 2
        for ci in range(n + LA):
            if ci < n:
                emit_qk(ci)
            if 0 <= ci - LA < n:
                emit_av(ci - LA)

    def diff_combine(head, qbs, dst_fn, dst_res_fn):
        acc_copy_out([sl for q_ in qbs for sl in (q_[3], 4 + q_[3])], qbs[0][2])
        for (qbi, qcol, nq, slot) in qbs:
            c = cmb[slot]; cr = cmbR[slot]
            a1, a2 = acc_sb(slot, nq, 130), acc_sb(4 + slot, nq, 130)
            rr = c['rr']
            S.op(DVE, lambda: V.reciprocal(out=rr[0:nq, 0:1], in_=a1[:, 128:129]), reads=[accsR[slot // 3]], writes=[cr['rr']])
            S.op(DVE, lambda: V.reciprocal(out=rr[0:nq, 1:2], in_=a2[:, 128:129]), reads=[accsR[(4 + slot) // 3]], writes=[cr['rr']], add=True)
        for (qbi, qcol, nq, slot) in qbs:
            c = cmb[slot]; cr = cmbR[slot]; rr = c['rr']
            S.op(DVE, lambda: V.tensor_scalar(out=rr[0:nq, 2:3], in0=rr[0:nq, 1:2], scalar1=NLAM(nq), scalar2=None, op0=ALU.mult),
                 reads=[cr['rr'], c4R], writes=[cr['rr']])
            a1 = acc_sb(slot, nq, 130)
            S.op(DVE, lambda: V.tensor_scalar(out=c['t1'][0:nq, :], in0=a1[:, 0:128], scalar1=rr[0:nq, 0:1], scalar2=None, op0=ALU.mult),
                 reads=[accsR[slot // 3], cr['rr']], writes=[cr['t1']])
        for (qbi, qcol, nq, slot) in qbs:
            c = cmb[slot]; cr = cmbR[slot]; rr = c['rr']
            a2 = acc_sb(4 + slot, nq, 130)
            S.op(DVE, lambda: V.scalar_tensor_tensor(out=c['o'][0:nq, :], in0=a2[:, 0:128], scalar=rr[0:nq, 2:3], in1=c['t1'][0:nq, :],
                                                     op0=ALU.mult, op1=ALU.add), reads=[accsR[(4 + slot) // 3], cr['rr'], cr['t1']], writes=[cr['o']])
        for (qbi, qcol, nq, slot) in qbs:
            c = cmb[slot]; cr = cmbR[slot]; rr = c['rr']
            S.op(DVE, lambda: V.scalar_tensor_tensor(out=c['t1'][0:nq, :], in0=c['o'][0:nq, :], scalar=1.0, in1=c['o'][0:nq, :],
                                                     op0=ALU.mult, op1=ALU.mult, accum_out=rr[0:nq, 3:4]),
                 reads=[cr['o']], writes=[cr['t1'], cr['rr']])
        for (qbi, qcol, nq, slot) in qbs:
            c = cmb[slot]; cr = cmbR[slot]; rr = c['rr']
            rstd_from_sumsq(nq, rr[0:nq, 3:4], rr[0:nq, 4:5], 1.0 / 128, cr['rr'])
        for (qbi, qcol, nq, slot) in qbs:
            c = cmb[slot]; cr = cmbR[slot]; rr = c['rr']
            S.op(DVE, lambda: V.scalar_tensor_tensor(out=c['ob'][0:nq, :], in0=c['o'][0:nq, :], scalar=rr[0:nq, 4:5], in1=gsub_b[0:nq, :],
                                                     op0=ALU.mult, op1=ALU.mult), reads=[cr['o'], cr['rr'], c2R], writes=[cr['ob']])
        for (qbi, qcol, nq, slot) in qbs:
            c = cmb[slot]; cr = cmbR[slot]
            S.op(PE, lambda: TE.transpose(TR[:, slot * 128:slot * 128 + nq], c['ob'][0:nq, :], ident[0:nq, 0:nq]),
                 reads=[cr['ob'], constR], writes=[TRR], add=(slot != qbs[0][3]))
        q0 = qbs[0][1]
        ncols = qbs[-1][1] + qbs[-1][2] - q0
        s0 = qbs[0][3]
        S.op(DVE, lambda: V.tensor_copy(out=dst_fn(q0, ncols), in_=TR[:, s0 * 128:s0 * 128 + ncols]), reads=[TRR], writes=dst_res_fn())

    def mla_combine(pair, qbs, dst_fn, dst_res_fn):
        acc_copy_out([sl for q_ in qbs for sl in (q_[3], 4 + q_[3])], qbs[0][2])
        for u in range(2):
            for (qbi, qcol, nq, slot) in qbs:
                c = cmb[slot]; cr = cmbR[slot]; rr = c['rr']
                a = acc_sb(4 * u + slot, nq, 66)
                S.op(DVE, lambda: V.reciprocal(out=rr[0:nq, u:u + 1], in_=a[:, 64:65]), reads=[accsR[(4 * u + slot) // 3]], writes=[cr['rr']],
                     add=(u > 0))
        for u in range(2):
            for (qbi, qcol, nq, slot) in qbs:
                c = cmb[slot]; cr = cmbR[slot]; rr = c['rr']
                a = acc_sb(4 * u + slot, nq, 66)
                S.op(DVE, lambda: V.tensor_scalar(out=c['ob'][0:nq, u * 64:(u + 1) * 64], in0=a[:, 0:64], scalar1=rr[0:nq, u:u + 1],
                                                  scalar2=None, op0=ALU.mult), reads=[accsR[(4 * u + slot) // 3], cr['rr']], writes=[cr['ob']],
                     add=(u > 0))
        for (qbi, qcol, nq, slot) in qbs:
            c = cmb[slot]; cr = cmbR[slot]
            S.op(PE, lambda: TE.transpose(TR[:, slot * 128:slot * 128 + nq], c['ob'][0:nq, :], ident[0:nq, 0:nq]),
                 reads=[cr['ob'], constR], writes=[TRR], add=(slot != qbs[0][3]))
        q0 = qbs[0][1]
        ncols = qbs[-1][1] + qbs[-1][2] - q0
        s0 = qbs[0][3]
        S.op(DVE, lambda: V.tensor_copy(out=dst_fn(q0, ncols), in_=TR[:, s0 * 128:s0 * 128 + ncols]), reads=[TRR], writes=dst_res_fn())

    def proj_fm(dstT, rows, wt, wcol, M, srcT, kch, tq, src_res_fn, wres, dst_res_fn, evac):
        for c0 in range(0, tq, 512):
            n = min(512, tq - c0)
            pk, pkR = pbank()
            for c in range(kch):
                S.op(PE, lambda c=c: TE.matmul(pk[0:M, 0:n], wt[:, c, wcol:wcol + M], srcT[:, c, c0:c0 + n],
                                               start=(c == 0), stop=(c == kch - 1)),
                     reads=src_res_fn(c0, n) + [wres], writes=[pkR])
            evac(pk, pkR, c0, n)

    STOP = int(os.environ.get("DEV_STOP", "99"))

    sbi = [0]

    def sbank():
        i = sbi[0]
        sbi[0] = 1 - i
        return (Pb[0], PbR[0]) if i == 0 else (P1t, TRR)

    def interleave(steps, pieces):
        n, m = len(steps), len(pieces)
        k = 0
        for i, st in enumerate(steps):
            st()
            if bg and i % 8 == 7:
                bg.pop(0)()
            tgt = ((i + 1) * m) // n if n else m
            while k < tgt:
                pieces[k]()
                k += 1
        while k < m:
            pieces[k]()
            k += 1

    def prompt_attention(s):
        tq = T
        qblocks = [(b, b * 128, 128) for b in range(NB)]
        groups = [[(4 * g + i, (4 * g + i) * 128, 128, i) for i in range(4)] for g in range(NG)]
        kbs_all = [(b, b * 128, 128, b) for b in range(NB)]
        hsrc = lambda c0, n: [hTR[b] for b in range(c0 // 128, (c0 + n + 127) // 128)]
        cqsrc = lambda c0, n: [cqTR[b] for b in range(c0 // 128, (c0 + n + 127) // 128)]

        fence(VmR[0] + VmR[1], VdR[1])
        S.op(POOL, lambda: G.memset(Vd[1][:, :, 128:130], 1.0), writes=VdR[1])

        def diff_setup(h):
            qi = h % 2
            qt, ktile, vt = QT[qi], KT[qi], Vd[qi]
            box = {}
            pieces = []

            def p_load():
                hw, hwR = wbuf()
                hwv = hw[:, 0:3072].rearrange("p (c n) -> p c n", n=384)
                wload3("w_in", COL_Q + h * 128, 128, 8, hwv[:, :, 0:128], hwR)
                wload3("w_in", COL_K + h * 128, 128, 8, hwv[:, :, 128:256], hwR, add=True)
                wload3("w_in", COL_V + h * 128, 128, 8, hwv[:, :, 256:384], hwR, add=True)
                box['w'] = (hwv, hwR)
            pieces.append(p_load)

            def p_proj(c0, which):
                hwv, hwR = box['w']
                pk, pkR = sbank()
                wc = 0 if which == 'q' else 128
                for c in range(8):
                    S.op(PE, lambda c=c: TE.matmul(pk[:, 0:512], hwv[:, c, wc:wc + 128], hT[:, c, c0:c0 + 512],
                                                   start=(c == 0), stop=(c == 7)), reads=hsrc(c0, 512) + [hwR], writes=[pkR])
                if which == 'q':
                    S.op(DVE, lambda: V.tensor_scalar(out=qt[:, c0:c0 + 512], in0=pk[:, 0:512], scalar1=0.125, scalar2=None, op0=ALU.mult),
                         reads=[pkR], writes=[QTR[qi][c0 // 512]])
                else:
                    S.op(DVE, lambda: V.tensor_copy(out=ktile[:, c0:c0 + 512], in_=pk[:, 0:512]),
                         reads=[pkR], writes=[KTR[qi][c0 // 512], KTropeR[qi]])

            def p_kv(bi):
                hwv, hwR = box['w']
                qcol = bi * 128
                pk, pkR = sbank()
                for c in range(8):
                    S.op(PE, lambda c=c: TE.matmul(pk[:, 0:256], hT[:, c, qcol:qcol + 128], hwv[:, c, 128:384],
                                                   start=(c == 0), stop=(c == 7)), reads=[hTR[bi], hwR], writes=[pkR])
                sg = (bi // 4) % 2
                S.op(DVE, lambda: V.tensor_copy(out=kvst[sg][:, bi % 4, :], in_=pk[:, 0:256]), reads=[pkR], writes=[kvstR[sg]],
                     add=(bi % 4 != 0))
                S.op(POOL, lambda: G.tensor_copy(out=vt[:, bi, 0:128], in_=kvst[sg][:, bi % 4, 128:256]),
                     reads=[kvstR[sg]], writes=[VdR[qi][bi // 4]], add=(bi % 4 != 0))
                if bi % 4 == 3:
                    r0 = s * T + (bi - 3) * 128
                    kdst = kd_p[r0:r0 + 512, h * 128:(h + 1) * 128].rearrange("(b p) e -> p b e", p=128)
                    vdst = vd_p[r0:r0 + 512, h * 128:(h + 1) * 128].rearrange("(b p) e -> p b e", p=128)
                    S.dma(POOL, kdst, kvst[sg][:, :, 0:128], reads=[kvstR[sg]])
                    S.dma(POOL, vdst, kvst[sg][:, :, 128:256], reads=[kvstR[sg]])
            for c0 in range(0, T, 512):
                pieces.append(lambda c0=c0: p_proj(c0, 'q'))
                pieces.append(lambda c0=c0: p_proj(c0, 'k'))
                for bi in range(c0 // 128, c0 // 128 + 4):
                    pieces.append(lambda bi=bi: p_kv(bi))
            return pieces

        for pc in diff_setup(0):
            pc()
        for h in range(8):
            qi = h % 2
            units = []
            for m in range(2):
                units.append(dict(QT=QT[qi], qrows=(m * 64, (m + 1) * 64), qres=lambda vis, qi=qi: [QTR[qi][vis[0][1] // 512]],
                                  KT=KT[qi], kres=lambda kbi, qi=qi: [KTR[qi][kbi // 4]],
                                  V=Vd[qi], vres=lambda kbi, qi=qi: [VdR[qi][kbi // 4]], vc=130, accbase=4 * m))
            steps = []
            for grp in groups:
                gk = [k for k in kbs_all if k[0] <= grp[-1][0]]
                steps += attn_group(units, grp, gk, True, 'diff', h, defer=True)
                gi = grp[0][1] // 512
                steps.append(lambda grp=grp, gi=gi, h=h: diff_combine(
                    h, grp, lambda q0, ncols, h=h: doT[:, h, q0:q0 + ncols], lambda h=h, gi=gi: [doTR[h][gi]]))
            interleave(steps, diff_setup(h + 1) if h < 7 else [])

        if s == 0:
            bg_flush()
        fence(hTR, [r for g_ in moTR for r in g_])
        fence(VdR[1], VmR[0] + VmR[1])
        moT = hT
        for i_ in range(2):
            S.op(POOL, lambda i_=i_: G.memset(Vm[i_][:, :, 64:66], 1.0), writes=VmR[i_])
        for u in range(2):
            for c0 in range(0, T, 512):
                if (c0 // 512 + u) % 2 == 0:
                    S.op(ACT, lambda u=u, c0=c0: A.copy(out=KT[u][64:96, c0:c0 + 512], in_=krT[64:96, c0:c0 + 512]),
                         reads=[krTR[b] for b in range(c0 // 128, c0 // 128 + 4)], writes=[KTropeR[u], KTR[u][c0 // 512]], add=True)
                else:
                    S.op(DVE, lambda u=u, c0=c0: V.tensor_copy(out=KT[u][64:96, c0:c0 + 512], in_=krT[64:96, c0:c0 + 512]),
                         reads=[krTR[b] for b in range(c0 // 128, c0 // 128 + 4)], writes=[KTropeR[u], KTR[u][c0 // 512]], add=True)
        ropei = [0]

        def mla_setup(hh):
            u = hh % 2
            qt, ktile, vt = QT[u], KT[u], Vm[u]
            box = {}
            pieces = []

            def p_load():
                mw, mwR = wbuf()
                mwv = mw[:, 0:640].rearrange("p (c n) -> p c n", n=320)
                wload3("w_uqA", hh * 96, 96, 2, mwv[:, :, 0:96], mwR)
                wload3("w_uqB", hh * 96, 96, 2, mwv[:, :, 96:192], mwR, add=True)
                wload3("w_uk", hh * 64, 64, 2, mwv[:, :, 192:256], mwR, add=True)
                wload3("w_uv", hh * 64, 64, 2, mwv[:, :, 256:320], mwR, add=True)
                box['w'] = (mwv, mwR)
                p_tab(0)
            pieces.append(p_load)

            def p_tab(cc):
                ti = ropei[0] % 2
                ropei[0] += 1
                S.dma(SP, ropet[64:96, 1 + 2 * ti, :], cosq_p_d[:, cc:cc + 256], writes=[ropeT2R[ti]])
                S.dma(SP, ropet[64:96, 2 + 2 * ti, :], sinq_p_d[:, cc:cc + 256], writes=[ropeT2R[ti]], add=True)
                box[('tab', cc)] = ti

            def p_q(c0):
                mwv, mwR = box['w']
                pa, paR = Pb[0], PbR[0]
                pb_, pbR_ = P1t, TRR
                for c in range(2):
                    S.op(PE, lambda c=c: TE.matmul(pa[0:96, 0:512], mwv[:, c, 0:96], cqT[:, c, c0:c0 + 512],
                                                   start=(c == 0), stop=(c == 1)), reads=cqsrc(c0, 512) + [mwR], writes=[paR])
                for c in range(2):
                    S.op(PE, lambda c=c: TE.matmul(pb_[0:96, 0:512], mwv[:, c, 96:192], cqT[:, c, c0:c0 + 512],
                                                   start=(c == 0), stop=(c == 1)), reads=cqsrc(c0, 512) + [mwR], writes=[pbR_])
                S.op(DVE, lambda: V.tensor_scalar(out=qt[0:64, c0:c0 + 512], in0=pa[0:64, 0:512], scalar1=96 ** -0.5, scalar2=None,
                                                  op0=ALU.mult), reads=[paR], writes=[QTR[u][c0 // 512]])
                for hf in range(2):
                    cc = c0 + hf * 256
                    o_ = hf * 256
                    ti = box[('tab', cc)]
                    if cc + 256 < T:
                        p_tab(cc + 256)
                    S.op(DVE, lambda o_=o_, ti=ti: V.tensor_tensor(out=ropet[64:96, 0, :], in0=pb_[64:96, o_:o_ + 256],
                                                                   in1=ropet[64:96, 2 + 2 * ti, :], op=ALU.mult),
                         reads=[pbR_, ropeT2R[ti]], writes=[ropeR])
                    S.op(DVE, lambda o_=o_, ti=ti: V.tensor_tensor(out=pa[64:96, o_:o_ + 256], in0=pa[64:96, o_:o_ + 256],
                                                                   in1=ropet[64:96, 1 + 2 * ti, :], op=ALU.mult),
                         reads=[paR, ropeT2R[ti]], writes=[paR])
                    S.op(DVE, lambda o_=o_, cc=cc: V.tensor_tensor(out=qt[64:96, cc:cc + 256], in0=ropet[64:96, 0, :],
                                                                   in1=pa[64:96, o_:o_ + 256], op=ALU.add),
                         reads=[ropeR, paR], writes=[QTR[u][c0 // 512]], add=True)

            def p_k(c0):
                mwv, mwR = box['w']
                pk, pkR = sbank()
                for c in range(2):
                    S.op(PE, lambda c=c: TE.matmul(pk[0:64, 0:512], mwv[:, c, 192:256], ckvT[:, c, c0:c0 + 512],
                                                   start=(c == 0), stop=(c == 1)),
                         reads=[ckvTR[b] for b in range(c0 // 128, c0 // 128 + 4)] + [mwR], writes=[pkR])
                S.op(DVE, lambda: V.tensor_copy(out=ktile[0:64, c0:c0 + 512], in_=pk[0:64, 0:512]), reads=[pkR],
                     writes=[KTR[u][c0 // 512]])

            def p_v(b8):
                mwv, mwR = box['w']
                nb8 = min(8, NB - b8)
                pk, pkR = sbank()
                for j in range(nb8):
                    kcol = (b8 + j) * 128
                    for c in range(2):
                        S.op(PE, lambda c=c, j=j, kcol=kcol: TE.matmul(
                            pk[:, j * 64:(j + 1) * 64], ckvT[:, c, kcol:kcol + 128], mwv[:, c, 256:320],
                            start=(c == 0), stop=(c == 1)), reads=[ckvTR[kcol // 128], mwR], writes=[pkR], add=(j + c > 0))
                S.op(DVE, lambda: V.tensor_copy(out=vt[:, b8:b8 + nb8, 0:64], in_=pk[:, 0:nb8 * 64].rearrange("p (b e) -> p b e", e=64)),
                     reads=[pkR], writes=[VmR[u][g_] for g_ in range(b8 // 4, (b8 + nb8 + 3) // 4)])
            for c0 in range(0, T, 512):
                pieces.append(lambda c0=c0: p_q(c0))
                pieces.append(lambda c0=c0: p_k(c0))
            for b8 in range(0, NB, 8):
                pieces.append(lambda b8=b8: p_v(b8))
            return pieces

        def mla_head_combine(hh, qbs):
            u = hh % 2
            p = hh // 2
            acc_copy_out([4 * u + q_[3] for q_ in qbs], 128)
            for (qbi, qcol, nq, slot) in qbs:
                c = cmb[slot]; cr = cmbR[slot]; rr = c['rr']
                a = acc_sb(4 * u + slot, nq, 66)
                S.op(DVE, lambda: V.reciprocal(out=rr[0:nq, u:u + 1], in_=a[:, 64:65]), reads=[accsR[(4 * u + slot) // 3]], writes=[cr['rr']])
            for (qbi, qcol, nq, slot) in qbs:
                c = cmb[slot]; cr = cmbR[slot]; rr = c['rr']
                a = acc_sb(4 * u + slot, nq, 66)
                S.op(DVE, lambda: V.tensor_scalar(out=c['ob'][0:nq, u * 64:(u + 1) * 64], in0=a[:, 0:64], scalar1=rr[0:nq, u:u + 1],
                                                  scalar2=None, op0=ALU.mult), reads=[accsR[(4 * u + slot) // 3], cr['rr']], writes=[cr['ob']])
            for (qbi, qcol, nq, slot) in qbs:
                c = cmb[slot]; cr = cmbR[slot]
                S.op(PE, lambda: TE.transpose(TR[:, slot * 128:slot * 128 + nq], c['ob'][0:nq, :], ident[0:nq, 0:nq]),
                     reads=[cr['ob'], constR], writes=[TRR], add=(slot != qbs[0][3]))
            q0 = qbs[0][1]
            gi = q0 // 512
            S.op(DVE, lambda: V.tensor_copy(out=moT[u * 64:(u + 1) * 64, p, q0:q0 + 512], in_=TR[u * 64:(u + 1) * 64, 0:512]),
                 reads=[TRR], writes=[moTR[p][gi]], add=(u == 1))

        for pc in mla_setup(0):
            pc()
        for hh in range(16):
            u = hh % 2
            units = [dict(QT=QT[u], qrows=(0, 96), qres=lambda vis, u=u: [QTR[u][vis[0][1] // 512]],
                          KT=KT[u], kres=lambda kbi, u=u: [KTR[u][kbi // 4], KTropeR[u]],
                          V=Vm[u], vres=lambda kbi, u=u: [VmR[u][kbi // 4]], vc=66, accbase=4 * u)]
            steps = []
            for grp in groups:
                gk = [k for k in kbs_all if k[0] <= grp[-1][0]]
                steps += attn_group(units, grp, gk, True, 'mla', 0, defer=True)
                steps.append(lambda grp=grp, hh=hh: mla_head_combine(hh, grp))
            interleave(steps, mla_setup(hh + 1) if hh < 15 else [])

    def run_sequence(is_prompt, s):
        if STOP <= 0:
            return
        if is_prompt:
            tq = T
            qblocks = [(b, b * 128, 128) for b in range(NB)]
        else:
            tq = DEC
            qblocks = [(PAST // 128, 0, DEC)]
        nqb = len(qblocks)

        fence([r for g_ in moTR for r in g_], hTR)
        fence(uTR + [mTR] + sigR + [emR], ARES)
        S.op(POOL, lambda: G.memset(Vd[0][:, :, 128:130], 1.0), writes=VdR[0])
        if not is_prompt:
            fence(VmR[0] + VmR[1], VdR[1])
            S.op(POOL, lambda: G.memset(Vd[1][:, :, 128:130], 1.0), writes=VdR[1])
        S.dma(SP, gtmp[:], g_mix.broadcast_to([128, D]), writes=[gtmpR])
        wtok, wtokR = wbuf()
        wtv = wtok[:, 0:4096].rearrange("p (c n) -> p c n", n=512)
        wload3("w_in", COL_CQ, 512, 8, wtv, wtokR)
        wload3("w_in", COL_KR, 32, 8, wtok2[:, :, 0:32], wtok2R)
        wload3("w_in", COL_KRS, 32, 8, wtok2[:, :, 32:64], wtok2R, add=True)
        for bi, (qbi, qcol, nq) in enumerate(qblocks):
            if is_prompt:
                row0 = s * T + qcol
                phase1(xp[row0:row0 + nq, :], nq, bi, qcol, wtv, wtokR, ckv_p[row0:row0 + nq, :], kr_p[row0:row0 + nq, :],
                       cosk[0:nq, bi, :], sink[0:nq, bi, :], qcol)
            else:
                phase1(xs[0:nq, :], nq, bi, 0, wtv, wtokR, ckv_s[0:nq, :], kr_s[0:nq, :],
                       cosks[0:nq, :], sinks[0:nq, :], NEWC)

        if EARLY and is_prompt and s == 0:
            build_bias()
            cast_weights(["w_uqA", "w_uqB", "w_uk", "w_uv", "w_od", "w_om", "w_out", "w_up", "w_dn"], defer=True)
            if NSEQ == 1:
                cast_caches(defer=True)
        if EARLY and is_prompt and s == 1:
            cast_caches(defer=True)
        if not is_prompt:
            bg_flush()
        if STOP <= 1:
            return
        hsrc = lambda c0, n: [hTR[b] for b in range(c0 // 128, (c0 + n + 127) // 128)]
        groups = []
        if is_prompt:
            for g in range(NG):
                groups.append([(4 * g + i, (4 * g + i) * 128, 128, i) for i in range(4)])
        else:
            groups.append([(PAST // 128, 0, DEC, 0)])

        if is_prompt:
            segs = [('new', 0, T)]
        else:
            segs = [('cache', k0, min(TKm, PAST - k0)) for k0 in range(0, PAST, TKm)] + [('new', PAST, DEC)]

        if is_prompt:
            prompt_attention(s)
        for h in range(8):
            if is_prompt:
                break
            if STOP <= 2 and h >= 1:
                break
            hw, hwR = wbuf()
            hwv = hw[:, 0:3072].rearrange("p (c n) -> p c n", n=384)
            wload3("w_in", COL_Q + h * 128, 128, 8, hwv[:, :, 0:128], hwR)
            wload3("w_in", COL_K + h * 128, 128, 8, hwv[:, :, 128:256], hwR, add=True)
            wload3("w_in", COL_V + h * 128, 128, 8, hwv[:, :, 256:384], hwR, add=True)
            qi = h % 2
            qt, ktile, vt = QT[qi], KT[qi], Vd[qi]

            def evq(pk, pkR, c0, n, qt=qt, qi=qi):
                S.op(ACT, lambda: A.activation(out=qt[:, c0:c0 + n], in_=pk[:, 0:n], func=AF.Copy, scale=0.125),
                     reads=[pkR], writes=[QTR[qi][c0 // 512]])
            DP = int(os.environ.get("DEV_P", "255"))
            if DP & 2:
                proj_fm(qt, 128, hwv, 0, 128, hT, 8, tq, hsrc, hwR, None, evq)

            for si, (skind, k0, klen) in enumerate(segs):
                nkb = (klen + 127) // 128
                if skind == 'new':
                    kbase = 0 if is_prompt else 0

                    def evk(pk, pkR, c0, n, ktile=ktile, qi=qi):
                        S.op(DVE, lambda: V.tensor_copy(out=ktile[:, c0:c0 + n], in_=pk[:, 0:n]),
                             reads=[pkR], writes=[KTR[qi][c0 // 512], KTropeR[qi]])
                    if DP & 4:
                        proj_fm(ktile, 128, hwv, 128, 128, hT, 8, tq, hsrc, hwR, None, evk)
                    for bi, (qbi, qcol, nq) in enumerate(qblocks):
                        if not (DP & 8):
                            break
                        pk, pkR = pbank()
                        for c in range(8):
                            S.op(PE, lambda c=c: TE.matmul(pk[0:nq, 0:256], hT[:, c, qcol:qcol + nq], hwv[:, c, 128:384],
                                                           start=(c == 0), stop=(c == 7)),
                                 reads=[hTR[qcol // 128], hwR], writes=[pkR])
                        sg = (bi // 4) % 2
                        DQ = int(os.environ.get("DEV_Q", "3"))
                        if DQ & 1:
                            S.op(ACT, lambda: A.copy(out=kvst[sg][0:nq, bi % 4, :], in_=pk[0:nq, 0:256]), reads=[pkR], writes=[kvstR[sg]],
                                 add=(bi % 4 != 0))
                        if DQ & 2:
                            S.op(POOL, lambda: G.tensor_copy(out=vt[0:nq, bi, 0:128], in_=kvst[sg][0:nq, bi % 4, 128:256]),
                                 reads=[kvstR[sg]], writes=[VdR[qi][bi // 4]], add=(bi % 4 != 0))
                        if (bi % 4 == 3 or bi == nqb - 1) and (DP & 16):
                            b0_ = bi - (bi % 4)
                            nb_ = bi - b0_ + 1
                            if is_prompt:
                                r0 = s * T + b0_ * 128
                                kdst = kd_p[r0:r0 + nb_ * 128, h * 128:(h + 1) * 128].rearrange("(b p) e -> p b e", p=128)
                                vdst = vd_p[r0:r0 + nb_ * 128, h * 128:(h + 1) * 128].rearrange("(b p) e -> p b e", p=128)
                                S.dma(POOL, kdst, kvst[sg][:, 0:nb_, 0:128], reads=[kvstR[sg]])
                                S.dma(POOL, vdst, kvst[sg][:, 0:nb_, 128:256], reads=[kvstR[sg]])
                            else:
                                S.dma(POOL, kd_s[0:nq, h * 128:(h + 1) * 128], kvst[sg][0:nq, 0, 0:128], reads=[kvstR[sg]])
                                S.dma(POOL, vd_s[0:nq, h * 128:(h + 1) * 128], kvst[sg][0:nq, 0, 128:256], reads=[kvstR[sg]])
                    kbs = [(qbi, qcol, nq, bi) for bi, (qbi, qcol, nq) in enumerate(qblocks)]
                else:
                    S.dma(SP, vt[:, 0:nkb, 0:128], cb['cv'][k0:k0 + klen, h * 128:(h + 1) * 128].rearrange("(b p) e -> p b e", p=128),
                          reads=[cacheR['cv']], writes=VdR[qi][0:max(1, nkb // 4)])
                    for b8 in range(0, nkb, 8):
                        nb8 = min(8, nkb - b8)
                        kstg, kstgR = next_kstg()
                        S.dma(SP, kstg[:, 0:nb8, :],
                              cb['ck'][k0 + b8 * 128:k0 + (b8 + nb8) * 128, h * 128:(h + 1) * 128].rearrange("(b p) e -> p b e", p=128),
                              reads=[cacheR['ck']], writes=[kstgR])
                        for j in range(nb8):
                            S.op(PE, lambda j=j: TE.transpose(TR[:, j * 128:(j + 1) * 128], kstg[:, j, :], ident[:, :]),
                                 reads=[kstgR, constR], writes=[TRR], add=(j > 0))
                        S.op(DVE, lambda b8=b8, nb8=nb8: V.tensor_copy(out=ktile[:, b8 * 128:(b8 + nb8) * 128], in_=TR[:, 0:nb8 * 128]),
                             reads=[TRR], writes=KTR[qi][b8 // 4:(b8 + nb8 + 3) // 4] + [KTropeR[qi]], add=False)
                    kbs = [(k0 // 128 + j, j * 128, 128, j) for j in range(nkb)]
                units = []
                for m in range(2):
                    units.append(dict(QT=qt, qrows=(m * 64, (m + 1) * 64), qres=lambda vis, qi=qi: [QTR[qi][v[1] // 512] for v in vis][:1],
                                      KT=ktile, kres=lambda kbi, qi=qi, kbs=kbs: [KTR[qi][min(len(KTR[qi]) - 1, [k[1] for k in kbs if k[0] == kbi][0] // 512)]],
                                      V=vt, vres=lambda kbi, kbs=kbs: [VdR[qi][min(len(VdR[qi]) - 1, [k[3] for k in kbs if k[0] == kbi][0] // 4)]],
                                      vc=130, accbase=4 * m))
                SUB = int(os.environ.get("DEV_SUB", "9"))
                for grp in groups:
                    if SUB <= 1:
                        break
                    gk = [k for k in kbs if (not is_prompt) or k[0] <= grp[-1][0]]
                    attn_group_sample(units, grp[0], gk, 'diff', h, first_seg=(si == 0), last_seg=(si == len(segs) - 1))
                    if si == len(segs) - 1 and SUB > 2:
                        gi = grp[0][1] // 512
                        diff_combine(h, grp, lambda q0, ncols, h=h: doT[:, h, q0:q0 + ncols], lambda h=h, gi=gi: [doTR[h][gi]])

        if STOP <= 3:
            return
        fence(hTR, [r for g_ in moTR for r in g_])
        moT = hT
        cqsrc = lambda c0, n: [cqTR[b] for b in range(c0 // 128, (c0 + n + 127) // 128)]
        cq_d, sq_d = (cosq_p_d, sinq_p_d) if is_prompt else (cosq_s_d, sinq_s_d)
        if not is_prompt:
            fence(VdR[1], VmR[0] + VmR[1])
            for i_ in range(2):
                S.op(POOL, lambda i_=i_: G.memset(Vm[i_][:, :, 64:66], 1.0), writes=VmR[i_])
        for p in range(8):
            if is_prompt:
                break
            mw, mwR = wbuf()
            mwv = mw[:, 0:1280].rearrange("p (c n) -> p c n", n=640)
            wload3("w_uqA", p * 192, 192, 2, mwv[:, :, 0:192], mwR)
            wload3("w_uqB", p * 192, 192, 2, mwv[:, :, 192:384], mwR, add=True)
            wload3("w_uk", p * 128, 128, 2, mwv[:, :, 384:512], mwR, add=True)
            wload3("w_uv", p * 128, 128, 2, mwv[:, :, 512:640], mwR, add=True)
            for u in range(2):
                qt = QT[u]
                for c0 in range(0, tq, 512):
                    n = min(512, tq - c0)
                    pa, paR = Pb[0], PbR[0]
                    pb_, pbR_ = Pb[3], PbR[3]
                    for c in range(2):
                        S.op(PE, lambda c=c: TE.matmul(pa[0:96, 0:n], mwv[:, c, u * 96:(u + 1) * 96], cqT[:, c, c0:c0 + n],
                                                       start=(c == 0), stop=(c == 1)), reads=cqsrc(c0, n) + [mwR], writes=[paR])
                    for c in range(2):
                        S.op(PE, lambda c=c: TE.matmul(pb_[0:96, 0:n], mwv[:, c, 192 + u * 96:192 + (u + 1) * 96], cqT[:, c, c0:c0 + n],
                                                       start=(c == 0), stop=(c == 1)), reads=cqsrc(c0, n) + [mwR], writes=[pbR_])
                    S.op(ACT, lambda: A.activation(out=qt[0:64, c0:c0 + n], in_=pa[0:64, 0:n], func=AF.Copy, scale=96 ** -0.5),
                         reads=[paR], writes=[QTR[u][c0 // 512]])
                    S.dma(SP, ropet[64:96, 2, 0:n], cq_d[:, c0:c0 + n], writes=[ropeTR])
                    S.dma(SP, ropet[64:96, 3, 0:n], sq_d[:, c0:c0 + n], writes=[ropeTR], add=True)
                    S.op(DVE, lambda: V.tensor_tensor(out=ropet[64:96, 0, 0:n], in0=pb_[64:96, 0:n], in1=ropet[64:96, 3, 0:n], op=ALU.mult),
                         reads=[pbR_, ropeTR], writes=[ropeR])
                    S.op(DVE, lambda: V.tensor_tensor(out=ropet[64:96, 1, 0:n], in0=pa[64:96, 0:n], in1=ropet[64:96, 2, 0:n], op=ALU.mult),
                         reads=[paR, ropeTR], writes=[ropeR], add=True)
                    S.op(DVE, lambda: V.tensor_tensor(out=qt[64:96, c0:c0 + n], in0=ropet[64:96, 0, 0:n], in1=ropet[64:96, 1, 0:n], op=ALU.add),
                         reads=[ropeR], writes=[QTR[u][c0 // 512]], add=True)
            for si, (skind, k0, klen) in enumerate(segs):
                nkb = (klen + 127) // 128
                if skind == 'cache' and p > 0:
                    S.dma(SP, ckvT[:, :, 0:klen], ckvT_d[si][:, :, 0:klen], reads=[latR[si]], writes=ckvTR[0:nkb])
                    S.dma(SP, krT[64:96, 0:klen], krT_d[si][:, 0:klen], reads=[latR[si]], writes=krTR[0:nkb])
                    kbs = [(k0 // 128 + j, j * 128, 128, j) for j in range(nkb)]
                    src0 = 0
                elif skind == 'cache':
                    for b4 in range(0, nkb, 4):
                        nb4 = min(4, nkb - b4)
                        kstg, kstgR = next_kstg()
                        st_ = kstg[:, 0:8, :].rearrange("p (b c) e -> p b (c e)", c=2)
                        S.dma(SP, st_[:, 0:nb4, :], cb['cckv'][k0 + b4 * 128:k0 + (b4 + nb4) * 128, :].rearrange("(b p) e -> p b e", p=128),
                              reads=[cacheR['cckv']], writes=[kstgR])
                        for j in range(nb4):
                            for c in range(2):
                                S.op(PE, lambda j=j, c=c: TE.transpose(TR[:, (j * 2 + c) * 128:(j * 2 + c + 1) * 128],
                                                                       st_[:, j, c * 128:(c + 1) * 128], ident[:, :]),
                                     reads=[kstgR, constR], writes=[TRR], add=(j + c > 0))
                        trv = TR[:, 0:nb4 * 256].rearrange("p (b c t) -> p c b t", c=2, t=128)
                        for c in range(2):
                            S.op(DVE, lambda c=c, b4=b4, nb4=nb4, trv=trv: V.tensor_copy(
                                out=ckvT[:, c, b4 * 128:(b4 + nb4) * 128].rearrange("p (b t) -> p b t", t=128), in_=trv[:, c, 0:nb4, :]),
                                reads=[TRR], writes=ckvTR[b4:b4 + nb4], add=(c > 0))
                    for b8 in range(0, nkb, 8):
                        nb8 = min(8, nkb - b8)
                        kstg, kstgR = next_kstg()
                        S.op(POOL, lambda: G.memset(kstg[:, :, 0:64], 0.0), writes=[kstgR])
                        S.dma(SP, kstg[:, 0:nb8, 64:96], cb['ckr'][k0 + b8 * 128:k0 + (b8 + nb8) * 128, :].rearrange("(b p) e -> p b e", p=128),
                              reads=[kstgR, cacheR['ckr']], writes=[kstgR], add=True)
                        for j in range(nb8):
                            S.op(PE, lambda j=j: TE.transpose(TR[0:96, j * 128:(j + 1) * 128], kstg[:, j, 0:96], ident[:, :]),
                                 reads=[kstgR, constR], writes=[TRR], add=(j > 0))
                        S.op(DVE, lambda b8=b8, nb8=nb8: V.tensor_copy(out=krT[64:96, b8 * 128:(b8 + nb8) * 128], in_=TR[64:96, 0:nb8 * 128]),
                             reads=[TRR], writes=krTR[b8:b8 + nb8])
                    S.dma(POOL, ckvT_d[si][:, :, 0:klen], ckvT[:, :, 0:klen], reads=ckvTR[0:nkb], writes=[latR[si]])
                    S.dma(POOL, krT_d[si][:, 0:klen], krT[64:96, 0:klen], reads=krTR[0:nkb], writes=[latR[si]], add=True)
                    kbs = [(k0 // 128 + j, j * 128, 128, j) for j in range(nkb)]
                    src0 = 0
                elif not is_prompt:
                    kbs = [(PAST // 128, 0, DEC, 0)]
                    src0 = NEWC
                else:
                    kbs = [(qbi, qcol, nq, bi) for bi, (qbi, qcol, nq) in enumerate(qblocks)]
                    src0 = 0
                kspan = kbs[-1][1] + kbs[-1][2]
                lsrc = lambda c0, n: [ckvTR[b] for b in range((src0 + c0) // 128, (src0 + c0 + n + 127) // 128)]
                units = []
                for u in range(2):
                    ktile, vt = KT[u], Vm[u]
                    for c0 in range(0, kspan, 512):
                        n = min(512, kspan - c0)
                        pk, pkR = pbank()
                        for c in range(2):
                            S.op(PE, lambda c=c: TE.matmul(pk[0:64, 0:n], mwv[:, c, 384 + u * 64:384 + (u + 1) * 64],
                                                           ckvT[:, c, src0 + c0:src0 + c0 + n], start=(c == 0), stop=(c == 1)),
                                 reads=lsrc(c0, n) + [mwR], writes=[pkR])
                        S.op(DVE, lambda: V.tensor_copy(out=ktile[0:64, c0:c0 + n], in_=pk[0:64, 0:n]), reads=[pkR],
                             writes=[KTR[u][min(len(KTR[u]) - 1, c0 // 512)]])
                    S.op(DVE, lambda: V.tensor_copy(out=ktile[64:96, 0:kspan], in_=krT[64:96, src0:src0 + kspan]),
                         reads=[krTR[b] for b in range(src0 // 128, (src0 + kspan + 127) // 128)], writes=[KTropeR[u]] + KTR[u], add=True)
                    for b8 in range(0, len(kbs), 8):
                        sub = kbs[b8:b8 + 8]
                        pk, pkR = pbank()
                        for j, (kbi, kcol, nk, vblk) in enumerate(sub):
                            for c in range(2):
                                S.op(PE, lambda c=c, j=j, kcol=kcol, nk=nk: TE.matmul(
                                    pk[0:nk, j * 64:(j + 1) * 64], ckvT[:, c, src0 + kcol:src0 + kcol + nk],
                                    mwv[:, c, 512 + u * 64:512 + (u + 1) * 64], start=(c == 0), stop=(c == 1)),
                                    reads=[ckvTR[(src0 + kcol) // 128], mwR], writes=[pkR], add=(j + c > 0))
                        nk0 = sub[0][2]
                        v0 = sub[0][3]
                        S.op(ACT, lambda sub=sub, nk0=nk0, v0=v0, pk=pk: A.copy(
                            out=vt[0:nk0, v0:v0 + len(sub), 0:64], in_=pk[0:nk0, 0:len(sub) * 64].rearrange("p (b e) -> p b e", e=64)),
                            reads=[pkR], writes=[VmR[u][min(len(VmR[u]) - 1, g_)] for g_ in range(v0 // 4, (v0 + len(sub) + 3) // 4)])
                    units.append(dict(QT=QT[u], qrows=(0, 96), qres=lambda vis, u=u: [QTR[u][vis[0][1] // 512]],
                                      KT=ktile,
                                      kres=lambda kbi, u=u, kbs=kbs: [KTR[u][min(len(KTR[u]) - 1, [k[1] for k in kbs if k[0] == kbi][0] // 512)], KTropeR[u]],
                                      V=vt,
                                      vres=lambda kbi, u=u, kbs=kbs: [VmR[u][min(len(VmR[u]) - 1, [k[3] for k in kbs if k[0] == kbi][0] // 4)]],
                                      vc=66, accbase=4 * u))
                for grp in groups:
                    gk = [k for k in kbs if (not is_prompt) or k[0] <= grp[-1][0]]
                    attn_group_sample(units, grp[0], gk, 'mla', 0, first_seg=(si == 0), last_seg=(si == len(segs) - 1))
                    if si == len(segs) - 1:
                        gi = grp[0][1] // 512
                        mla_combine(p, grp, lambda q0, ncols, p=p: moT[:, p, q0:q0 + ncols], lambda p=p, gi=gi: [moTR[p][gi]])

        if STOP <= 4:
            return
        fence(ARES, uTR + [mTR] + sigR)
        def chunk_blks(c0):
            n = min(512, tq - c0)
            nblk = (n + 127) // 128
            return [(c0 + b * 128, min(128, n - b * 128)) for b in range(nblk)]

        def step_a_block(c0, b):
            col, nq = chunk_blks(c0)[b]
            ot, otR = ost()
            xt = ot[0:nq, :]
            src = xp[s * T + col:s * T + col + nq, :] if is_prompt else xs[0:nq, :]
            S.dma(SP, xt, src, writes=[otR])
            sc, sres = statcol()
            S.op(DVE, lambda: V.scalar_tensor_tensor(out=junk[0:nq, :], in0=xt, scalar=1.0, in1=xt, op0=ALU.mult, op1=ALU.mult,
                                                     accum_out=sc(nq, 0)), reads=[otR], writes=JK + [sres])
            rstd_from_sumsq(nq, sc(nq, 0), sc(nq, 1), 1.0 / D, sres)
            if b == 0:
                S.dma(SP, gtmp[:], g_mix.broadcast_to([128, D]), writes=[gtmpR])
            S.op(DVE, lambda: V.scalar_tensor_tensor(out=hb[0:nq, :], in0=xt, scalar=sc(nq, 1), in1=gtmp[0:nq, :],
                                                     op0=ALU.mult, op1=ALU.mult), reads=[otR, sres, gtmpR], writes=[hbR])
            transposes_to(lambda b=b, nq=nq: mT[:, :, b * 128:b * 128 + nq], hb, nq, 8, 128, hbR, [mTR], ACT)

        for b in range(len(chunk_blks(0))):
            step_a_block(0, b)
        for c0 in range(0, tq, 512):
            n = min(512, tq - c0)
            gi = c0 // 512
            blks = chunk_blks(c0)
            nblk = len(blks)
            has_next = c0 + 512 < tq
            for b, (col, nq) in enumerate(blks):
                src = xp[s * T + col:s * T + col + nq, :] if is_prompt else xs[0:nq, :]
                S.dma(SP, xres[0:nq, b, :], src, writes=[xresR[b]])
            mg = uT
            moT_ = hT
            for dc in range(8):
                gw, gwR = wbuf()
                gv = gw[:, 0:4096].rearrange("p (k c n) -> p k c n", k=4, n=128)
                wload3("w_od", dc * 128, 128, 8, gv[:, 0], gwR)
                wload3("w_om", dc * 128, 128, 8, gv[:, 1], gwR, add=True)
                wload3("w_in", COL_GD + dc * 128, 128, 8, gv[:, 2], gwR, add=True)
                wload3("w_in", COL_GM + dc * 128, 128, 8, gv[:, 3], gwR, add=True)
                banks = [(Sb[0], SbR[0]), (Sb[1], SbR[1]), (Pb[0], PbR[0]), (Sb[2], SbR[2])]
                srcs = [(doT, doTR), (moT, moTR)]
                for k in range(2):
                    bk, bkR = banks[k]
                    src, srcR = srcs[k]
                    for c in range(8):
                        S.op(PE, lambda c=c, bk=bk, src=src, k=k: TE.matmul(bk[:, 0:n], gv[:, k, c, :], src[:, c, c0:c0 + n],
                                                                            start=(c == 0), stop=(c == 7)),
                             reads=[srcR[c][gi], gwR], writes=[bkR])
                for k in range(2, 4):
                    bk, bkR = banks[k]
                    for c in range(8):
                        S.op(PE, lambda c=c, bk=bk, k=k: TE.matmul(bk[:, 0:n], gv[:, k, c, :], mT[:, c, 0:n],
                                                                   start=(c == 0), stop=(c == 7)), reads=[mTR, gwR], writes=[bkR])
                for k in range(2):
                    bk, bkR = banks[2 + k]
                    S.op(ACT, lambda bk=bk, k=k: A.activation(out=sigb[k][:, 0:n], in_=bk[:, 0:n], func=AF.Sigmoid),
                         reads=[bkR], writes=[sigR[k]])
                S.op(DVE, lambda: V.tensor_tensor(out=sigb[0][:, 0:n], in0=sigb[0][:, 0:n], in1=Sb[0][:, 0:n], op=ALU.mult),
                     reads=[sigR[0], SbR[0]], writes=[sigR[0]])
                S.op(DVE, lambda: V.tensor_tensor(out=sigb[1][:, 0:n], in0=sigb[1][:, 0:n], in1=Sb[1][:, 0:n], op=ALU.mult),
                     reads=[sigR[1], SbR[1]], writes=[sigR[1]])
                S.op(DVE, lambda dc=dc: V.tensor_tensor(out=mg[:, dc, 0:n], in0=sigb[0][:, 0:n], in1=sigb[1][:, 0:n], op=ALU.add),
                     reads=[sigR[0], sigR[1]], writes=[uTR[dc]])
            for half in range(2):
                ow, owR = wbuf()
                ov = ow[:, 0:4096].rearrange("p (c n) -> p c n", n=512)
                wload3("w_out", half * 512, 512, 8, ov, owR)
                for b, (col, nq) in enumerate(blks):
                    pk, pkR = pbank()
                    for c in range(8):
                        S.op(PE, lambda c=c: TE.matmul(pk[0:nq, 0:512], mg[:, c, b * 128:b * 128 + nq], ov[:, c, :],
                                                       start=(c == 0), stop=(c == 7)), reads=[uTR[c], owR], writes=[pkR])
                    S.op(DVE, lambda: V.tensor_tensor(out=xres[0:nq, b, half * 512:(half + 1) * 512],
                                                      in0=xres[0:nq, b, half * 512:(half + 1) * 512], in1=pk[0:nq, 0:512], op=ALU.add),
                         reads=[xresR[b], pkR], writes=[xresR[b]])
            for b, (col, nq) in enumerate(blks):
                xt = xres[0:nq, b, :]
                sc, sres = statcol()
                S.op(DVE, lambda: V.scalar_tensor_tensor(out=junk[0:nq, :], in0=xt, scalar=1.0, in1=xt, op0=ALU.mult, op1=ALU.mult,
                                                         accum_out=sc(nq, 0)), reads=[xresR[b]], writes=JK + [sres])
                rstd_from_sumsq(nq, sc(nq, 0), sc(nq, 1), 1.0 / D, sres)
                if b == 0:
                    S.dma(SP, gtmp[:], g_mlp.broadcast_to([128, D]), writes=[gtmpR])
                S.op(DVE, lambda: V.scalar_tensor_tensor(out=hb[0:nq, :], in0=xt, scalar=sc(nq, 1), in1=gtmp[0:nq, :],
                                                         op0=ALU.mult, op1=ALU.mult), reads=[xresR[b], sres, gtmpR], writes=[hbR])
                transposes_to(lambda b=b, nq=nq: mT[:, :, b * 128:b * 128 + nq], hb, nq, 8, 128, hbR, [mTR], ACT)
            for f4 in range(8):
                uw, uwR = wbuf()
                uv = uw[:, 0:4096].rearrange("p (c n) -> p c n", n=512)
                wload3("w_up", f4 * 512, 512, 8, uv, uwR)
                for fi in range(4):
                    f = f4 * 4 + fi
                    pk, pkR = pbank()
                    for c in range(8):
                        S.op(PE, lambda c=c: TE.matmul(pk[:, 0:n], uv[:, c, fi * 128:(fi + 1) * 128], mT[:, c, 0:n],
                                                       start=(c == 0), stop=(c == 7)), reads=[mTR, uwR], writes=[pkR])
                    sg_ = sigb[f % 2]; sgR_ = sigR[f % 2]
                    S.op(ACT, lambda: A.activation(out=sg_[:, 0:n], in_=pk[:, 0:n], func=AF.Relu), reads=[pkR], writes=[sgR_])
                    S.op(DVE, lambda f=f: V.tensor_tensor(out=uT[:, f, 0:n], in0=sg_[:, 0:n], in1=sg_[:, 0:n], op=ALU.mult),
                         reads=[sgR_], writes=[uTR[f]])
            for half in range(2):
                accb = [(Sb[0], [SbR[0]]), (Sb[1], [SbR[1]]), (Ab[0], [_bankR[0]]), (Ab[1], [_bankR[1]])]
                for f8 in range(4):
                    dw, dwR = wbuf()
                    dv = dw[:, 0:4096].rearrange("p (c n) -> p c n", n=512)
                    src = wb["w_dn"].rearrange("(c p) n -> p c n", p=128)[:, f8 * 8:(f8 + 1) * 8, half * 512:(half + 1) * 512]
                    S.dma(SP, dv, src, reads=[wbR["w_dn"]], writes=[dwR])
                    for b, (col, nq) in enumerate(blks):
                        bk, bkR = accb[b]
                        for c in range(8):
                            f = f8 * 8 + c
                            S.op(PE, lambda c=c, f=f, bk=bk: TE.matmul(bk[0:nq, 0:512], uT[:, f, b * 128:b * 128 + nq], dv[:, c, :],
                                                                       start=(f == 0), stop=(f == 31)), reads=[uTR[f], dwR], writes=bkR)
                    if half == 0 and has_next and f8 < len(chunk_blks(c0 + 512)):
                        step_a_block(c0 + 512, f8)
                for b, (col, nq) in enumerate(blks):
                    bk, bkR = accb[b]
                    S.op(DVE, lambda bk=bk: V.tensor_tensor(out=xres[0:nq, b, half * 512:(half + 1) * 512],
                                                            in0=xres[0:nq, b, half * 512:(half + 1) * 512], in1=bk[0:nq, 0:512], op=ALU.add),
                         reads=[xresR[b]] + bkR, writes=[xresR[b]])
            for b, (col, nq) in enumerate(blks):
                xt = xres[0:nq, b, :]
                sc, sres = statcol()
                S.op(DVE, lambda: V.scalar_tensor_tensor(out=junk[0:nq, :], in0=xt, scalar=1.0, in1=xt, op0=ALU.mult, op1=ALU.mult,
                                                         accum_out=sc(nq, 0)), reads=[xresR[b]], writes=JK + [sres])
                rstd_from_sumsq(nq, sc(nq, 0), sc(nq, 1), 1.0 / D, sres)
                ot, otR = ost()
                if b == 0:
                    S.dma(SP, gtmp[:], g_fin.broadcast_to([128, D]), writes=[gtmpR])
                S.op(DVE, lambda: V.scalar_tensor_tensor(out=ot[0:nq, :], in0=xt, scalar=sc(nq, 1), in1=gtmp[0:nq, :],
                                                         op0=ALU.mult, op1=ALU.mult), reads=[xresR[b], sres, gtmpR], writes=[otR])
                dst = y_p[s * T + col:s * T + col + nq, :] if is_prompt else y_s[0:nq, :]
                S.dma(POOL, dst, ot[0:nq, :], reads=[otR])

    for s in range(NSEQ):
        run_sequence(True, s)
    if PAST > 0 and STOP > 5:
        run_sequence(False, 0)
    S.finish(POOL)
    S.finish(SP)
    return nc, S


def _prep_weights(inp):
    w_in = np.asarray(inp['w_in'][0], np.float32)
    kr = w_in[:, COL_KR:COL_KR + 32]
    w_in_ext = np.concatenate([w_in, kr[:, 16:32], kr[:, 0:16]], axis=1)
    uq = np.asarray(inp['mla_w_uq'][0], np.float32)
    uqA = uq.reshape(256, MH * 96)
    uqB = np.concatenate([uq[:, :, 0:64], uq[:, :, 80:96], uq[:, :, 64:80]], axis=2).reshape(256, MH * 96)
    lamv = np.concatenate([inp['lam_q1'][0], inp['lam_k1'][0], inp['lam_q2'][0], inp['lam_k2'][0]])[None, :]
    f = lambda a: np.ascontiguousarray(np.asarray(a, np.float32))
    return dict(
        w_in=f(w_in_ext), w_uqA=f(uqA), w_uqB=f(uqB),
        w_uk=f(np.asarray(inp['mla_w_uk'][0]).reshape(256, 1024)), w_uv=f(np.asarray(inp['mla_w_uv'][0]).reshape(256, 1024)),
        w_od=f(np.asarray(inp['w_o_diff'][0]).reshape(1024, D)), w_om=f(np.asarray(inp['w_o_mla'][0]).reshape(1024, D)),
        w_out=f(inp['w_out'][0]), w_up=f(inp['w_up'][0]), w_dn=f(inp['w_down'][0]),
        g_mix=f(inp['norm_mix'][0][None, :]), g_mlp=f(inp['norm_mlp'][0][None, :]), g_fin=f(np.asarray(inp['norm_final'])[None, :]),
        g_sub=f(inp['diff_subln'][0][None, :]), g_q=f(inp['mla_q_norm'][0][None, :]), g_kv=f(inp['mla_kv_norm'][0][None, :]),
        lamv=f(lamv), relb=f(np.asarray(inp['rel_bias']).reshape(1, 256)))


def run(inp, n_cores=8):
    x_prompt = np.asarray(inp['x_prompt'], np.float32)
    x_sample = np.asarray(inp['x_sample'], np.float32)
    B, T, _ = x_prompt.shape
    DB = x_sample.shape[0]
    PAST = inp['cache_diff_k'].shape[2]
    assert B % n_cores == 0 and DB == n_cores
    NSEQ = B // n_cores
    _, S1 = build(NSEQ, T, PAST)
    nc, S = build(NSEQ, T, PAST, needed=S1.used)
    common = _prep_weights(inp)
    common.update(_static_tables(T, PAST))
    in_maps = []
    for c in range(n_cores):
        m = dict(common)
        m['xp'] = np.ascontiguousarray(x_prompt[c * NSEQ:(c + 1) * NSEQ].reshape(NSEQ * T, D))
        m['xs'] = np.ascontiguousarray(x_sample[c])
        m['ck'] = np.ascontiguousarray(np.asarray(inp['cache_diff_k'][0, c], np.float32).reshape(PAST, 1024))
        m['cv'] = np.ascontiguousarray(np.asarray(inp['cache_diff_v'][0, c], np.float32).reshape(PAST, 1024))
        m['cckv'] = np.ascontiguousarray(np.asarray(inp['cache_mla_ckv'][0, c], np.float32))
        m['ckr'] = np.ascontiguousarray(np.asarray(inp['cache_mla_krope'][0, c], np.float32))
        in_maps.append(m)
    res = run_bass_kernel_spmd(nc, in_maps, core_ids=list(range(n_cores)))
    rs = res.results
    cat = lambda k: np.concatenate([r[k] for r in rs], axis=0)
    y_p = cat('y_p').reshape(B, T, D)
    y_s = np.stack([r['y_s'] for r in rs])
    kd_p = cat('kd_p').reshape(1, B, T, NH, 128)
    vd_p = cat('vd_p').reshape(1, B, T, NH, 128)
    ckv_p = cat('ckv_p').reshape(1, B, T, 256)
    kr_p = cat('kr_p').reshape(1, B, T, 32)
    kd_s = np.stack([r['kd_s'] for r in rs]).reshape(1, DB, DEC, NH, 128)
    vd_s = np.stack([r['vd_s'] for r in rs]).reshape(1, DB, DEC, NH, 128)
    ckv_s = np.stack([r['ckv_s'] for r in rs]).reshape(1, DB, DEC, 256)
    kr_s = np.stack([r['kr_s'] for r in rs]).reshape(1, DB, DEC, 32)
    return tuple(np.ascontiguousarray(a, dtype=np.float32) for a in
                 (y_p, y_s, kd_p, vd_p, ckv_p, kr_p, kd_s, vd_s, ckv_s, kr_s))


def kernel(**inputs):
    return run(inputs, 8)
```

```python
import math
import os
import numpy as np
import concourse.bass as bass
import concourse.mybir as mybir
from concourse.bass_utils import run_bass_kernel_spmd

F32 = mybir.dt.float32
BF16 = mybir.dt.bfloat16
ALU = mybir.AluOpType
AF = mybir.ActivationFunctionType

D = 1024
NH = 8
MH = 16
DFF = 4096
COL_Q, COL_K, COL_V, COL_CQ, COL_CKV, COL_KR, COL_GD, COL_GM, COL_KRS, NCOL = (
    0, 1024, 2048, 3072, 3328, 3584, 3616, 4640, 5664, 5696)
EPS = 1e-6
LAMBDA_INIT = 0.8 - 0.6 * math.exp(-0.3 * 0)
NEG = -30000.0
DEC = 16


class Res:
    __slots__ = ("name", "lw", "rd")

    def __init__(self, name=""):
        self.name = name
        self.lw = []
        self.rd = {}


class Eng:
    def __init__(self, S, name, eng, is_pe=False):
        self.name = name
        self.eng = eng
        self.sem = S.nc.alloc_semaphore("cs_" + name)
        self.count = 0
        self.semval = 0
        self.rank = {}
        self.waited = {}
        self.is_pe = is_pe


class Sched:
    def __init__(self, nc, needed=None, n_dma_sems=(28, 28)):
        self.nc = nc
        self.needed = needed
        self.used = {}
        self.pe = Eng(self, "pe", nc.tensor, True)
        self.act = Eng(self, "act", nc.scalar)
        self.dve = Eng(self, "dve", nc.vector)
        self.pool = Eng(self, "pool", nc.gpsimd)
        self.sp = Eng(self, "sp", nc.sync)
        self.dq = {}
        for q, n in zip((self.sp, self.pool), n_dma_sems):
            self.dq[q.name] = dict(sems=[nc.alloc_semaphore(f"dq_{q.name}_{i}") for i in range(n)],
                                   cnt=[0] * n, i=0)
        self.n_wait = 0
        self.n_inst = 0

    def _wait(self, E, dep):
        if dep[0] == 'eng':
            _, P, c = dep
            if P is E and E.is_pe:
                return
            key = P.name
            if E.waited.get(key, 0) >= c:
                return
            self.used.setdefault(P.name, set()).add(c)
            if self.needed is None:
                E.eng.wait_ge(P.sem, c)
            else:
                E.eng.wait_ge(P.sem, P.rank[c])
            E.waited[key] = c
        else:
            _, sem, v, key = dep
            if E.waited.get(key, 0) >= v:
                return
            E.eng.wait_ge(sem, v)
            E.waited[key] = v
        self.n_wait += 1

    def _deps(self, E, reads, writes, same_war=False):
        for r in reads:
            for d in r.lw:
                self._wait(E, d)
        for w in writes:
            for d in w.lw:
                self._wait(E, d)
            for d in list(w.rd.values()):
                if d[0] == 'eng' and d[1] is E and E.is_pe:
                    continue
                self._wait(E, d)

    def _record(self, me, reads, writes, add):
        key = me[1].name if me[0] == 'eng' else me[3]
        for r in reads:
            r.rd[key] = me
        for w in writes:
            if add:
                w.lw.append(me)
            else:
                w.lw = [me]
                w.rd = {}

    def op(self, E, fn, reads=(), writes=(), add=False):
        self._deps(E, reads, writes)
        inst = fn()
        E.count += 1
        if self.needed is None:
            inst.then_inc(E.sem, 1)
        elif E.count in self.needed.get(E.name, ()):
            inst.then_inc(E.sem, 1)
            E.semval += 1
            E.rank[E.count] = E.semval
        self._record(('eng', E, E.count), reads, writes, add)
        self.n_inst += 1
        return inst

    def dma(self, Q, out, in_, reads=(), writes=(), add=False):
        q = self.dq[Q.name]
        i = q['i']
        q['i'] = (i + 1) % len(q['sems'])
        sem = q['sems'][i]
        key = f"dq_{Q.name}_{i}"
        if q['cnt'][i] > 0:
            self._wait(Q, ('dma', sem, q['cnt'][i], key))
        self._deps(Q, reads, writes, same_war=True)
        inst = Q.eng.dma_start(out=out, in_=in_)
        q['cnt'][i] += 16
        inst.then_inc(sem, 16)
        self._record(('dma', sem, q['cnt'][i], key), reads, writes, add)
        self.n_inst += 1
        return inst

    def finish(self, E):
        for qn, q in self.dq.items():
            for i, sem in enumerate(q['sems']):
                if q['cnt'][i] > 0:
                    self._wait(E, ('dma', sem, q['cnt'][i], f"dq_{qn}_{i}"))


def fence(old, new):
    pend = {}

    def put(d):
        key = d[1].name if d[0] == 'eng' else d[3]
        cur = pend.get(key)
        val = d[2]
        if cur is None or cur[2] < val:
            pend[key] = d
    for r in old:
        for d in r.lw:
            put(d)
        for d in r.rd.values():
            put(d)
    for r in new:
        for key, d in pend.items():
            cur = r.rd.get(key)
            if cur is None or cur[2] < d[2]:
                r.rd[key] = d


def _bucket(rel):
    half, max_exact = 16, 8
    n = np.abs(rel)
    nf = np.maximum(n, max_exact).astype(np.float32)
    large = max_exact + (np.log(nf / np.float32(max_exact)) / np.float32(math.log(128 / max_exact))
                         * np.float32(half - max_exact)).astype(np.int32)
    large = np.minimum(large, half - 1)
    return np.where(rel > 0, half, 0) + np.where(n < max_exact, n, large)


def _static_tables(T, PAST):
    k = np.arange(128)[:, None]
    q = np.arange(128)[None, :]
    b0 = _bucket(k - q)
    b1 = _bucket(k - q - 128)
    E = np.zeros((47, 128, 128), np.float32)
    for b in range(32):
        E[b] = (b0 == b)
    for b in range(1, 16):
        E[31 + b] = (b1 == b)
    emask = np.ascontiguousarray(E.transpose(1, 0, 2))
    inv = (10000.0 ** (-np.arange(16, dtype=np.float32) * 2.0 / 32)).astype(np.float32)

    def tabs(pos):
        ang = pos.astype(np.float32)[:, None] * inv[None, :]
        c, s = np.cos(ang).astype(np.float32), np.sin(ang).astype(np.float32)
        return np.concatenate([c, c], 1), np.concatenate([-s, s], 1)
    cp, sp = tabs(np.arange(T))
    cs, ss = tabs(PAST + np.arange(DEC))
    sc = np.float32(96 ** -0.5)
    return dict(
        emask=emask,
        ident=np.eye(128, dtype=np.float32),
        cosk_p=np.ascontiguousarray(cp.reshape(T // 128, 128, 32).transpose(1, 0, 2)),
        sink_p=np.ascontiguousarray(sp.reshape(T // 128, 128, 32).transpose(1, 0, 2)),
        cosk_s=cs, sink_s=ss,
        cosq_p=np.ascontiguousarray((cp * sc).T), sinq_p=np.ascontiguousarray((sp * sc).T),
        cosq_s=np.ascontiguousarray((cs * sc).T), sinq_s=np.ascontiguousarray((ss * sc).T),
    )


def build(NSEQ, T, PAST, needed=None):
    assert T % 512 == 0 and PAST % 128 == 0
    NB = T // 128
    NG = T // 512
    nc = bass.Bass("TRN2", target_bir_lowering=False)
    S = Sched(nc, needed)
    PE, ACT, DVE, POOL, SP = S.pe, S.act, S.dve, S.pool, S.sp

    def din(name, shape):
        return nc.dram_tensor(name, list(shape), F32, kind="ExternalInput").ap()

    def dout(name, shape):
        return nc.dram_tensor(name, list(shape), F32, kind="ExternalOutput").ap()

    xp = din("xp", [NSEQ * T, D])
    xs = din("xs", [DEC, D])
    ck = din("ck", [PAST, 1024]); cv = din("cv", [PAST, 1024])
    cckv = din("cckv", [PAST, 256]); ckr = din("ckr", [PAST, 32])
    wsrc = dict(
        w_in=din("w_in", [D, NCOL]), w_uqA=din("w_uqA", [256, MH * 96]), w_uqB=din("w_uqB", [256, MH * 96]),
        w_uk=din("w_uk", [256, 1024]), w_uv=din("w_uv", [256, 1024]), w_od=din("w_od", [1024, D]),
        w_om=din("w_om", [1024, D]), w_out=din("w_out", [D, D]), w_up=din("w_up", [D, DFF]),
        w_dn=din("w_dn", [DFF, D]))
    g_mix = din("g_mix", [1, D]); g_mlp = din("g_mlp", [1, D]); g_fin = din("g_fin", [1, D])
    g_sub = din("g_sub", [1, 128]); g_q = din("g_q", [1, 256]); g_kv = din("g_kv", [1, 256])
    lamv = din("lamv", [1, 256]); relb = din("relb", [1, 256])
    emask_d = din("emask", [128, 47, 128]); ident_d = din("ident", [128, 128])
    cosk_p_d = din("cosk_p", [128, NB, 32]); sink_p_d = din("sink_p", [128, NB, 32])
    cosk_s_d = din("cosk_s", [DEC, 32]); sink_s_d = din("sink_s", [DEC, 32])
    cosq_p_d = din("cosq_p", [32, T]); sinq_p_d = din("sinq_p", [32, T])
    cosq_s_d = din("cosq_s", [32, DEC]); sinq_s_d = din("sinq_s", [32, DEC])

    y_p = dout("y_p", [NSEQ * T, D]); y_s = dout("y_s", [DEC, D])
    kd_p = dout("kd_p", [NSEQ * T, 1024]); vd_p = dout("vd_p", [NSEQ * T, 1024])
    ckv_p = dout("ckv_p", [NSEQ * T, 256]); kr_p = dout("kr_p", [NSEQ * T, 32])
    kd_s = dout("kd_s", [DEC, 1024]); vd_s = dout("vd_s", [DEC, 1024])
    ckv_s = dout("ckv_s", [DEC, 256]); kr_s = dout("kr_s", [DEC, 32])

    wb = {}
    wbR = {}
    for name, src in wsrc.items():
        wb[name] = nc.dram_tensor(name + "_b", list(src.shape), BF16).ap()
        wbR[name] = Res(name + "_b")

    cache_src = dict(ck=ck, cv=cv, cckv=cckv, ckr=ckr)
    cb = {k_: nc.dram_tensor(k_ + "_b", list(v_.shape), BF16).ap() for k_, v_ in cache_src.items()}
    cacheR = {k_: Res(k_ + "_b") for k_ in cache_src}
    NSEGS = max(1, (PAST + 2047) // 2048)
    ckvT_d = [nc.dram_tensor(f"ckvT_d{i}", [128, 2, 2048], BF16).ap() for i in range(NSEGS)]
    krT_d = [nc.dram_tensor(f"krT_d{i}", [32, 2048], BF16).ap() for i in range(NSEGS)]
    latR = [Res(f"lat{i}") for i in range(NSEGS)]

    def sb(name, shape, dt):
        return nc.alloc_sbuf_tensor(name, list(shape), dt)

    TK = 2048 if PAST > 0 else T
    TKm = max(T, min(TK, max(PAST, 128)))
    NBK = TKm // 128
    NEWC = TKm
    hT = sb("hT", [128, 8, T], BF16)
    doT = sb("doT", [128, 8, T], BF16)
    sizes = dict(cqT=2 * T, ckvT=2 * (TKm + 128), krT=TKm + 128, QT0=T, QT1=T, KT0=TKm, KT1=TKm, Vd0=NBK * 130,
                 Vm0=NBK * 66, Vm1=NBK * 66, PT0=512, PT1=512, PT2=512, PT3=512)
    offs = {}
    o_ = 0
    for k_, v_ in sizes.items():
        offs[k_] = o_
        o_ += (v_ + 15) // 16 * 16
    ARENA_N = max(o_, 16384 + 4096 + 2048)
    arena = sb("arena", [128, ARENA_N], BF16)

    def av(k_):
        return arena[:, offs[k_]:offs[k_] + sizes[k_]]
    cqT = av("cqT").rearrange("p (c t) -> p c t", c=2)
    ckvT = av("ckvT").rearrange("p (c t) -> p c t", c=2)
    krT = av("krT")
    QT = [av("QT0"), av("QT1")]
    KT = [av("KT0"), av("KT1")]
    Vd = [arena[:, offs["Vd0"]:offs["Vd0"] + NBK * 130].rearrange("p (b e) -> p b e", e=130)]
    Vm = [arena[:, offs[k_]:offs[k_] + NBK * 66].rearrange("p (b e) -> p b e", e=66) for k_ in ("Vm0", "Vm1")]
    assert offs["Vm1"] == offs["Vm0"] + NBK * 66 + (-(NBK * 66) % 16) and 2 * NBK * 66 >= NBK * 130
    Vd.append(arena[:, offs["Vm0"]:offs["Vm0"] + NBK * 130].rearrange("p (b e) -> p b e", e=130))
    NPT = 4
    PT = [av(f"PT{i}") for i in range(NPT)]
    uT = arena[:, 0:16384].rearrange("p (f t) -> p f t", t=512)
    mT = arena[:, 16384:20480].rearrange("p (c t) -> p c t", t=512)
    sigb = [arena[:, 20480 + i * 1024:20480 + (i + 1) * 1024].bitcast(F32) for i in range(2)]
    EARLY = T >= 1024
    if EARLY:
        emask = doT[:, :, :].rearrange("p h t -> p (h t)")[:, 0:47 * 128].rearrange("p (b q) -> p b q", q=128)
    else:
        emask = arena[:, 0:47 * 128].rearrange("p (b q) -> p b q", q=128)
    NWB = 3
    WBUF = [sb(f"wbuf{i}", [128, 4096], BF16) for i in range(NWB)]
    wtok2 = sb("wtok2", [128, 8, 64], BF16)
    xres = sb("xres", [128, 4, D], F32)
    hb = sb("hb", [128, D], BF16)
    zt = sb("zt", [128, 576], F32)
    ostage = [sb(f"ostage{i}", [128, D], F32) for i in range(2)]
    kvst = [ostage[i][:, :].rearrange("p (b e) -> p b e", e=256) for i in range(2)]
    smallb = sb("smallb", [128, 512], BF16)
    krpad = sb("krpad", [128, 96], BF16)
    krf = [sb(f"krf{i}", [128, 32], F32) for i in range(2)]
    stat = sb("stat", [128, 64], F32)
    cmb = [dict(t1=sb(f"cm_t1_{i}", [128, 128], F32), o=sb(f"cm_o_{i}", [128, 128], F32),
                ob=sb(f"cm_ob_{i}", [128, 128], BF16), rr=sb(f"cm_rr_{i}", [128, 8], F32)) for i in range(4)]
    accs = sb("accs", [128, 3, 390], F32)
    junk = accs[:, :, :].rearrange("p b e -> p (b e)")[:, 0:D]
    ropet = sb("ropet", [128, 5, 256], F32)
    gtmp = sb("gtmp", [128, D], F32)
    gsub_b = sb("gsub_b", [128, 128], F32); gq_b = sb("gq_b", [128, 256], F32); gkv_b = sb("gkv_b", [128, 256], F32)
    lam_b = sb("lam_b", [128, 256], F32); tb = sb("tb", [128, 256], F32)
    cst = sb("cst", [128, 16], F32)
    ident = sb("ident_b", [128, 128], BF16)
    B0 = xres[:, 0, :].rearrange("p (h q) -> p h q", q=128); B1 = xres[:, 1, :].rearrange("p (h q) -> p h q", q=128)
    Bh = [sb(f"Bh{i}", [128, 8, 128], BF16) for i in range(2)]
    Bl = [sb(f"Bl{i}", [128, 8, 128], BF16) for i in range(2)]
    M0 = sb("M0", [128, 128], BF16)
    cosk = sb("cosk", [128, NB, 32], F32); sink = sb("sink", [128, NB, 32], F32)
    cosks = sb("cosks", [128, 32], F32); sinks = sb("sinks", [128, 32], F32)
    kstg2 = [sb("kstg0", [128, 8, 128], BF16), hb[:, :].rearrange("p (b e) -> p b e", e=128)]

    Sb = [nc.alloc_psum_tensor(f"S{i}", [128, 512], F32) for i in range(3)]
    Ab = [nc.alloc_psum_tensor(f"A{i}", [128, 512], F32) for i in range(3)]
    Pb = [nc.alloc_psum_tensor(f"P{i}", [128, 512], F32) for i in range(1)]
    P1t = nc.alloc_psum_tensor("P1t", [128, 512], F32)
    TR = P1t[:, :].bitcast(BF16)
    SbR = [Res(f"S{i}") for i in range(3)]
    PbR = [Res(f"P{i}") for i in range(1)]
    Pb = Pb + Sb
    PbR = PbR + SbR
    TRR = Res("TR")
    _bankR = [Res(f"accbank{i}") for i in range(3)]
    accsR = [Res(f"accs{i}") for i in range(3)]
    JK = accsR
    accR = [_bankR[i // 3] for i in range(9)]

    def acc_ap(a, n, w):
        return Ab[a // 3][0:n, (a % 3) * 130:(a % 3) * 130 + w]

    def acc_sb(a, n, w):
        return accs[0:n, a // 3, (a % 3) * 130:(a % 3) * 130 + w]

    def acc_copy_out(used, n):
        for b in sorted({a // 3 for a in used}):
            cols = [(a % 3) * 130 for a in used if a // 3 == b]
            c0, c1 = min(cols), max(cols) + 130
            if b == 1:
                S.op(ACT, lambda b=b, c0=c0, c1=c1: A.copy(out=accs[0:n, b, c0:c1], in_=Ab[b][0:n, c0:c1]),
                     reads=[_bankR[b]], writes=[accsR[b]])
            else:
                S.op(DVE, lambda b=b, c0=c0, c1=c1: V.tensor_copy(out=accs[0:n, b, c0:c1], in_=Ab[b][0:n, c0:c1]),
                     reads=[_bankR[b]], writes=[accsR[b]])

    R = lambda n: Res(n)
    hTR = [R(f"hT{b}") for b in range(NB)]
    cqTR = [R(f"cqT{b}") for b in range(NB)]
    ckvTR = [R(f"ckvT{b}") for b in range(NBK + 1)]
    krTR = [R(f"krT{b}") for b in range(NBK + 1)]
    QTR = [[R(f"QT{i}_{g}") for g in range(NG)] for i in range(2)]
    KTR = [[R(f"KT{i}_{g}") for g in range(NBK // 4 if NBK >= 4 else 1)] for i in range(2)]
    KTropeR = [R(f"KTrope{i}") for i in range(2)]
    VdR = [[R(f"Vd{i}_{g}") for g in range(max(1, NBK // 4))] for i in range(2)]
    VmR = [[R(f"Vm{i}_{g}") for g in range(max(1, NBK // 4))] for i in range(2)]
    PTR = [R(f"PT{i}") for i in range(NPT)]
    WBR = [R(f"wbuf{i}") for i in range(NWB)]
    xresR = [R(f"xres{i}") for i in range(4)]
    doTR = [[R(f"doT{h}_{g}") for g in range(NG)] for h in range(8)]
    moTR = [[R(f"moT{h}_{g}") for g in range(NG)] for h in range(8)]
    hbR = R("hb"); ztR = R("zt"); smallR = R("smallb"); krpadR = R("krpad")
    ostR = [R("ost0"), R("ost1")]; kvstR = ostR; gtmpR = R("gtmp"); emR = R("emask"); krfR = [R("krf0"), R("krf1")]
    statR = [R(f"stat{i}") for i in range(16)]
    cmbR = [dict(t1=R("t1"), o=R("o"), ob=R("ob"), rr=R("rr")) for i in range(4)]
    ropeR = R("ropet"); ropeTR = R("ropetab"); ropeT2R = [R("ropetab0"), R("ropetab1")]; sigR = [R("sig0"), R("sig1")]
    mTR = R("mT"); uTR = [R(f"uT{f}") for f in range(32)]
    constR = R("const")
    wtok2R = R("wtok2")
    ARES = (cqTR + ckvTR + krTR + [r for l in QTR for r in l] + [r for l in KTR for r in l] + KTropeR
            + [r for l in VdR for r in l] + [r for l in VmR for r in l] + PTR)
    kstgR2 = [R("kstg0"), hbR]
    kstgi = [0]

    def next_kstg():
        i = kstgi[0]
        kstgi[0] = 1 - i
        return kstg2[i], kstgR2[i]

    wbi = [0]

    def wbuf():
        i = wbi[0]
        wbi[0] = (i + 1) % NWB
        return WBUF[i], WBR[i]

    sti = [0]

    def statcol():
        i = sti[0]
        sti[0] = (i + 1) % 16
        return (lambda n, c, i=i: stat[0:n, i * 4 + c:i * 4 + c + 1]), statR[i]

    V = nc.vector; A = nc.scalar; G = nc.gpsimd; TE = nc.tensor

    S.dma(POOL, ident[:], ident_d, writes=[constR], add=True)
    S.dma(POOL, emask, emask_d, writes=[emR])
    bg = []

    def cast_weights(names, defer=False):
        for name in names:
            src = wsrc[name]
            rows = src.shape[0]
            step = 128 if src.shape[1] > 2048 else 256
            for r0 in range(0, rows, step):
                fn = (lambda name=name, src=src, r0=r0, step=step:
                      S.dma(POOL, wb[name][r0:r0 + step, :], src[r0:r0 + step, :], writes=[wbR[name]], add=True))
                if defer:
                    bg.append(fn)
                else:
                    fn()

    def cast_caches(defer=False):
        for k_, src in cache_src.items():
            for r0 in range(0, PAST, 512):
                r1 = min(PAST, r0 + 512)
                fn = (lambda k_=k_, src=src, r0=r0, r1=r1:
                      S.dma(POOL, cb[k_][r0:r1, :], src[r0:r1, :], writes=[cacheR[k_]], add=True))
                if defer:
                    bg.append(fn)
                else:
                    fn()

    def bg_flush():
        while bg:
            bg.pop(0)()
    cast_weights(["w_in"])
    if not EARLY:
        cast_weights(["w_uqA", "w_uqB", "w_uk", "w_uv", "w_od", "w_om", "w_out", "w_up", "w_dn"])
        cast_caches()
    def bload(dst, src_row, n):
        S.dma(SP, dst, src_row.broadcast(0, 128) if hasattr(src_row, "broadcast") else src_row, writes=[constR], add=True)

    def bc(src, n):
        return bass.AP(src.tensor, src.offset, [[0, 128], [1, n]])

    for dst, src, n in ((gsub_b, g_sub, 128),
                        (gq_b, g_q, 256), (gkv_b, g_kv, 256), (lam_b, lamv, 256), (tb, relb, 256)):
        S.dma(SP, dst[:], src.broadcast_to([128, n]), writes=[constR], add=True)
    S.dma(SP, cosk[:], cosk_p_d, writes=[constR], add=True)
    S.dma(SP, sink[:], sink_p_d, writes=[constR], add=True)
    S.dma(SP, cosks[0:DEC, :], cosk_s_d, writes=[constR], add=True)
    S.dma(SP, sinks[0:DEC, :], sink_s_d, writes=[constR], add=True)
    c2R = R("const2")
    S.op(POOL, lambda: G.memset(cst[:, 0:1], EPS), writes=[c2R], add=True)
    S.op(POOL, lambda: G.memset(cst[:, 1:2], 0.0), writes=[c2R], add=True)
    S.op(POOL, lambda: G.memset(krpad[:], 0.0), writes=[krpadR])
    S.op(POOL, lambda: G.memset(M0[:], 0.0), writes=[c2R], add=True)
    S.op(POOL, lambda: G.memset(M0[64:128, 0:64], NEG), reads=[c2R], writes=[c2R], add=True)
    S.op(DVE, lambda: V.tensor_scalar(out=gsub_b[:], in0=gsub_b[:], scalar1=1.0 - LAMBDA_INIT, scalar2=None, op0=ALU.mult),
         reads=[constR], writes=[c2R], add=True)
    S.op(DVE, lambda: V.scalar_tensor_tensor(out=junk[:, 0:64], in0=lam_b[:, 0:64], scalar=1.0, in1=lam_b[:, 64:128],
                                             op0=ALU.mult, op1=ALU.mult, accum_out=cst[:, 3:4]),
         reads=[constR], writes=JK + [c2R], add=True)
    S.op(DVE, lambda: V.scalar_tensor_tensor(out=junk[:, 64:128], in0=lam_b[:, 128:192], scalar=1.0, in1=lam_b[:, 192:256],
                                             op0=ALU.mult, op1=ALU.mult, accum_out=cst[:, 4:5]),
         reads=[constR], writes=JK + [c2R], add=True)
    c3R = R("const3")
    S.op(ACT, lambda: A.activation(out=cst[:, 5:7], in_=cst[:, 3:5], func=AF.Exp), reads=[c2R], writes=[c3R])
    c4R = R("const4")
    S.op(DVE, lambda: V.scalar_tensor_tensor(out=cst[:, 7:8], in0=cst[:, 6:7], scalar=-LAMBDA_INIT, in1=cst[:, 5:6],
                                             op0=ALU.add, op1=ALU.subtract), reads=[c3R], writes=[c4R])
    NLAM = lambda n: cst[0:n, 7:8]
    bR = [R(f"bias{h}") for h in range(16)]
    bhR = [R("bh0"), R("bh1")]; blR = [R("bl0"), R("bl1")]

    def build_bias():
        fence(xresR[0:2], bR)
        for b in range(32):
            for h in range(8):
                sc_ap = tb[:, b * 8 + h:b * 8 + h + 1]
                if b == 0:
                    S.op(DVE, lambda h=h, sc_ap=sc_ap: V.tensor_scalar(out=B0[:, h, :], in0=emask[:, 0, :], scalar1=sc_ap,
                                                                       scalar2=None, op0=ALU.mult), reads=[constR, emR], writes=[bR[h]])
                else:
                    S.op(DVE, lambda h=h, b=b, sc_ap=sc_ap: V.scalar_tensor_tensor(
                        out=B0[:, h, :], in0=emask[:, b, :], scalar=sc_ap, in1=B0[:, h, :], op0=ALU.mult, op1=ALU.add),
                        reads=[constR, emR, bR[h]], writes=[bR[h]])
        for b in range(1, 16):
            for h in range(8):
                sc_ap = tb[:, b * 8 + h:b * 8 + h + 1]
                if b == 1:
                    S.op(DVE, lambda h=h, sc_ap=sc_ap: V.tensor_scalar(out=B1[:, h, :], in0=emask[:, 32, :], scalar1=sc_ap,
                                                                       scalar2=None, op0=ALU.mult), reads=[constR, emR], writes=[bR[8 + h]])
                else:
                    S.op(DVE, lambda h=h, b=b, sc_ap=sc_ap: V.scalar_tensor_tensor(
                        out=B1[:, h, :], in0=emask[:, 31 + b, :], scalar=sc_ap, in1=B1[:, h, :], op0=ALU.mult, op1=ALU.add),
                        reads=[constR, emR, bR[8 + h]], writes=[bR[8 + h]])
        for h in range(8):
            c_ap = tb[:, 15 * 8 + h:15 * 8 + h + 1]
            S.op(DVE, lambda h=h, c_ap=c_ap: V.tensor_scalar(out=B0[:, h, :], in0=B0[:, h, :], scalar1=c_ap, scalar2=None,
                                                             op0=ALU.subtract), reads=[bR[h]], writes=[bR[h]])
            S.op(DVE, lambda h=h, c_ap=c_ap: V.tensor_scalar(out=B1[:, h, :], in0=B1[:, h, :], scalar1=c_ap, scalar2=None,
                                                             op0=ALU.subtract), reads=[bR[8 + h]], writes=[bR[8 + h]])
        for h in range(8):
            S.op(DVE, lambda h=h: V.memset(B0[64:128, h, 0:64], NEG), reads=[bR[h]], writes=[bR[h]])
        for i_, Bf in enumerate((B0, B1)):
            rs = bR[8 * i_:8 * i_ + 8]
            S.op(DVE, lambda i_=i_, Bf=Bf: V.tensor_copy(out=Bh[i_][:, :, :], in_=Bf), reads=rs, writes=[bhR[i_]])
            S.op(DVE, lambda i_=i_, Bf=Bf: V.tensor_tensor(out=Bf, in0=Bf, in1=Bh[i_][:, :, :], op=ALU.subtract),
                 reads=rs + [bhR[i_]], writes=rs)
            S.op(DVE, lambda i_=i_, Bf=Bf: V.tensor_copy(out=Bl[i_][:, :, :], in_=Bf), reads=rs, writes=[blR[i_]])
        fence(bR, xresR[0:2])
        if EARLY:
            fence([emR], [r for l in doTR for r in l])
    if not EARLY:
        build_bias()
    allconst = [constR, c2R, c3R, c4R]
    if not EARLY:
        fence([emR], ARES)
    for t_ in Vd[0:1]:
        S.op(POOL, lambda t_=t_: G.memset(t_, 1.0), writes=VdR[0])
    for i_, t_ in enumerate(Vm):
        S.op(POOL, lambda t_=t_: G.memset(t_, 1.0), writes=VmR[i_])

    def rstd_from_sumsq(n, ss_ap, out_ap, inv_n, sres):
        S.op(ACT, lambda: A.activation(out=out_ap, in_=ss_ap, func=AF.Ln, scale=inv_n, bias=cst[0:n, 0:1]),
             reads=[sres, c2R], writes=[sres])
        S.op(ACT, lambda: A.activation(out=out_ap, in_=out_ap, func=AF.Exp, scale=-0.5), reads=[sres], writes=[sres])

    def wload3(name, c0, ncols, kch, dst_ap, dres, add=False):
        src = wb[name].rearrange("(c p) n -> p c n", p=128)[:, 0:kch, c0:c0 + ncols]
        S.dma(SP, dst_ap, src, reads=[wbR[name]], writes=[dres], add=add)

    def transposes_to(dst_fn, src_tile, n, nchunks, width, src_res, dst_res, evac_eng):
        for c in range(nchunks):
            S.op(PE, lambda c=c: TE.transpose(TR[0:width, c * 128:c * 128 + n], src_tile[0:n, c * width:(c + 1) * width],
                                              ident[0:n, 0:n]), reads=[src_res, constR], writes=[TRR], add=(c > 0))
        src_ap = TR[0:width, 0:nchunks * 128].rearrange("p (c t) -> p c t", t=128)[:, :, 0:n]
        if evac_eng is ACT:
            S.op(ACT, lambda: A.copy(out=dst_fn(), in_=src_ap), reads=[TRR], writes=dst_res)
        else:
            S.op(DVE, lambda: V.tensor_copy(out=dst_fn(), in_=src_ap), reads=[TRR], writes=dst_res)

    pbi = [0]

    def pbank():
        i = pbi[0]
        pbi[0] = (i + 1) % 4
        return Pb[i], PbR[i]

    osti = [0]

    def ost():
        i = osti[0]
        osti[0] = 1 - i
        return ostage[i], ostR[i]

    def phase1(x_src, n, blk, hcol, wtok, wtokR, ckv_dst, kr_dst, cosk_ap, sink_ap, kcol):
        xs_i = blk % 4
        xt = xres[0:n, xs_i, :]
        S.dma(SP, xt, x_src, writes=[xresR[xs_i]])
        sc, sres = statcol()
        S.op(DVE, lambda: V.scalar_tensor_tensor(out=junk[0:n, :], in0=xt, scalar=1.0, in1=xt, op0=ALU.mult, op1=ALU.mult,
                                                 accum_out=sc(n, 0)), reads=[xresR[xs_i]], writes=JK + [sres])
        rstd_from_sumsq(n, sc(n, 0), sc(n, 1), 1.0 / D, sres)
        S.op(DVE, lambda: V.scalar_tensor_tensor(out=hb[0:n, :], in0=xt, scalar=sc(n, 1), in1=gtmp[0:n, :],
                                                 op0=ALU.mult, op1=ALU.mult), reads=[xresR[xs_i], sres, gtmpR], writes=[hbR])
        hres = hTR[hcol // 128]
        transposes_to(lambda: hT[:, :, hcol:hcol + n], hb, n, 8, 128, hbR, [hres], ACT)
        pa, paR = Pb[0], PbR[0]
        pb_, pbR_ = Pb[3], PbR[3]
        for c in range(8):
            S.op(PE, lambda c=c: TE.matmul(pa[0:n, 0:512], hT[:, c, hcol:hcol + n], wtok[:, c, 0:512], start=(c == 0), stop=(c == 7)),
                 reads=[hres, wtokR], writes=[paR])
        for c in range(8):
            S.op(PE, lambda c=c: TE.matmul(pb_[0:n, 0:64], hT[:, c, hcol:hcol + n], wtok2[:, c, 0:64], start=(c == 0), stop=(c == 7)),
                 reads=[hres, wtok2R], writes=[pbR_])
        S.op(ACT, lambda: A.copy(out=zt[0:n, 0:512], in_=pa[0:n, 0:512]), reads=[paR], writes=[ztR])
        S.op(ACT, lambda: A.copy(out=zt[0:n, 512:576], in_=pb_[0:n, 0:64]), reads=[pbR_], writes=[ztR], add=True)
        sc2, sres2 = statcol()
        S.op(DVE, lambda: V.scalar_tensor_tensor(out=junk[0:n, 0:256], in0=zt[0:n, 0:256], scalar=1.0, in1=zt[0:n, 0:256],
                                                 op0=ALU.mult, op1=ALU.mult, accum_out=sc2(n, 0)), reads=[ztR], writes=JK + [sres2])
        S.op(DVE, lambda: V.scalar_tensor_tensor(out=junk[0:n, 256:512], in0=zt[0:n, 256:512], scalar=1.0, in1=zt[0:n, 256:512],
                                                 op0=ALU.mult, op1=ALU.mult, accum_out=sc2(n, 2)), reads=[ztR], writes=JK + [sres2], add=True)
        rstd_from_sumsq(n, sc2(n, 0), sc2(n, 1), 1.0 / 256, sres2)
        rstd_from_sumsq(n, sc2(n, 2), sc2(n, 3), 1.0 / 256, sres2)
        S.op(DVE, lambda: V.scalar_tensor_tensor(out=smallb[0:n, 0:256], in0=zt[0:n, 0:256], scalar=sc2(n, 1), in1=gq_b[0:n, :],
                                                 op0=ALU.mult, op1=ALU.mult), reads=[ztR, sres2, constR], writes=[smallR])
        ot, otR = ost()
        S.op(DVE, lambda: V.scalar_tensor_tensor(out=ot[0:n, 0:256], in0=zt[0:n, 256:512], scalar=sc2(n, 3), in1=gkv_b[0:n, :],
                                                 op0=ALU.mult, op1=ALU.mult), reads=[ztR, sres2, constR], writes=[otR])
        S.dma(POOL, ckv_dst, ot[0:n, 0:256], reads=[otR])
        S.op(ACT, lambda: A.copy(out=smallb[0:n, 256:512], in_=ot[0:n, 0:256]), reads=[otR], writes=[smallR], add=True)
        cres = cqTR[hcol // 128]
        kres = ckvTR[kcol // 128]
        for c in range(4):
            S.op(PE, lambda c=c: TE.transpose(TR[:, c * 128:c * 128 + n], smallb[0:n, c * 128:(c + 1) * 128], ident[0:n, 0:n]),
                 reads=[smallR, constR], writes=[TRR], add=(c > 0))
        trv = TR[:, 0:512].rearrange("p (c t) -> p c t", t=128)
        S.op(DVE, lambda: V.tensor_copy(out=cqT[:, :, hcol:hcol + n], in_=trv[:, 0:2, 0:n]), reads=[TRR], writes=[cres])
        S.op(DVE, lambda: V.tensor_copy(out=ckvT[:, :, kcol:kcol + n], in_=trv[:, 2:4, 0:n]), reads=[TRR], writes=[kres])
        kf, kfR = krf[blk % 2], krfR[blk % 2]
        S.op(DVE, lambda: V.tensor_tensor(out=junk[0:n, 512:544], in0=zt[0:n, 544:576], in1=sink_ap, op=ALU.mult),
             reads=[ztR, constR], writes=JK)
        S.op(DVE, lambda: V.tensor_tensor(out=kf[0:n, :], in0=zt[0:n, 512:544], in1=cosk_ap, op=ALU.mult),
             reads=[ztR, constR], writes=[kfR])
        S.op(DVE, lambda: V.tensor_tensor(out=kf[0:n, :], in0=kf[0:n, :], in1=junk[0:n, 512:544], op=ALU.add),
             reads=[kfR] + JK, writes=[kfR])
        S.dma(POOL, kr_dst, kf[0:n, :], reads=[kfR])
        S.op(ACT, lambda: A.copy(out=krpad[0:n, 64:96], in_=kf[0:n, :]), reads=[kfR], writes=[krpadR])
        S.op(PE, lambda: TE.transpose(TR[0:96, 0:n], krpad[0:n, 0:96], ident[0:n, 0:n]), reads=[krpadR, constR], writes=[TRR])
        S.op(DVE, lambda: V.tensor_copy(out=krT[64:96, kcol:kcol + n], in_=TR[64:96, 0:n]), reads=[TRR], writes=[krTR[kcol // 128]])

    chunk_ctr = [0]

    def bias_mm(dst, nk, nq, kind, rel, head, sbR):
        if kind == 'diff':
            tiles = [(Bh[rel][0:nk, head, 0:nq], bhR[rel]), (Bl[rel][0:nk, head, 0:nq], blR[rel])]
        else:
            tiles = [(M0[0:nk, 0:nq], c2R)]
        for (bt, br) in tiles:
            S.op(PE, lambda bt=bt: TE.matmul(dst, ident[0:nk, 0:nk], bt, start=False, stop=False, skip_group_check=True),
                 reads=[br, constR], writes=[sbR], add=True)

    def attn_group(units, qbs, kbs, is_prompt, kind, head, first_seg=True, last_seg=True, defer=False):
        chunks = []
        for (kbi, kcol, nk, vblk) in kbs:
            vis = [qb for qb in qbs if (not is_prompt) or qb[0] >= kbi]
            if not vis:
                continue
            for u in units:
                chunks.append((u, kbi, kcol, nk, vblk, vis))
        started = set()
        base = chunk_ctr[0]
        chunk_ctr[0] += len(chunks)
        last_kb = kbs[-1][0]
        first_kb = kbs[0][0]

        def emit_qk(ci):
            u, kbi, kcol, nk, vblk, vis = chunks[ci]
            sbk, sbR = Sb[(base + ci) % 3], SbR[(base + ci) % 3]
            q0 = vis[0][1]
            ncols = vis[-1][1] + vis[-1][2] - q0
            lo, hi = u['qrows']
            S.op(PE, lambda: TE.matmul(sbk[0:nk, 0:ncols], u['KT'][lo:hi, kcol:kcol + nk], u['QT'][lo:hi, q0:q0 + ncols],
                                       start=True, stop=False, skip_group_check=True),
                 reads=u['kres'](kbi) + u['qres'](vis), writes=[sbR])
            for (qbi, qcol, nq, slot) in vis:
                rel = qbi - kbi
                o_ = qcol - q0
                if kind == 'diff' and rel in (0, 1):
                    bias_mm(sbk[0:nk, o_:o_ + nq], nk, nq, 'diff', rel, head, sbR)
                elif kind == 'mla' and rel == 0 and is_prompt:
                    bias_mm(sbk[0:nk, o_:o_ + nq], nk, nq, 'mla', 0, 0, sbR)
            pt, ptR = PT[(base + ci) % NPT], PTR[(base + ci) % NPT]
            bias_ap = tb[0:nk, 15 * 8 + head:15 * 8 + head + 1] if kind == 'diff' else cst[0:nk, 1:2]
            S.op(ACT, lambda: A.activation(out=pt[0:nk, 0:ncols], in_=sbk[0:nk, 0:ncols], func=AF.Exp, bias=bias_ap, scale=1.0),
                 reads=[sbR, constR, c2R], writes=[ptR])

        def emit_av(ci):
            u, kbi, kcol, nk, vblk, vis = chunks[ci]
            pt, ptR = PT[(base + ci) % NPT], PTR[(base + ci) % NPT]
            q0 = vis[0][1]
            vc = u['vc']
            for (qbi, qcol, nq, slot) in vis:
                a = u['accbase'] + slot
                o_ = qcol - q0
                st_ = first_seg and (kbi == first_kb) and ((a // 3) not in started)
                if first_seg and (kbi == first_kb):
                    started.add(a // 3)
                sp_ = False
                S.op(PE, lambda a=a, o_=o_, nq=nq, st_=st_, sp_=sp_: TE.matmul(
                    acc_ap(a, nq, vc), pt[0:nk, o_:o_ + nq], u['V'][0:nk, vblk, 0:vc], start=st_, stop=sp_,
                    skip_group_check=True),
                    reads=[ptR] + u['vres'](kbi), writes=[accR[a]])

        n = len(chunks)
        LA = 2
        steps = []
        for ci in range(n + LA):
            def step(ci=ci):
                if ci < n:
                    emit_qk(ci)
                if 0 <= ci - LA < n:
                    emit_av(ci - LA)
            steps.append(step)
        if defer:
            return steps
        for st in steps:
            st()

    def attn_group_sample(units, qb, kbs, kind, head, first_seg, last_seg):
        (qbi, qcol, nq, slot) = qb
        supers = []
        cur = []
        for kb in kbs:
            if cur and (kb[2] != cur[0][2] or len(cur) >= 16):
                supers.append(cur)
                cur = []
            cur.append(kb)
        if cur:
            supers.append(cur)
        chunks = [(u, sup) for sup in supers for u in units]
        base = chunk_ctr[0]
        chunk_ctr[0] += len(chunks)
        started = set()
        first_kb = kbs[0][0]

        def emit_qk(ci):
            u, sup = chunks[ci]
            sbk, sbR = Sb[(base + ci) % 3], SbR[(base + ci) % 3]
            nk = sup[0][2]
            lo, hi = u['qrows']
            for idx, (kbi, kcol, nk_, vblk) in enumerate(sup):
                S.op(PE, lambda idx=idx, kcol=kcol: TE.matmul(sbk[0:nk, idx * nq:(idx + 1) * nq], u['KT'][lo:hi, kcol:kcol + nk],
                                                              u['QT'][lo:hi, qcol:qcol + nq], start=True, stop=False,
                                                              skip_group_check=True),
                     reads=u['kres'](kbi) + u['qres']([qb]), writes=[sbR], add=(idx > 0))
                if kind == 'diff' and (qbi - kbi) in (0, 1):
                    bias_mm(sbk[0:nk, idx * nq:(idx + 1) * nq], nk, nq, 'diff', qbi - kbi, head, sbR)
            pt, ptR = PT[(base + ci) % NPT], PTR[(base + ci) % NPT]
            ncols = len(sup) * nq
            bias_ap = tb[0:nk, 15 * 8 + head:15 * 8 + head + 1] if kind == 'diff' else cst[0:nk, 1:2]
            S.op(ACT, lambda: A.activation(out=pt[0:nk, 0:ncols], in_=sbk[0:nk, 0:ncols], func=AF.Exp, bias=bias_ap, scale=1.0),
                 reads=[sbR, constR, c2R], writes=[ptR])

        def emit_av(ci):
            u, sup = chunks[ci]
            pt, ptR = PT[(base + ci) % NPT], PTR[(base + ci) % NPT]
            nk = sup[0][2]
            vc = u['vc']
            a = u['accbase'] + slot
            for idx, (kbi, kcol, nk_, vblk) in enumerate(sup):
                st_ = first_seg and (kbi == first_kb) and ((a // 3) not in started)
                if first_seg and (kbi == first_kb):
                    started.add(a // 3)
                S.op(PE, lambda idx=idx, vblk=vblk, st_=st_: TE.matmul(
                    acc_ap(a, nq, vc), pt[0:nk, idx * nq:(idx + 1) * nq], u['V'][0:nk, vblk, 0:vc], start=st_, stop=False,
                    skip_group_check=True), reads=[ptR] + u['vres'](kbi), writes=[accR[a]])

        n = len(chunks)
        LA = 2
        for ci in range(n + LA):
            if ci < n:
                emit_qk(ci)
            if 0 <= ci - LA < n:
                emit_av(ci - LA)

    def diff_combine(head, qbs, dst_fn, dst_res_fn):
        acc_copy_out([sl for q_ in qbs for sl in (q_[3], 4 + q_[3])], qbs[0][2])
        for (qbi, qcol, nq, slot) in qbs:
            c = cmb[slot]; cr = cmbR[slot]
            a1, a2 = acc_sb(slot, nq, 130), acc_sb(4 + slot, nq, 130)
            rr = c['rr']
            S.op(DVE, lambda: V.reciprocal(out=rr[0:nq, 0:1], in_=a1[:, 128:129]), reads=[accsR[slot // 3]], writes=[cr['rr']])
            S.op(DVE, lambda: V.reciprocal(out=rr[0:nq, 1:2], in_=a2[:, 128:129]), reads=[accsR[(4 + slot) // 3]], writes=[cr['rr']], add=True)
        for (qbi, qcol, nq, slot) in qbs:
            c = cmb[slot]; cr = cmbR[slot]; rr = c['rr']
            S.op(DVE, lambda: V.tensor_scalar(out=rr[0:nq, 2:3], in0=rr[0:nq, 1:2], scalar1=NLAM(nq), scalar2=None, op0=ALU.mult),
                 reads=[cr['rr'], c4R], writes=[cr['rr']])
            a1 = acc_sb(slot, nq, 130)
            S.op(DVE, lambda: V.tensor_scalar(out=c['t1'][0:nq, :], in0=a1[:, 0:128], scalar1=rr[0:nq, 0:1], scalar2=None, op0=ALU.mult),
                 reads=[accsR[slot // 3], cr['rr']], writes=[cr['t1']])
        for (qbi, qcol, nq, slot) in qbs:
            c = cmb[slot]; cr = cmbR[slot]; rr = c['rr']
            a2 = acc_sb(4 + slot, nq, 130)
            S.op(DVE, lambda: V.scalar_tensor_tensor(out=c['o'][0:nq, :], in0=a2[:, 0:128], scalar=rr[0:nq, 2:3], in1=c['t1'][0:nq, :],
                                                     op0=ALU.mult, op1=ALU.add), reads=[accsR[(4 + slot) // 3], cr['rr'], cr['t1']], writes=[cr['o']])
        for (qbi, qcol, nq, slot) in qbs:
            c = cmb[slot]; cr = cmbR[slot]; rr = c['rr']
            S.op(DVE, lambda: V.scalar_tensor_tensor(out=c['t1'][0:nq, :], in0=c['o'][0:nq, :], scalar=1.0, in1=c['o'][0:nq, :],
                                                     op0=ALU.mult, op1=ALU.mult, accum_out=rr[0:nq, 3:4]),
                 reads=[cr['o']], writes=[cr['t1'], cr['rr']])
        for (qbi, qcol, nq, slot) in qbs:
            c = cmb[slot]; cr = cmbR[slot]; rr = c['rr']
            rstd_from_sumsq(nq, rr[0:nq, 3:4], rr[0:nq, 4:5], 1.0 / 128, cr['rr'])
        for (qbi, qcol, nq, slot) in qbs:
            c = cmb[slot]; cr = cmbR[slot]; rr = c['rr']
            S.op(DVE, lambda: V.scalar_tensor_tensor(out=c['ob'][0:nq, :], in0=c['o'][0:nq, :], scalar=rr[0:nq, 4:5], in1=gsub_b[0:nq, :],
                                                     op0=ALU.mult, op1=ALU.mult), reads=[cr['o'], cr['rr'], c2R], writes=[cr['ob']])
        for (qbi, qcol, nq, slot) in qbs:
            c = cmb[slot]; cr = cmbR[slot]
            S.op(PE, lambda: TE.transpose(TR[:, slot * 128:slot * 128 + nq], c['ob'][0:nq, :], ident[0:nq, 0:nq]),
                 reads=[cr['ob'], constR], writes=[TRR], add=(slot != qbs[0][3]))
        q0 = qbs[0][1]
        ncols = qbs[-1][1] + qbs[-1][2] - q0
        s0 = qbs[0][3]
        S.op(ACT, lambda: A.copy(out=dst_fn(q0, ncols), in_=TR[:, s0 * 128:s0 * 128 + ncols]), reads=[TRR], writes=dst_res_fn())

    def mla_combine(pair, qbs, dst_fn, dst_res_fn):
        acc_copy_out([sl for q_ in qbs for sl in (q_[3], 4 + q_[3])], qbs[0][2])
        for u in range(2):
            for (qbi, qcol, nq, slot) in qbs:
                c = cmb[slot]; cr = cmbR[slot]; rr = c['rr']
                a = acc_sb(4 * u + slot, nq, 66)
                S.op(DVE, lambda: V.reciprocal(out=rr[0:nq, u:u + 1], in_=a[:, 64:65]), reads=[accsR[(4 * u + slot) // 3]], writes=[cr['rr']],
                     add=(u > 0))
        for u in range(2):
            for (qbi, qcol, nq, slot) in qbs:
                c = cmb[slot]; cr = cmbR[slot]; rr = c['rr']
                a = acc_sb(4 * u + slot, nq, 66)
                S.op(DVE, lambda: V.tensor_scalar(out=c['ob'][0:nq, u * 64:(u + 1) * 64], in0=a[:, 0:64], scalar1=rr[0:nq, u:u + 1],
                                                  scalar2=None, op0=ALU.mult), reads=[accsR[(4 * u + slot) // 3], cr['rr']], writes=[cr['ob']],
                     add=(u > 0))
        for (qbi, qcol, nq, slot) in qbs:
            c = cmb[slot]; cr = cmbR[slot]
            S.op(PE, lambda: TE.transpose(TR[:, slot * 128:slot * 128 + nq], c['ob'][0:nq, :], ident[0:nq, 0:nq]),
                 reads=[cr['ob'], constR], writes=[TRR], add=(slot != qbs[0][3]))
        q0 = qbs[0][1]
        ncols = qbs[-1][1] + qbs[-1][2] - q0
        s0 = qbs[0][3]
        S.op(ACT, lambda: A.copy(out=dst_fn(q0, ncols), in_=TR[:, s0 * 128:s0 * 128 + ncols]), reads=[TRR], writes=dst_res_fn())

    def proj_fm(dstT, rows, wt, wcol, M, srcT, kch, tq, src_res_fn, wres, dst_res_fn, evac):
        for c0 in range(0, tq, 512):
            n = min(512, tq - c0)
            pk, pkR = pbank()
            for c in range(kch):
                S.op(PE, lambda c=c: TE.matmul(pk[0:M, 0:n], wt[:, c, wcol:wcol + M], srcT[:, c, c0:c0 + n],
                                               start=(c == 0), stop=(c == kch - 1)),
                     reads=src_res_fn(c0, n) + [wres], writes=[pkR])
            evac(pk, pkR, c0, n)

    STOP = int(os.environ.get("DEV_STOP", "99"))

    sbi = [0]

    def sbank():
        i = sbi[0]
        sbi[0] = 1 - i
        return (Pb[0], PbR[0]) if i == 0 else (P1t, TRR)

    def interleave(steps, pieces):
        n, m = len(steps), len(pieces)
        k = 0
        for i, st in enumerate(steps):
            st()
            if bg and i % 8 == 7:
                bg.pop(0)()
            tgt = ((i + 1) * m) // n if n else m
            while k < tgt:
                pieces[k]()
                k += 1
        while k < m:
            pieces[k]()
            k += 1

    def prompt_attention(s):
        tq = T
        qblocks = [(b, b * 128, 128) for b in range(NB)]
        groups = [[(4 * g + i, (4 * g + i) * 128, 128, i) for i in range(4)] for g in range(NG)]
        kbs_all = [(b, b * 128, 128, b) for b in range(NB)]
        hsrc = lambda c0, n: [hTR[b] for b in range(c0 // 128, (c0 + n + 127) // 128)]
        cqsrc = lambda c0, n: [cqTR[b] for b in range(c0 // 128, (c0 + n + 127) // 128)]

        fence(VmR[0] + VmR[1], VdR[1])
        S.op(POOL, lambda: G.memset(Vd[1][:, :, 128:130], 1.0), writes=VdR[1])

        def diff_setup(h):
            qi = h % 2
            qt, ktile, vt = QT[qi], KT[qi], Vd[qi]
            box = {}
            pieces = []

            def p_load():
                hw, hwR = wbuf()
                hwv = hw[:, 0:3072].rearrange("p (c n) -> p c n", n=384)
                wload3("w_in", COL_Q + h * 128, 128, 8, hwv[:, :, 0:128], hwR)
                wload3("w_in", COL_K + h * 128, 128, 8, hwv[:, :, 128:256], hwR, add=True)
                wload3("w_in", COL_V + h * 128, 128, 8, hwv[:, :, 256:384], hwR, add=True)
                box['w'] = (hwv, hwR)
            pieces.append(p_load)

            def p_proj(c0, which):
                hwv, hwR = box['w']
                pk, pkR = sbank()
                wc = 0 if which == 'q' else 128
                for c in range(8):
                    S.op(PE, lambda c=c: TE.matmul(pk[:, 0:512], hwv[:, c, wc:wc + 128], hT[:, c, c0:c0 + 512],
                                                   start=(c == 0), stop=(c == 7)), reads=hsrc(c0, 512) + [hwR], writes=[pkR])
                if which == 'q':
                    S.op(ACT, lambda: A.activation(out=qt[:, c0:c0 + 512], in_=pk[:, 0:512], func=AF.Copy, scale=0.125),
                         reads=[pkR], writes=[QTR[qi][c0 // 512]])
                else:
                    S.op(DVE, lambda: V.tensor_copy(out=ktile[:, c0:c0 + 512], in_=pk[:, 0:512]),
                         reads=[pkR], writes=[KTR[qi][c0 // 512], KTropeR[qi]])

            def p_kv(bi):
                hwv, hwR = box['w']
                qcol = bi * 128
                pk, pkR = sbank()
                for c in range(8):
                    S.op(PE, lambda c=c: TE.matmul(pk[:, 0:256], hT[:, c, qcol:qcol + 128], hwv[:, c, 128:384],
                                                   start=(c == 0), stop=(c == 7)), reads=[hTR[bi], hwR], writes=[pkR])
                sg = (bi // 4) % 2
                S.op(ACT, lambda: A.copy(out=kvst[sg][:, bi % 4, :], in_=pk[:, 0:256]), reads=[pkR], writes=[kvstR[sg]],
                     add=(bi % 4 != 0))
                S.op(POOL, lambda: G.tensor_copy(out=vt[:, bi, 0:128], in_=kvst[sg][:, bi % 4, 128:256]),
                     reads=[kvstR[sg]], writes=[VdR[qi][bi // 4]], add=(bi % 4 != 0))
                if bi % 4 == 3:
                    r0 = s * T + (bi - 3) * 128
                    kdst = kd_p[r0:r0 + 512, h * 128:(h + 1) * 128].rearrange("(b p) e -> p b e", p=128)
                    vdst = vd_p[r0:r0 + 512, h * 128:(h + 1) * 128].rearrange("(b p) e -> p b e", p=128)
                    S.dma(POOL, kdst, kvst[sg][:, :, 0:128], reads=[kvstR[sg]])
                    S.dma(POOL, vdst, kvst[sg][:, :, 128:256], reads=[kvstR[sg]])
            for c0 in range(0, T, 512):
                pieces.append(lambda c0=c0: p_proj(c0, 'q'))
                pieces.append(lambda c0=c0: p_proj(c0, 'k'))
                for bi in range(c0 // 128, c0 // 128 + 4):
                    pieces.append(lambda bi=bi: p_kv(bi))
            return pieces

        for pc in diff_setup(0):
            pc()
        if EARLY and s == 0:
            build_bias()
        for h in range(8):
            qi = h % 2
            units = []
            for m in range(2):
                units.append(dict(QT=QT[qi], qrows=(m * 64, (m + 1) * 64), qres=lambda vis, qi=qi: [QTR[qi][vis[0][1] // 512]],
                                  KT=KT[qi], kres=lambda kbi, qi=qi: [KTR[qi][kbi // 4]],
                                  V=Vd[qi], vres=lambda kbi, qi=qi: [VdR[qi][kbi // 4]], vc=130, accbase=4 * m))
            steps = []
            for grp in groups:
                gk = [k for k in kbs_all if k[0] <= grp[-1][0]]
                steps += attn_group(units, grp, gk, True, 'diff', h, defer=True)
                gi = grp[0][1] // 512
                steps.append(lambda grp=grp, gi=gi, h=h: diff_combine(
                    h, grp, lambda q0, ncols, h=h: doT[:, h, q0:q0 + ncols], lambda h=h, gi=gi: [doTR[h][gi]]))
            interleave(steps, diff_setup(h + 1) if h < 7 else [])

        if s == 0:
            bg_flush()
        fence(hTR, [r for g_ in moTR for r in g_])
        fence(VdR[1], VmR[0] + VmR[1])
        moT = hT
        for i_ in range(2):
            S.op(POOL, lambda i_=i_: G.memset(Vm[i_][:, :, 64:66], 1.0), writes=VmR[i_])
        for u in range(2):
            for c0 in range(0, T, 512):
                if (c0 // 512 + u) % 2 == 0:
                    S.op(ACT, lambda u=u, c0=c0: A.copy(out=KT[u][64:96, c0:c0 + 512], in_=krT[64:96, c0:c0 + 512]),
                         reads=[krTR[b] for b in range(c0 // 128, c0 // 128 + 4)], writes=[KTropeR[u], KTR[u][c0 // 512]], add=True)
                else:
                    S.op(DVE, lambda u=u, c0=c0: V.tensor_copy(out=KT[u][64:96, c0:c0 + 512], in_=krT[64:96, c0:c0 + 512]),
                         reads=[krTR[b] for b in range(c0 // 128, c0 // 128 + 4)], writes=[KTropeR[u], KTR[u][c0 // 512]], add=True)
        ropei = [0]

        def mla_setup(hh):
            u = hh % 2
            qt, ktile, vt = QT[u], KT[u], Vm[u]
            box = {}
            pieces = []

            def p_load():
                mw, mwR = wbuf()
                mwv = mw[:, 0:640].rearrange("p (c n) -> p c n", n=320)
                wload3("w_uqA", hh * 96, 96, 2, mwv[:, :, 0:96], mwR)
                wload3("w_uqB", hh * 96, 96, 2, mwv[:, :, 96:192], mwR, add=True)
                wload3("w_uk", hh * 64, 64, 2, mwv[:, :, 192:256], mwR, add=True)
                wload3("w_uv", hh * 64, 64, 2, mwv[:, :, 256:320], mwR, add=True)
                box['w'] = (mwv, mwR)
                p_tab(0)
            pieces.append(p_load)

            def p_tab(cc):
                ti = ropei[0] % 2
                ropei[0] += 1
                S.dma(SP, ropet[64:96, 1 + 2 * ti, :], cosq_p_d[:, cc:cc + 256], writes=[ropeT2R[ti]])
                S.dma(SP, ropet[64:96, 2 + 2 * ti, :], sinq_p_d[:, cc:cc + 256], writes=[ropeT2R[ti]], add=True)
                box[('tab', cc)] = ti

            def p_q(c0):
                mwv, mwR = box['w']
                pa, paR = Pb[0], PbR[0]
                pb_, pbR_ = P1t, TRR
                for c in range(2):
                    S.op(PE, lambda c=c: TE.matmul(pa[0:96, 0:512], mwv[:, c, 0:96], cqT[:, c, c0:c0 + 512],
                                                   start=(c == 0), stop=(c == 1)), reads=cqsrc(c0, 512) + [mwR], writes=[paR])
                for c in range(2):
                    S.op(PE, lambda c=c: TE.matmul(pb_[0:96, 0:512], mwv[:, c, 96:192], cqT[:, c, c0:c0 + 512],
                                                   start=(c == 0), stop=(c == 1)), reads=cqsrc(c0, 512) + [mwR], writes=[pbR_])
                S.op(ACT, lambda: A.activation(out=qt[0:64, c0:c0 + 512], in_=pa[0:64, 0:512], func=AF.Copy, scale=96 ** -0.5),
                     reads=[paR], writes=[QTR[u][c0 // 512]])
                for hf in range(2):
                    cc = c0 + hf * 256
                    o_ = hf * 256
                    ti = box[('tab', cc)]
                    if cc + 256 < T:
                        p_tab(cc + 256)
                    S.op(DVE, lambda o_=o_, ti=ti: V.tensor_tensor(out=ropet[64:96, 0, :], in0=pb_[64:96, o_:o_ + 256],
                                                                   in1=ropet[64:96, 2 + 2 * ti, :], op=ALU.mult),
                         reads=[pbR_, ropeT2R[ti]], writes=[ropeR])
                    S.op(DVE, lambda o_=o_, ti=ti: V.tensor_tensor(out=pa[64:96, o_:o_ + 256], in0=pa[64:96, o_:o_ + 256],
                                                                   in1=ropet[64:96, 1 + 2 * ti, :], op=ALU.mult),
                         reads=[paR, ropeT2R[ti]], writes=[paR])
                    S.op(DVE, lambda o_=o_, cc=cc: V.tensor_tensor(out=qt[64:96, cc:cc + 256], in0=ropet[64:96, 0, :],
                                                                   in1=pa[64:96, o_:o_ + 256], op=ALU.add),
                         reads=[ropeR, paR], writes=[QTR[u][c0 // 512]], add=True)

            def p_k(c0):
                mwv, mwR = box['w']
                pk, pkR = sbank()
                for c in range(2):
                    S.op(PE, lambda c=c: TE.matmul(pk[0:64, 0:512], mwv[:, c, 192:256], ckvT[:, c, c0:c0 + 512],
                                                   start=(c == 0), stop=(c == 1)),
                         reads=[ckvTR[b] for b in range(c0 // 128, c0 // 128 + 4)] + [mwR], writes=[pkR])
                S.op(DVE, lambda: V.tensor_copy(out=ktile[0:64, c0:c0 + 512], in_=pk[0:64, 0:512]), reads=[pkR],
                     writes=[KTR[u][c0 // 512]])

            def p_v(b8):
                mwv, mwR = box['w']
                nb8 = min(8, NB - b8)
                pk, pkR = sbank()
                for j in range(nb8):
                    kcol = (b8 + j) * 128
                    for c in range(2):
                        S.op(PE, lambda c=c, j=j, kcol=kcol: TE.matmul(
                            pk[:, j * 64:(j + 1) * 64], ckvT[:, c, kcol:kcol + 128], mwv[:, c, 256:320],
                            start=(c == 0), stop=(c == 1)), reads=[ckvTR[kcol // 128], mwR], writes=[pkR], add=(j + c > 0))
                S.op(ACT, lambda: A.copy(out=vt[:, b8:b8 + nb8, 0:64], in_=pk[:, 0:nb8 * 64].rearrange("p (b e) -> p b e", e=64)),
                     reads=[pkR], writes=[VmR[u][g_] for g_ in range(b8 // 4, (b8 + nb8 + 3) // 4)])
            for c0 in range(0, T, 512):
                pieces.append(lambda c0=c0: p_q(c0))
                pieces.append(lambda c0=c0: p_k(c0))
            for b8 in range(0, NB, 8):
                pieces.append(lambda b8=b8: p_v(b8))
            return pieces

        def mla_head_combine(hh, qbs):
            u = hh % 2
            p = hh // 2
            acc_copy_out([4 * u + q_[3] for q_ in qbs], 128)
            for (qbi, qcol, nq, slot) in qbs:
                c = cmb[slot]; cr = cmbR[slot]; rr = c['rr']
                a = acc_sb(4 * u + slot, nq, 66)
                S.op(DVE, lambda: V.reciprocal(out=rr[0:nq, u:u + 1], in_=a[:, 64:65]), reads=[accsR[(4 * u + slot) // 3]], writes=[cr['rr']])
            for (qbi, qcol, nq, slot) in qbs:
                c = cmb[slot]; cr = cmbR[slot]; rr = c['rr']
                a = acc_sb(4 * u + slot, nq, 66)
                S.op(DVE, lambda: V.tensor_scalar(out=c['ob'][0:nq, u * 64:(u + 1) * 64], in0=a[:, 0:64], scalar1=rr[0:nq, u:u + 1],
                                                  scalar2=None, op0=ALU.mult), reads=[accsR[(4 * u + slot) // 3], cr['rr']], writes=[cr['ob']])
            for (qbi, qcol, nq, slot) in qbs:
                c = cmb[slot]; cr = cmbR[slot]
                S.op(PE, lambda: TE.transpose(TR[:, slot * 128:slot * 128 + nq], c['ob'][0:nq, :], ident[0:nq, 0:nq]),
                     reads=[cr['ob'], constR], writes=[TRR], add=(slot != qbs[0][3]))
            q0 = qbs[0][1]
            gi = q0 // 512
            S.op(ACT, lambda: A.copy(out=moT[u * 64:(u + 1) * 64, p, q0:q0 + 512], in_=TR[u * 64:(u + 1) * 64, 0:512]),
                 reads=[TRR], writes=[moTR[p][gi]], add=(u == 1))

        for pc in mla_setup(0):
            pc()
        for hh in range(16):
            u = hh % 2
            units = [dict(QT=QT[u], qrows=(0, 96), qres=lambda vis, u=u: [QTR[u][vis[0][1] // 512]],
                          KT=KT[u], kres=lambda kbi, u=u: [KTR[u][kbi // 4], KTropeR[u]],
                          V=Vm[u], vres=lambda kbi, u=u: [VmR[u][kbi // 4]], vc=66, accbase=4 * u)]
            steps = []
            for grp in groups:
                gk = [k for k in kbs_all if k[0] <= grp[-1][0]]
                steps += attn_group(units, grp, gk, True, 'mla', 0, defer=True)
                steps.append(lambda grp=grp, hh=hh: mla_head_combine(hh, grp))
            interleave(steps, mla_setup(hh + 1) if hh < 15 else [])

    def run_sequence(is_prompt, s):
        if STOP <= 0:
            return
        if is_prompt:
            tq = T
            qblocks = [(b, b * 128, 128) for b in range(NB)]
        else:
            tq = DEC
            qblocks = [(PAST // 128, 0, DEC)]
        nqb = len(qblocks)

        fence([r for g_ in moTR for r in g_], hTR)
        fence(uTR + [mTR] + sigR + [emR], ARES)
        S.op(POOL, lambda: G.memset(Vd[0][:, :, 128:130], 1.0), writes=VdR[0])
        if not is_prompt:
            fence(VmR[0] + VmR[1], VdR[1])
            S.op(POOL, lambda: G.memset(Vd[1][:, :, 128:130], 1.0), writes=VdR[1])
        S.dma(SP, gtmp[:], g_mix.broadcast_to([128, D]), writes=[gtmpR])
        wtok, wtokR = wbuf()
        wtv = wtok[:, 0:4096].rearrange("p (c n) -> p c n", n=512)
        wload3("w_in", COL_CQ, 512, 8, wtv, wtokR)
        wload3("w_in", COL_KR, 32, 8, wtok2[:, :, 0:32], wtok2R)
        wload3("w_in", COL_KRS, 32, 8, wtok2[:, :, 32:64], wtok2R, add=True)
        for bi, (qbi, qcol, nq) in enumerate(qblocks):
            if is_prompt:
                row0 = s * T + qcol
                phase1(xp[row0:row0 + nq, :], nq, bi, qcol, wtv, wtokR, ckv_p[row0:row0 + nq, :], kr_p[row0:row0 + nq, :],
                       cosk[0:nq, bi, :], sink[0:nq, bi, :], qcol)
            else:
                phase1(xs[0:nq, :], nq, bi, 0, wtv, wtokR, ckv_s[0:nq, :], kr_s[0:nq, :],
                       cosks[0:nq, :], sinks[0:nq, :], NEWC)

        if EARLY and is_prompt and s == 0:
            cast_weights(["w_uqA", "w_uqB", "w_uk", "w_uv", "w_od", "w_om", "w_out", "w_up", "w_dn"], defer=True)
            if NSEQ == 1:
                cast_caches(defer=True)
        if EARLY and is_prompt and s == 1:
            cast_caches(defer=True)
        if not is_prompt:
            bg_flush()
        if STOP <= 1:
            return
        hsrc = lambda c0, n: [hTR[b] for b in range(c0 // 128, (c0 + n + 127) // 128)]
        groups = []
        if is_prompt:
            for g in range(NG):
                groups.append([(4 * g + i, (4 * g + i) * 128, 128, i) for i in range(4)])
        else:
            groups.append([(PAST // 128, 0, DEC, 0)])

        if is_prompt:
            segs = [('new', 0, T)]
        else:
            segs = [('cache', k0, min(TKm, PAST - k0)) for k0 in range(0, PAST, TKm)] + [('new', PAST, DEC)]

        if is_prompt:
            prompt_attention(s)
        for h in range(8):
            if is_prompt:
                break
            if STOP <= 2 and h >= 1:
                break
            hw, hwR = wbuf()
            hwv = hw[:, 0:3072].rearrange("p (c n) -> p c n", n=384)
            wload3("w_in", COL_Q + h * 128, 128, 8, hwv[:, :, 0:128], hwR)
            wload3("w_in", COL_K + h * 128, 128, 8, hwv[:, :, 128:256], hwR, add=True)
            wload3("w_in", COL_V + h * 128, 128, 8, hwv[:, :, 256:384], hwR, add=True)
            qi = h % 2
            qt, ktile, vt = QT[qi], KT[qi], Vd[qi]

            def evq(pk, pkR, c0, n, qt=qt, qi=qi):
                S.op(ACT, lambda: A.activation(out=qt[:, c0:c0 + n], in_=pk[:, 0:n], func=AF.Copy, scale=0.125),
                     reads=[pkR], writes=[QTR[qi][c0 // 512]])
            DP = int(os.environ.get("DEV_P", "255"))
            if DP & 2:
                proj_fm(qt, 128, hwv, 0, 128, hT, 8, tq, hsrc, hwR, None, evq)

            for si, (skind, k0, klen) in enumerate(segs):
                nkb = (klen + 127) // 128
                if skind == 'new':
                    kbase = 0 if is_prompt else 0

                    def evk(pk, pkR, c0, n, ktile=ktile, qi=qi):
                        S.op(DVE, lambda: V.tensor_copy(out=ktile[:, c0:c0 + n], in_=pk[:, 0:n]),
                             reads=[pkR], writes=[KTR[qi][c0 // 512], KTropeR[qi]])
                    if DP & 4:
                        proj_fm(ktile, 128, hwv, 128, 128, hT, 8, tq, hsrc, hwR, None, evk)
                    for bi, (qbi, qcol, nq) in enumerate(qblocks):
                        if not (DP & 8):
                            break
                        pk, pkR = pbank()
                        for c in range(8):
                            S.op(PE, lambda c=c: TE.matmul(pk[0:nq, 0:256], hT[:, c, qcol:qcol + nq], hwv[:, c, 128:384],
                                                           start=(c == 0), stop=(c == 7)),
                                 reads=[hTR[qcol // 128], hwR], writes=[pkR])
                        sg = (bi // 4) % 2
                        DQ = int(os.environ.get("DEV_Q", "3"))
                        if DQ & 1:
                            S.op(ACT, lambda: A.copy(out=kvst[sg][0:nq, bi % 4, :], in_=pk[0:nq, 0:256]), reads=[pkR], writes=[kvstR[sg]],
                                 add=(bi % 4 != 0))
                        if DQ & 2:
                            S.op(POOL, lambda: G.tensor_copy(out=vt[0:nq, bi, 0:128], in_=kvst[sg][0:nq, bi % 4, 128:256]),
                                 reads=[kvstR[sg]], writes=[VdR[qi][bi // 4]], add=(bi % 4 != 0))
                        if (bi % 4 == 3 or bi == nqb - 1) and (DP & 16):
                            b0_ = bi - (bi % 4)
                            nb_ = bi - b0_ + 1
                            if is_prompt:
                                r0 = s * T + b0_ * 128
                                kdst = kd_p[r0:r0 + nb_ * 128, h * 128:(h + 1) * 128].rearrange("(b p) e -> p b e", p=128)
                                vdst = vd_p[r0:r0 + nb_ * 128, h * 128:(h + 1) * 128].rearrange("(b p) e -> p b e", p=128)
                                S.dma(POOL, kdst, kvst[sg][:, 0:nb_, 0:128], reads=[kvstR[sg]])
                                S.dma(POOL, vdst, kvst[sg][:, 0:nb_, 128:256], reads=[kvstR[sg]])
                            else:
                                S.dma(POOL, kd_s[0:nq, h * 128:(h + 1) * 128], kvst[sg][0:nq, 0, 0:128], reads=[kvstR[sg]])
                                S.dma(POOL, vd_s[0:nq, h * 128:(h + 1) * 128], kvst[sg][0:nq, 0, 128:256], reads=[kvstR[sg]])
                    kbs = [(qbi, qcol, nq, bi) for bi, (qbi, qcol, nq) in enumerate(qblocks)]
                else:
                    S.dma(SP, vt[:, 0:nkb, 0:128], cb['cv'][k0:k0 + klen, h * 128:(h + 1) * 128].rearrange("(b p) e -> p b e", p=128),
                          reads=[cacheR['cv']], writes=VdR[qi][0:max(1, nkb // 4)])
                    for b8 in range(0, nkb, 8):
                        nb8 = min(8, nkb - b8)
                        kstg, kstgR = next_kstg()
                        S.dma(SP, kstg[:, 0:nb8, :],
                              cb['ck'][k0 + b8 * 128:k0 + (b8 + nb8) * 128, h * 128:(h + 1) * 128].rearrange("(b p) e -> p b e", p=128),
                              reads=[cacheR['ck']], writes=[kstgR])
                        for j in range(nb8):
                            S.op(PE, lambda j=j: TE.transpose(TR[:, j * 128:(j + 1) * 128], kstg[:, j, :], ident[:, :]),
                                 reads=[kstgR, constR], writes=[TRR], add=(j > 0))
                        S.op(DVE, lambda b8=b8, nb8=nb8: V.tensor_copy(out=ktile[:, b8 * 128:(b8 + nb8) * 128], in_=TR[:, 0:nb8 * 128]),
                             reads=[TRR], writes=KTR[qi][b8 // 4:(b8 + nb8 + 3) // 4] + [KTropeR[qi]], add=False)
                    kbs = [(k0 // 128 + j, j * 128, 128, j) for j in range(nkb)]
                units = []
                for m in range(2):
                    units.append(dict(QT=qt, qrows=(m * 64, (m + 1) * 64), qres=lambda vis, qi=qi: [QTR[qi][v[1] // 512] for v in vis][:1],
                                      KT=ktile, kres=lambda kbi, qi=qi, kbs=kbs: [KTR[qi][min(len(KTR[qi]) - 1, [k[1] for k in kbs if k[0] == kbi][0] // 512)]],
                                      V=vt, vres=lambda kbi, kbs=kbs: [VdR[qi][min(len(VdR[qi]) - 1, [k[3] for k in kbs if k[0] == kbi][0] // 4)]],
                                      vc=130, accbase=4 * m))
                SUB = int(os.environ.get("DEV_SUB", "9"))
                for grp in groups:
                    if SUB <= 1:
                        break
                    gk = [k for k in kbs if (not is_prompt) or k[0] <= grp[-1][0]]
                    attn_group_sample(units, grp[0], gk, 'diff', h, first_seg=(si == 0), last_seg=(si == len(segs) - 1))
                    if si == len(segs) - 1 and SUB > 2:
                        gi = grp[0][1] // 512
                        diff_combine(h, grp, lambda q0, ncols, h=h: doT[:, h, q0:q0 + ncols], lambda h=h, gi=gi: [doTR[h][gi]])

        if STOP <= 3:
            return
        fence(hTR, [r for g_ in moTR for r in g_])
        moT = hT
        cqsrc = lambda c0, n: [cqTR[b] for b in range(c0 // 128, (c0 + n + 127) // 128)]
        cq_d, sq_d = (cosq_p_d, sinq_p_d) if is_prompt else (cosq_s_d, sinq_s_d)
        if not is_prompt:
            fence(VdR[1], VmR[0] + VmR[1])
            for i_ in range(2):
                S.op(POOL, lambda i_=i_: G.memset(Vm[i_][:, :, 64:66], 1.0), writes=VmR[i_])
        for p in range(8):
            if is_prompt:
                break
            mw, mwR = wbuf()
            mwv = mw[:, 0:1280].rearrange("p (c n) -> p c n", n=640)
            wload3("w_uqA", p * 192, 192, 2, mwv[:, :, 0:192], mwR)
            wload3("w_uqB", p * 192, 192, 2, mwv[:, :, 192:384], mwR, add=True)
            wload3("w_uk", p * 128, 128, 2, mwv[:, :, 384:512], mwR, add=True)
            wload3("w_uv", p * 128, 128, 2, mwv[:, :, 512:640], mwR, add=True)
            for u in range(2):
                qt = QT[u]
                for c0 in range(0, tq, 512):
                    n = min(512, tq - c0)
                    pa, paR = Pb[0], PbR[0]
                    pb_, pbR_ = Pb[3], PbR[3]
                    for c in range(2):
                        S.op(PE, lambda c=c: TE.matmul(pa[0:96, 0:n], mwv[:, c, u * 96:(u + 1) * 96], cqT[:, c, c0:c0 + n],
                                                       start=(c == 0), stop=(c == 1)), reads=cqsrc(c0, n) + [mwR], writes=[paR])
                    for c in range(2):
                        S.op(PE, lambda c=c: TE.matmul(pb_[0:96, 0:n], mwv[:, c, 192 + u * 96:192 + (u + 1) * 96], cqT[:, c, c0:c0 + n],
                                                       start=(c == 0), stop=(c == 1)), reads=cqsrc(c0, n) + [mwR], writes=[pbR_])
                    S.op(ACT, lambda: A.activation(out=qt[0:64, c0:c0 + n], in_=pa[0:64, 0:n], func=AF.Copy, scale=96 ** -0.5),
                         reads=[paR], writes=[QTR[u][c0 // 512]])
                    S.dma(SP, ropet[64:96, 2, 0:n], cq_d[:, c0:c0 + n], writes=[ropeTR])
                    S.dma(SP, ropet[64:96, 3, 0:n], sq_d[:, c0:c0 + n], writes=[ropeTR], add=True)
                    S.op(DVE, lambda: V.tensor_tensor(out=ropet[64:96, 0, 0:n], in0=pb_[64:96, 0:n], in1=ropet[64:96, 3, 0:n], op=ALU.mult),
                         reads=[pbR_, ropeTR], writes=[ropeR])
                    S.op(DVE, lambda: V.tensor_tensor(out=ropet[64:96, 1, 0:n], in0=pa[64:96, 0:n], in1=ropet[64:96, 2, 0:n], op=ALU.mult),
                         reads=[paR, ropeTR], writes=[ropeR], add=True)
                    S.op(DVE, lambda: V.tensor_tensor(out=qt[64:96, c0:c0 + n], in0=ropet[64:96, 0, 0:n], in1=ropet[64:96, 1, 0:n], op=ALU.add),
                         reads=[ropeR], writes=[QTR[u][c0 // 512]], add=True)
            for si, (skind, k0, klen) in enumerate(segs):
                nkb = (klen + 127) // 128
                if skind == 'cache' and p > 0:
                    S.dma(SP, ckvT[:, :, 0:klen], ckvT_d[si][:, :, 0:klen], reads=[latR[si]], writes=ckvTR[0:nkb])
                    S.dma(SP, krT[64:96, 0:klen], krT_d[si][:, 0:klen], reads=[latR[si]], writes=krTR[0:nkb])
                    kbs = [(k0 // 128 + j, j * 128, 128, j) for j in range(nkb)]
                    src0 = 0
                elif skind == 'cache':
                    for b4 in range(0, nkb, 4):
                        nb4 = min(4, nkb - b4)
                        kstg, kstgR = next_kstg()
                        st_ = kstg[:, 0:8, :].rearrange("p (b c) e -> p b (c e)", c=2)
                        S.dma(SP, st_[:, 0:nb4, :], cb['cckv'][k0 + b4 * 128:k0 + (b4 + nb4) * 128, :].rearrange("(b p) e -> p b e", p=128),
                              reads=[cacheR['cckv']], writes=[kstgR])
                        for j in range(nb4):
                            for c in range(2):
                                S.op(PE, lambda j=j, c=c: TE.transpose(TR[:, (j * 2 + c) * 128:(j * 2 + c + 1) * 128],
                                                                       st_[:, j, c * 128:(c + 1) * 128], ident[:, :]),
                                     reads=[kstgR, constR], writes=[TRR], add=(j + c > 0))
                        trv = TR[:, 0:nb4 * 256].rearrange("p (b c t) -> p c b t", c=2, t=128)
                        for c in range(2):
                            S.op(DVE, lambda c=c, b4=b4, nb4=nb4, trv=trv: V.tensor_copy(
                                out=ckvT[:, c, b4 * 128:(b4 + nb4) * 128].rearrange("p (b t) -> p b t", t=128), in_=trv[:, c, 0:nb4, :]),
                                reads=[TRR], writes=ckvTR[b4:b4 + nb4], add=(c > 0))
                    for b8 in range(0, nkb, 8):
                        nb8 = min(8, nkb - b8)
                        kstg, kstgR = next_kstg()
                        S.op(POOL, lambda: G.memset(kstg[:, :, 0:64], 0.0), writes=[kstgR])
                        S.dma(SP, kstg[:, 0:nb8, 64:96], cb['ckr'][k0 + b8 * 128:k0 + (b8 + nb8) * 128, :].rearrange("(b p) e -> p b e", p=128),
                              reads=[kstgR, cacheR['ckr']], writes=[kstgR], add=True)
                        for j in range(nb8):
                            S.op(PE, lambda j=j: TE.transpose(TR[0:96, j * 128:(j + 1) * 128], kstg[:, j, 0:96], ident[:, :]),
                                 reads=[kstgR, constR], writes=[TRR], add=(j > 0))
                        S.op(DVE, lambda b8=b8, nb8=nb8: V.tensor_copy(out=krT[64:96, b8 * 128:(b8 + nb8) * 128], in_=TR[64:96, 0:nb8 * 128]),
                             reads=[TRR], writes=krTR[b8:b8 + nb8])
                    S.dma(POOL, ckvT_d[si][:, :, 0:klen], ckvT[:, :, 0:klen], reads=ckvTR[0:nkb], writes=[latR[si]])
                    S.dma(POOL, krT_d[si][:, 0:klen], krT[64:96, 0:klen], reads=krTR[0:nkb], writes=[latR[si]], add=True)
                    kbs = [(k0 // 128 + j, j * 128, 128, j) for j in range(nkb)]
                    src0 = 0
                elif not is_prompt:
                    kbs = [(PAST // 128, 0, DEC, 0)]
                    src0 = NEWC
                else:
                    kbs = [(qbi, qcol, nq, bi) for bi, (qbi, qcol, nq) in enumerate(qblocks)]
                    src0 = 0
                kspan = kbs[-1][1] + kbs[-1][2]
                lsrc = lambda c0, n: [ckvTR[b] for b in range((src0 + c0) // 128, (src0 + c0 + n + 127) // 128)]
                units = []
                for u in range(2):
                    ktile, vt = KT[u], Vm[u]
                    for c0 in range(0, kspan, 512):
                        n = min(512, kspan - c0)
                        pk, pkR = pbank()
                        for c in range(2):
                            S.op(PE, lambda c=c: TE.matmul(pk[0:64, 0:n], mwv[:, c, 384 + u * 64:384 + (u + 1) * 64],
                                                           ckvT[:, c, src0 + c0:src0 + c0 + n], start=(c == 0), stop=(c == 1)),
                                 reads=lsrc(c0, n) + [mwR], writes=[pkR])
                        S.op(DVE, lambda: V.tensor_copy(out=ktile[0:64, c0:c0 + n], in_=pk[0:64, 0:n]), reads=[pkR],
                             writes=[KTR[u][min(len(KTR[u]) - 1, c0 // 512)]])
                    S.op(DVE, lambda: V.tensor_copy(out=ktile[64:96, 0:kspan], in_=krT[64:96, src0:src0 + kspan]),
                         reads=[krTR[b] for b in range(src0 // 128, (src0 + kspan + 127) // 128)], writes=[KTropeR[u]] + KTR[u], add=True)
                    for b8 in range(0, len(kbs), 8):
                        sub = kbs[b8:b8 + 8]
                        pk, pkR = pbank()
                        for j, (kbi, kcol, nk, vblk) in enumerate(sub):
                            for c in range(2):
                                S.op(PE, lambda c=c, j=j, kcol=kcol, nk=nk: TE.matmul(
                                    pk[0:nk, j * 64:(j + 1) * 64], ckvT[:, c, src0 + kcol:src0 + kcol + nk],
                                    mwv[:, c, 512 + u * 64:512 + (u + 1) * 64], start=(c == 0), stop=(c == 1)),
                                    reads=[ckvTR[(src0 + kcol) // 128], mwR], writes=[pkR], add=(j + c > 0))
                        nk0 = sub[0][2]
                        v0 = sub[0][3]
                        S.op(ACT, lambda sub=sub, nk0=nk0, v0=v0, pk=pk: A.copy(
                            out=vt[0:nk0, v0:v0 + len(sub), 0:64], in_=pk[0:nk0, 0:len(sub) * 64].rearrange("p (b e) -> p b e", e=64)),
                            reads=[pkR], writes=[VmR[u][min(len(VmR[u]) - 1, g_)] for g_ in range(v0 // 4, (v0 + len(sub) + 3) // 4)])
                    units.append(dict(QT=QT[u], qrows=(0, 96), qres=lambda vis, u=u: [QTR[u][vis[0][1] // 512]],
                                      KT=ktile,
                                      kres=lambda kbi, u=u, kbs=kbs: [KTR[u][min(len(KTR[u]) - 1, [k[1] for k in kbs if k[0] == kbi][0] // 512)], KTropeR[u]],
                                      V=vt,
                                      vres=lambda kbi, u=u, kbs=kbs: [VmR[u][min(len(VmR[u]) - 1, [k[3] for k in kbs if k[0] == kbi][0] // 4)]],
                                      vc=66, accbase=4 * u))
                for grp in groups:
                    gk = [k for k in kbs if (not is_prompt) or k[0] <= grp[-1][0]]
                    attn_group_sample(units, grp[0], gk, 'mla', 0, first_seg=(si == 0), last_seg=(si == len(segs) - 1))
                    if si == len(segs) - 1:
                        gi = grp[0][1] // 512
                        mla_combine(p, grp, lambda q0, ncols, p=p: moT[:, p, q0:q0 + ncols], lambda p=p, gi=gi: [moTR[p][gi]])

        if STOP <= 4:
            return
        fence(ARES, uTR + [mTR] + sigR)
        def chunk_blks(c0):
            n = min(512, tq - c0)
            nblk = (n + 127) // 128
            return [(c0 + b * 128, min(128, n - b * 128)) for b in range(nblk)]

        def step_a_block(c0, b):
            col, nq = chunk_blks(c0)[b]
            ot, otR = ost()
            xt = ot[0:nq, :]
            src = xp[s * T + col:s * T + col + nq, :] if is_prompt else xs[0:nq, :]
            S.dma(SP, xt, src, writes=[otR])
            sc, sres = statcol()
            S.op(DVE, lambda: V.scalar_tensor_tensor(out=junk[0:nq, :], in0=xt, scalar=1.0, in1=xt, op0=ALU.mult, op1=ALU.mult,
                                                     accum_out=sc(nq, 0)), reads=[otR], writes=JK + [sres])
            rstd_from_sumsq(nq, sc(nq, 0), sc(nq, 1), 1.0 / D, sres)
            if b == 0:
                S.dma(SP, gtmp[:], g_mix.broadcast_to([128, D]), writes=[gtmpR])
            S.op(DVE, lambda: V.scalar_tensor_tensor(out=hb[0:nq, :], in0=xt, scalar=sc(nq, 1), in1=gtmp[0:nq, :],
                                                     op0=ALU.mult, op1=ALU.mult), reads=[otR, sres, gtmpR], writes=[hbR])
            transposes_to(lambda b=b, nq=nq: mT[:, :, b * 128:b * 128 + nq], hb, nq, 8, 128, hbR, [mTR], ACT)

        for b in range(len(chunk_blks(0))):
            step_a_block(0, b)
        for c0 in range(0, tq, 512):
            n = min(512, tq - c0)
            gi = c0 // 512
            blks = chunk_blks(c0)
            nblk = len(blks)
            has_next = c0 + 512 < tq
            mg = uT
            moT_ = hT
            for dc in range(8):
                gw, gwR = wbuf()
                gv = gw[:, 0:4096].rearrange("p (k c n) -> p k c n", k=4, n=128)
                wload3("w_od", dc * 128, 128, 8, gv[:, 0], gwR)
                wload3("w_om", dc * 128, 128, 8, gv[:, 1], gwR, add=True)
                wload3("w_in", COL_GD + dc * 128, 128, 8, gv[:, 2], gwR, add=True)
                wload3("w_in", COL_GM + dc * 128, 128, 8, gv[:, 3], gwR, add=True)
                banks = [(Sb[0], SbR[0]), (Sb[1], SbR[1]), (Pb[0], PbR[0]), (Sb[2], SbR[2])]
                srcs = [(doT, doTR), (moT, moTR)]
                for k in range(2):
                    bk, bkR = banks[k]
                    src, srcR = srcs[k]
                    for c in range(8):
                        S.op(PE, lambda c=c, bk=bk, src=src, k=k: TE.matmul(bk[:, 0:n], gv[:, k, c, :], src[:, c, c0:c0 + n],
                                                                            start=(c == 0), stop=(c == 7)),
                             reads=[srcR[c][gi], gwR], writes=[bkR])
                for k in range(2, 4):
                    bk, bkR = banks[k]
                    for c in range(8):
                        S.op(PE, lambda c=c, bk=bk, k=k: TE.matmul(bk[:, 0:n], gv[:, k, c, :], mT[:, c, 0:n],
                                                                   start=(c == 0), stop=(c == 7)), reads=[mTR, gwR], writes=[bkR])
                for k in range(2):
                    bk, bkR = banks[2 + k]
                    S.op(ACT, lambda bk=bk, k=k: A.activation(out=sigb[k][:, 0:n], in_=bk[:, 0:n], func=AF.Sigmoid),
                         reads=[bkR], writes=[sigR[k]])
                S.op(DVE, lambda: V.tensor_tensor(out=sigb[0][:, 0:n], in0=sigb[0][:, 0:n], in1=Sb[0][:, 0:n], op=ALU.mult),
                     reads=[sigR[0], SbR[0]], writes=[sigR[0]])
                S.op(DVE, lambda: V.tensor_tensor(out=sigb[1][:, 0:n], in0=sigb[1][:, 0:n], in1=Sb[1][:, 0:n], op=ALU.mult),
                     reads=[sigR[1], SbR[1]], writes=[sigR[1]])
                S.op(DVE, lambda dc=dc: V.tensor_tensor(out=mg[:, dc, 0:n], in0=sigb[0][:, 0:n], in1=sigb[1][:, 0:n], op=ALU.add),
                     reads=[sigR[0], sigR[1]], writes=[uTR[dc]])
            for b, (col, nq) in enumerate(blks):
                src = xp[s * T + col:s * T + col + nq, :] if is_prompt else xs[0:nq, :]
                S.dma(SP, xres[0:nq, b, :], src, writes=[xresR[b]])
            for half in range(2):
                ow, owR = wbuf()
                ov = ow[:, 0:4096].rearrange("p (c n) -> p c n", n=512)
                wload3("w_out", half * 512, 512, 8, ov, owR)
                for b, (col, nq) in enumerate(blks):
                    pk, pkR = pbank()
                    for c in range(8):
                        S.op(PE, lambda c=c: TE.matmul(pk[0:nq, 0:512], mg[:, c, b * 128:b * 128 + nq], ov[:, c, :],
                                                       start=(c == 0), stop=(c == 7)), reads=[uTR[c], owR], writes=[pkR])
                    S.op(DVE, lambda: V.tensor_tensor(out=xres[0:nq, b, half * 512:(half + 1) * 512],
                                                      in0=xres[0:nq, b, half * 512:(half + 1) * 512], in1=pk[0:nq, 0:512], op=ALU.add),
                         reads=[xresR[b], pkR], writes=[xresR[b]])
            for b, (col, nq) in enumerate(blks):
                xt = xres[0:nq, b, :]
                sc, sres = statcol()
                S.op(DVE, lambda: V.scalar_tensor_tensor(out=junk[0:nq, :], in0=xt, scalar=1.0, in1=xt, op0=ALU.mult, op1=ALU.mult,
                                                         accum_out=sc(nq, 0)), reads=[xresR[b]], writes=JK + [sres])
                rstd_from_sumsq(nq, sc(nq, 0), sc(nq, 1), 1.0 / D, sres)
                if b == 0:
                    S.dma(SP, gtmp[:], g_mlp.broadcast_to([128, D]), writes=[gtmpR])
                S.op(DVE, lambda: V.scalar_tensor_tensor(out=hb[0:nq, :], in0=xt, scalar=sc(nq, 1), in1=gtmp[0:nq, :],
                                                         op0=ALU.mult, op1=ALU.mult), reads=[xresR[b], sres, gtmpR], writes=[hbR])
                transposes_to(lambda b=b, nq=nq: mT[:, :, b * 128:b * 128 + nq], hb, nq, 8, 128, hbR, [mTR], ACT)
            for f4 in range(8):
                uw, uwR = wbuf()
                uv = uw[:, 0:4096].rearrange("p (c n) -> p c n", n=512)
                wload3("w_up", f4 * 512, 512, 8, uv, uwR)
                for fi in range(4):
                    f = f4 * 4 + fi
                    pk, pkR = pbank()
                    for c in range(8):
                        S.op(PE, lambda c=c: TE.matmul(pk[:, 0:n], uv[:, c, fi * 128:(fi + 1) * 128], mT[:, c, 0:n],
                                                       start=(c == 0), stop=(c == 7)), reads=[mTR, uwR], writes=[pkR])
                    sg_ = sigb[f % 2]; sgR_ = sigR[f % 2]
                    S.op(ACT, lambda: A.activation(out=sg_[:, 0:n], in_=pk[:, 0:n], func=AF.Relu), reads=[pkR], writes=[sgR_])
                    S.op(DVE, lambda f=f: V.tensor_tensor(out=uT[:, f, 0:n], in0=sg_[:, 0:n], in1=sg_[:, 0:n], op=ALU.mult),
                         reads=[sgR_], writes=[uTR[f]])
            for half in range(2):
                accb = [(Sb[0], [SbR[0]]), (Sb[1], [SbR[1]]), (Ab[0], [_bankR[0]]), (Ab[1], [_bankR[1]])]
                for f8 in range(4):
                    dw, dwR = wbuf()
                    dv = dw[:, 0:4096].rearrange("p (c n) -> p c n", n=512)
                    src = wb["w_dn"].rearrange("(c p) n -> p c n", p=128)[:, f8 * 8:(f8 + 1) * 8, half * 512:(half + 1) * 512]
                    S.dma(SP, dv, src, reads=[wbR["w_dn"]], writes=[dwR])
                    for b, (col, nq) in enumerate(blks):
                        bk, bkR = accb[b]
                        for c in range(8):
                            f = f8 * 8 + c
                            S.op(PE, lambda c=c, f=f, bk=bk: TE.matmul(bk[0:nq, 0:512], uT[:, f, b * 128:b * 128 + nq], dv[:, c, :],
                                                                       start=(f == 0), stop=(f == 31)), reads=[uTR[f], dwR], writes=bkR)
                    if half == 0 and has_next and f8 < len(chunk_blks(c0 + 512)):
                        step_a_block(c0 + 512, f8)
                for b, (col, nq) in enumerate(blks):
                    bk, bkR = accb[b]
                    S.op(DVE, lambda bk=bk: V.tensor_tensor(out=xres[0:nq, b, half * 512:(half + 1) * 512],
                                                            in0=xres[0:nq, b, half * 512:(half + 1) * 512], in1=bk[0:nq, 0:512], op=ALU.add),
                         reads=[xresR[b]] + bkR, writes=[xresR[b]])
            for b, (col, nq) in enumerate(blks):
                xt = xres[0:nq, b, :]
                sc, sres = statcol()
                S.op(DVE, lambda: V.scalar_tensor_tensor(out=junk[0:nq, :], in0=xt, scalar=1.0, in1=xt, op0=ALU.mult, op1=ALU.mult,
                                                         accum_out=sc(nq, 0)), reads=[xresR[b]], writes=JK + [sres])
                rstd_from_sumsq(nq, sc(nq, 0), sc(nq, 1), 1.0 / D, sres)
                ot, otR = ost()
                if b == 0:
                    S.dma(SP, gtmp[:], g_fin.broadcast_to([128, D]), writes=[gtmpR])
                S.op(DVE, lambda: V.scalar_tensor_tensor(out=ot[0:nq, :], in0=xt, scalar=sc(nq, 1), in1=gtmp[0:nq, :],
                                                         op0=ALU.mult, op1=ALU.mult), reads=[xresR[b], sres, gtmpR], writes=[otR])
                dst = y_p[s * T + col:s * T + col + nq, :] if is_prompt else y_s[0:nq, :]
                S.dma(POOL, dst, ot[0:nq, :], reads=[otR])

    for s in range(NSEQ):
        run_sequence(True, s)
    if PAST > 0 and STOP > 5:
        run_sequence(False, 0)
    S.finish(POOL)
    S.finish(SP)
    return nc, S


def _prep_weights(inp):
    w_in = np.asarray(inp['w_in'][0], np.float32)
    kr = w_in[:, COL_KR:COL_KR + 32]
    w_in_ext = np.concatenate([w_in, kr[:, 16:32], kr[:, 0:16]], axis=1)
    uq = np.asarray(inp['mla_w_uq'][0], np.float32)
    uqA = uq.reshape(256, MH * 96)
    uqB = np.concatenate([uq[:, :, 0:64], uq[:, :, 80:96], uq[:, :, 64:80]], axis=2).reshape(256, MH * 96)
    lamv = np.concatenate([inp['lam_q1'][0], inp['lam_k1'][0], inp['lam_q2'][0], inp['lam_k2'][0]])[None, :]
    f = lambda a: np.ascontiguousarray(np.asarray(a, np.float32))
    return dict(
        w_in=f(w_in_ext), w_uqA=f(uqA), w_uqB=f(uqB),
        w_uk=f(np.asarray(inp['mla_w_uk'][0]).reshape(256, 1024)), w_uv=f(np.asarray(inp['mla_w_uv'][0]).reshape(256, 1024)),
        w_od=f(np.asarray(inp['w_o_diff'][0]).reshape(1024, D)), w_om=f(np.asarray(inp['w_o_mla'][0]).reshape(1024, D)),
        w_out=f(inp['w_out'][0]), w_up=f(inp['w_up'][0]), w_dn=f(inp['w_down'][0]),
        g_mix=f(inp['norm_mix'][0][None, :]), g_mlp=f(inp['norm_mlp'][0][None, :]), g_fin=f(np.asarray(inp['norm_final'])[None, :]),
        g_sub=f(inp['diff_subln'][0][None, :]), g_q=f(inp['mla_q_norm'][0][None, :]), g_kv=f(inp['mla_kv_norm'][0][None, :]),
        lamv=f(lamv), relb=f(np.asarray(inp['rel_bias']).reshape(1, 256)))


def run(inp, n_cores=8):
    x_prompt = np.asarray(inp['x_prompt'], np.float32)
    x_sample = np.asarray(inp['x_sample'], np.float32)
    B, T, _ = x_prompt.shape
    DB = x_sample.shape[0]
    PAST = inp['cache_diff_k'].shape[2]
    assert B % n_cores == 0 and DB == n_cores
    NSEQ = B // n_cores
    _, S1 = build(NSEQ, T, PAST)
    nc, S = build(NSEQ, T, PAST, needed=S1.used)
    common = _prep_weights(inp)
    common.update(_static_tables(T, PAST))
    in_maps = []
    for c in range(n_cores):
        m = dict(common)
        m['xp'] = np.ascontiguousarray(x_prompt[c * NSEQ:(c + 1) * NSEQ].reshape(NSEQ * T, D))
        m['xs'] = np.ascontiguousarray(x_sample[c])
        m['ck'] = np.ascontiguousarray(np.asarray(inp['cache_diff_k'][0, c], np.float32).reshape(PAST, 1024))
        m['cv'] = np.ascontiguousarray(np.asarray(inp['cache_diff_v'][0, c], np.float32).reshape(PAST, 1024))
        m['cckv'] = np.ascontiguousarray(np.asarray(inp['cache_mla_ckv'][0, c], np.float32))
        m['ckr'] = np.ascontiguousarray(np.asarray(inp['cache_mla_krope'][0, c], np.float32))
        in_maps.append(m)
    res = run_bass_kernel_spmd(nc, in_maps, core_ids=list(range(n_cores)))
    rs = res.results
    cat = lambda k: np.concatenate([r[k] for r in rs], axis=0)
    y_p = cat('y_p').reshape(B, T, D)
    y_s = np.stack([r['y_s'] for r in rs])
    kd_p = cat('kd_p').reshape(1, B, T, NH, 128)
    vd_p = cat('vd_p').reshape(1, B, T, NH, 128)
    ckv_p = cat('ckv_p').reshape(1, B, T, 256)
    kr_p = cat('kr_p').reshape(1, B, T, 32)
    kd_s = np.stack([r['kd_s'] for r in rs]).reshape(1, DB, DEC, NH, 128)
    vd_s = np.stack([r['vd_s'] for r in rs]).reshape(1, DB, DEC, NH, 128)
    ckv_s = np.stack([r['ckv_s'] for r in rs]).reshape(1, DB, DEC, 256)
    kr_s = np.stack([r['kr_s'] for r in rs]).reshape(1, DB, DEC, 32)
    return tuple(np.ascontiguousarray(a, dtype=np.float32) for a in
                 (y_p, y_s, kd_p, vd_p, ckv_p, kr_p, kd_s, vd_s, ckv_s, kr_s))


def kernel(**inputs):
    return run(inputs, 8)
```
